# Optimizing a Trainium2 kernel written in Bass

```python
import math
import jax, jax.numpy as jnp
from jax import lax
import numpy as np

D_MODEL = 2048
BATCH = 4
SEQ = 4096
DEPTH = 1

HEAD_DIM = 64
D_RWKV = D_MODEL // 2
D_SB = D_MODEL - D_RWKV
N_RWKV_HEADS = D_RWKV // HEAD_DIM
N_SB_HEADS = D_SB // HEAD_DIM
D_IN_PROJ = 3 * D_RWKV + 3 * D_SB
DECAY_LORA = 64
AAA_LORA = 64
GATE_LORA = 160
D_FF = ((8 * D_MODEL + 3 * 256 - 1) // (3 * 256)) * 256
SB_BLOCK = 128
RMS_EPS = 1e-6
GN_EPS = 64e-5
L2_EPS = 1e-12

kernel_name = "hymba_rwkv7_stickbreaking_block"


def rms_norm(x, gain):
    xf = x.astype(jnp.float32)
    y = xf * lax.rsqrt(jnp.mean(xf * xf, axis=-1, keepdims=True) + RMS_EPS)
    return (y * gain.astype(jnp.float32)).astype(x.dtype)


def token_shift(x):
    return jnp.pad(x[:, :-1], ((0, 0), (1, 0), (0, 0)))


def rwkv7_time_mix(h, p_rkv, mu_rkv, mu_w, mu_a, mu_g, w0, w1, w2, a0, a1, a2,
                   g1, g2, k_k, k_a, r_k, ln_x_gain, ln_x_bias):
    B, T, _ = h.shape
    H, N = N_RWKV_HEADS, HEAD_DIM
    dh = token_shift(h) - h
    xw = h + dh * mu_w
    xa = h + dh * mu_a
    xg = h + dh * mu_g
    p = p_rkv + (token_shift(p_rkv) - p_rkv) * mu_rkv
    r, k, v = jnp.split(p, 3, axis=-1)
    w_log = -jax.nn.softplus(-(w0 + jnp.tanh(xw @ w1) @ w2)) - 0.5
    decay = jnp.exp(-jnp.exp(w_log.astype(jnp.float32)))
    a = jax.nn.sigmoid(a0 + (xa @ a1) @ a2)
    g = jax.nn.sigmoid(xg @ g1) @ g2
    kk = (k * k_k).reshape(B, T, H, N).astype(jnp.float32)
    kk = kk * lax.rsqrt(jnp.sum(kk * kk, axis=-1, keepdims=True) + L2_EPS)
    k = k * (1.0 + (a - 1.0) * k_a)
    rh = r.reshape(B, T, H, N)
    kh = k.reshape(B, T, H, N)
    vh = v.reshape(B, T, H, N)
    ah = a.reshape(B, T, H, N).astype(jnp.float32)
    wh = decay.reshape(B, T, H, N)

    def step(S, inp):
        r_t, w_t, k_t, v_t, rem_t, wr_t = inp
        sa = jnp.einsum('bhvk,bhk->bhv', S, rem_t)
        S = (S * w_t[:, :, None, :] + sa[..., None] * wr_t[:, :, None, :]
             + v_t[..., None] * k_t[:, :, None, :])
        y = jnp.einsum('bhvk,bhk->bhv', S, r_t)
        return S, y

    xs = tuple(jnp.moveaxis(t.astype(jnp.float32), 1, 0)
               for t in (rh, wh, kh, vh, -kk, kk * ah))
    S0 = jnp.zeros((B, H, N, N), jnp.float32)
    _, y = lax.scan(step, S0, xs)
    y = jnp.moveaxis(y, 0, 1)
    mean = jnp.mean(y, axis=-1, keepdims=True)
    var = jnp.mean(jnp.square(y - mean), axis=-1, keepdims=True)
    y = ((y - mean) * lax.rsqrt(var + GN_EPS)).reshape(B, T, D_RWKV)
    y = (y * ln_x_gain + ln_x_bias).astype(h.dtype)
    bonus = jnp.sum(rh * kh * r_k, axis=-1, keepdims=True) * vh
    return (y + bonus.reshape(B, T, D_RWKV)) * g


def stick_breaking_attention(q, k, v):
    B, H, T, d = q.shape
    nblk = T // SB_BLOCK
    qb = q.reshape(B, H, nblk, SB_BLOCK, d).transpose(2, 0, 1, 3, 4)
    kpos = jnp.arange(T)
    inv_sqrt_d = 1.0 / math.sqrt(d)

    def block(args):
        qi, i = args
        z = jnp.einsum('bhqd,bhkd->bhqk', qi, k).astype(jnp.float32) * inv_sqrt_d
        qpos = i * SB_BLOCK + jnp.arange(SB_BLOCK)
        causal = kpos[None, :] < qpos[:, None]
        log_1m_beta = jnp.where(causal, -jax.nn.softplus(z), 0.0)
        tail = lax.cumsum(log_1m_beta, axis=3, reverse=True) - log_1m_beta
        log_a = jax.nn.log_sigmoid(z) + tail
        attn = jnp.where(causal, jnp.exp(log_a), 0.0)
        return jnp.einsum('bhqk,bhkd->bhqd', attn.astype(v.dtype), v)

    out = lax.map(block, (qb, jnp.arange(nblk)))
    return out.transpose(1, 2, 0, 3, 4).reshape(B, H, T, d)


def setup_inputs(seed: int = 0) -> dict:
    key = jax.random.key(seed)
    ks = jax.random.split(key, 32)
    L = DEPTH

    def nrm(k, shape, scale):
        return jax.random.normal(k, shape, jnp.float32) * scale

    def uni(k, shape, lo=0.0, hi=1.0):
        return jax.random.uniform(k, shape, jnp.float32, minval=lo, maxval=hi)

    Dm = D_MODEL
    return {
        "x": nrm(ks[0], (BATCH, SEQ, Dm), 1.0),
        "c": nrm(ks[1], (BATCH, Dm), 1.0),
        "w_ada": nrm(ks[2], (L, Dm, 6 * Dm), 0.5 * Dm ** -0.5),
        "b_ada": nrm(ks[3], (L, 6 * Dm), 0.02),
        "norm1_gain": 1.0 + nrm(ks[4], (L, Dm), 0.05),
        "norm2_gain": 1.0 + nrm(ks[5], (L, Dm), 0.05),
        "w_in": nrm(ks[6], (L, Dm, D_IN_PROJ), Dm ** -0.5),
        "mu_rkv": uni(ks[7], (L, 3 * D_RWKV)),
        "mu_w": uni(ks[8], (L, Dm)),
        "mu_a": uni(ks[9], (L, Dm)),
        "mu_g": uni(ks[10], (L, Dm)),
        "w0": uni(ks[11], (L, D_RWKV), -5.5, -0.5),
        "w1": nrm(ks[12], (L, Dm, DECAY_LORA), Dm ** -0.5),
        "w2": nrm(ks[13], (L, DECAY_LORA, D_RWKV), 0.3 * DECAY_LORA ** -0.5),
        "a0": nrm(ks[14], (L, D_RWKV), 0.1),
        "a1": nrm(ks[15], (L, Dm, AAA_LORA), Dm ** -0.5),
        "a2": nrm(ks[16], (L, AAA_LORA, D_RWKV), 0.3 * AAA_LORA ** -0.5),
        "g1": nrm(ks[17], (L, Dm, GATE_LORA), Dm ** -0.5),
        "g2": nrm(ks[18], (L, GATE_LORA, D_RWKV), GATE_LORA ** -0.5),
        "k_k": 0.85 + nrm(ks[19], (L, D_RWKV), 0.05),
        "k_a": 1.0 + nrm(ks[20], (L, D_RWKV), 0.05),
        "r_k": nrm(ks[21], (L, N_RWKV_HEADS, HEAD_DIM), 0.1),
        "ln_x_gain": 1.0 + nrm(ks[22], (L, D_RWKV), 0.05),
        "ln_x_bias": nrm(ks[23], (L, D_RWKV), 0.02),
        "q_norm_gain": 1.0 + nrm(ks[24], (L, HEAD_DIM), 0.05),
        "k_norm_gain": 1.0 + nrm(ks[25], (L, HEAD_DIM), 0.05),
        "w_out": nrm(ks[26], (L, Dm, Dm), Dm ** -0.5),
        "w_gate_up": nrm(ks[27], (L, Dm, 2 * D_FF), Dm ** -0.5),
        "w_down": nrm(ks[28], (L, D_FF, Dm), D_FF ** -0.5),
    }


def reference(x, c, w_ada, b_ada, norm1_gain, norm2_gain, w_in, mu_rkv, mu_w, mu_a,
              mu_g, w0, w1, w2, a0, a1, a2, g1, g2, k_k, k_a, r_k, ln_x_gain,
              ln_x_bias, q_norm_gain, k_norm_gain, w_out, w_gate_up, w_down):
    B, T, _ = x.shape
    c_act = jax.nn.silu(c)
    for l in range(DEPTH):
        mod = c_act @ w_ada[l] + b_ada[l]
        sh1, sc1, gt1, sh2, sc2, gt2 = [m[:, None, :] for m in jnp.split(mod, 6, axis=-1)]

        h = rms_norm(x, norm1_gain[l]) * (1.0 + sc1) + sh1
        p = h @ w_in[l]
        p_rkv, p_sb = p[..., :3 * D_RWKV], p[..., 3 * D_RWKV:]

        y_rwkv = rwkv7_time_mix(h, p_rkv, mu_rkv[l], mu_w[l], mu_a[l], mu_g[l],
                                w0[l], w1[l], w2[l], a0[l], a1[l], a2[l], g1[l], g2[l],
                                k_k[l], k_a[l], r_k[l], ln_x_gain[l], ln_x_bias[l])

        q, k, v = jnp.split(p_sb, 3, axis=-1)
        q = rms_norm(q.reshape(B, T, N_SB_HEADS, HEAD_DIM), q_norm_gain[l])
        k = rms_norm(k.reshape(B, T, N_SB_HEADS, HEAD_DIM), k_norm_gain[l])
        v = v.reshape(B, T, N_SB_HEADS, HEAD_DIM)
        y_sb = stick_breaking_attention(q.transpose(0, 2, 1, 3), k.transpose(0, 2, 1, 3),
                                        v.transpose(0, 2, 1, 3))
        y_sb = y_sb.transpose(0, 2, 1, 3).reshape(B, T, D_SB)

        mix = jnp.concatenate([y_rwkv, y_sb], axis=-1) @ w_out[l]
        x = x + gt1 * mix

        h2 = rms_norm(x, norm2_gain[l]) * (1.0 + sc2) + sh2
        gate, up = jnp.split(h2 @ w_gate_up[l], 2, axis=-1)
        x = x + gt2 * ((jax.nn.silu(gate) * up) @ w_down[l])
    return x
```

```python
import numpy as np
from concourse.bass_utils import run_bass_kernel_spmd
import concourse.bass as bass
import concourse.mybir as mybir
from contextlib import ExitStack

F32 = mybir.dt.float32
BF16 = mybir.dt.bfloat16
ALU = mybir.AluOpType
AF = mybir.ActivationFunctionType
AX = mybir.AxisListType

EPOCH = 16000
NDSEM = {"sp": 16, "pool": 12, "act": 8, "dve": 2, "pe": 2}
ENGS = ("pe", "dve", "act", "pool", "sp")


def _region(ap):
    t = ap.tensor
    name = t.name
    pat = ap.ap
    off = int(ap.offset)
    space = str(ap.space)
    if "DRAM" in space.upper() or "HBM" in space.upper():
        lo = off
        hi = off
        for st, cnt in pat:
            if st >= 0:
                hi += st * (cnt - 1)
            else:
                lo += st * (cnt - 1)
        return (name, 0, 1, lo, hi + 1)
    shp = t.shape
    row = 1
    for s in shp[1:]:
        row *= s
    p0 = off // row
    f0 = off % row
    np_ = pat[0][1]
    lo = f0
    hi = f0
    for st, cnt in pat[1:]:
        if st >= 0:
            hi += st * (cnt - 1)
        else:
            lo += st * (cnt - 1)
    if "PSUM" in space.upper():
        esz = 4 if ap.dtype == F32 else 2
        b0 = (lo * esz) // 2048
        b1 = (hi * esz) // 2048
        return (name, 0, 128, b0 * 2048 // esz, (b1 + 1) * 2048 // esz)
    return (name, p0, p0 + np_, lo, hi + 1)


def _overlap(a, b):
    return a[1] < b[2] and b[1] < a[2] and a[3] < b[4] and b[3] < a[4]


def _covers(a, b):
    return a[1] <= b[1] and a[2] >= b[2] and a[3] <= b[3] and a[4] >= b[4]


class Instr:
    __slots__ = ("eng", "fn", "deps", "is_dma", "idx", "inc_n", "dma_n", "raw_same")

    def __init__(self, eng, fn, is_dma):
        self.eng = eng
        self.fn = fn
        self.is_dma = is_dma
        self.deps = set()
        self.inc_n = None
        self.dma_n = None


class Prog:
    def __init__(self, nc):
        self.nc = nc
        self.ins = []
        self.track = {}
        self.n_dma = 0

    def op(self, eng, fn, reads=(), writes=(), dma=False):
        I = Instr(eng, fn, dma)
        I.idx = len(self.ins)
        self.ins.append(I)
        if dma:
            if not hasattr(self, "n_dma_e"):
                self.n_dma_e = {e: 0 for e in ENGS}
            I.dma_n = self.n_dma_e[eng]
            self.n_dma_e[eng] += 1
            self.n_dma += 1
        reads = list(reads)
        writes = list(writes)
        for ap in reads:
            if "PSUM" in str(ap.space).upper():
                writes.append(ap)
        for ap in reads:
            r = _region(ap)
            lst = self.track.setdefault(r[0], [])
            merged = False
            for rec in lst:
                q, j, w, en = rec
                if w:
                    if _overlap(r, q):
                        I.deps.add((j, "raw"))
                elif (not dma) and en == eng and q == r:
                    rec[1] = I.idx
                    merged = True
            if not merged:
                lst.append([r, I.idx, False, None if dma else eng])
        for ap in writes:
            r = _region(ap)
            lst = self.track.setdefault(r[0], [])
            keep = []
            for rec in lst:
                q, j, w, en = rec
                if j == I.idx:
                    keep.append(rec)
                    continue
                if _overlap(r, q):
                    I.deps.add((j, "waw" if w else "war"))
                    if _covers(r, q):
                        continue
                keep.append(rec)
            keep.append([r, I.idx, True, None if dma else eng])
            self.track[r[0]] = keep
        return I

    def barrier(self):
        last = {}
        dmas = []
        for I in self.ins[self._bar_start:] if hasattr(self, "_bar_start") else self.ins:
            if I.fn is None:
                continue
            if I.is_dma:
                dmas.append(I.idx)
            else:
                last[I.eng] = I.idx
        for e in ENGS:
            I = Instr(e, None, False)
            I.idx = len(self.ins)
            self.ins.append(I)
            for f, j in last.items():
                if f != e:
                    I.deps.add((j, "raw"))
            for j in dmas:
                I.deps.add((j, "raw"))
        self._bar_start = len(self.ins)
        self.track = {}

    def emit(self):
        nc = self.nc
        ins = self.ins
        need_inc = set()
        real = []
        for I in ins:
            rd = {}
            for (j, kind) in I.deps:
                J = ins[j]
                if J.is_dma:
                    rd[j] = True
                    continue
                if J.eng == I.eng and not I.is_dma:
                    if I.eng == "pe":
                        continue
                    if kind != "raw":
                        continue
                rd[j] = True
            real.append(sorted(rd.keys()))
            for j in rd:
                if not ins[j].is_dma:
                    need_inc.add(j)
        cnt = {e: 0 for e in ENGS}
        for I in ins:
            if (not I.is_dma) and I.idx in need_inc:
                I.inc_n = cnt[I.eng]
                cnt[I.eng] += 1
        with ExitStack() as es:
            sems = {}
            for e in ENGS:
                n_ep = (cnt[e] + EPOCH - 1) // EPOCH
                sems[e] = [es.enter_context(nc.semaphore(f"s_{e}_{k}")) for k in range(n_ep)]
            ndma_e = getattr(self, "n_dma_e", {e: 0 for e in ENGS})
            dsems = {}
            for e in ENGS:
                n = min(NDSEM[e], ndma_e.get(e, 0))
                dsems[e] = [es.enter_context(nc.semaphore(f"s_dma_{e}_{k}")) for k in range(n)]
            per_eng = {e: [] for e in ENGS}
            for I in ins:
                per_eng[I.eng].append(I)
            block = es.enter_context(nc.Block())

            def run_engine(ename, eh):
                seen = {}
                seen_d = {}
                for I in per_eng[ename]:
                    waits = {}
                    dwaits = {}
                    deps = list(real[I.idx])
                    for j in deps:
                        J = ins[j]
                        if J.is_dma:
                            K = len(dsems[J.eng])
                            k = (J.eng, J.dma_n % K)
                            v = 16 * (J.dma_n // K + 1)
                            if seen_d.get(k, 0) < v:
                                dwaits[k] = max(dwaits.get(k, 0), v)
                        else:
                            key = (J.eng, J.inc_n // EPOCH)
                            v = J.inc_n % EPOCH + 1
                            if seen.get(key, 0) < v:
                                waits[key] = max(waits.get(key, 0), v)
                    if I.is_dma:
                        K = len(dsems[I.eng])
                        if I.dma_n >= K:
                            k = (I.eng, I.dma_n % K)
                            v = 16 * (I.dma_n // K)
                            if seen_d.get(k, 0) < v:
                                dwaits[k] = max(dwaits.get(k, 0), v)
                    for key, v in waits.items():
                        eh.wait_ge(sems[key[0]][key[1]], v)
                        seen[key] = v
                    for k, v in dwaits.items():
                        eh.wait_ge(dsems[k[0]][k[1]], v)
                        seen_d[k] = v
                    if I.fn is None:
                        continue
                    bi = I.fn(eh)
                    if I.is_dma:
                        K = len(dsems[I.eng])
                        bi.then_inc(dsems[I.eng][I.dma_n % K], 16)
                    elif I.inc_n is not None:
                        bi.then_inc(sems[I.eng][I.inc_n // EPOCH], 1)
                last = {}
                for I in per_eng[ename]:
                    if I.is_dma and I.fn is not None:
                        K = len(dsems[I.eng])
                        k = (I.eng, I.dma_n % K)
                        last[k] = max(last.get(k, 0), 16 * (I.dma_n // K + 1))
                for k, v in last.items():
                    if seen_d.get(k, 0) < v:
                        eh.wait_ge(dsems[k[0]][k[1]], v)

            @block.sync
            def _(e):
                run_engine("sp", e)

            @block.tensor
            def _(e):
                run_engine("pe", e)

            @block.vector
            def _(e):
                run_engine("dve", e)

            @block.scalar
            def _(e):
                run_engine("act", e)

            @block.gpsimd
            def _(e):
                run_engine("pool", e)

    def dma(self, out, in_, eng="sp"):
        return self.op(eng, lambda e: e.dma_start(out=out, in_=in_), [in_], [out], dma=True)

    def mm(self, out, lhsT, rhs, start=True, stop=True, **kw):
        rd = [lhsT, rhs] + ([] if start else [out])
        return self.op("pe", lambda e: e.matmul(out, lhsT, rhs, start=start, stop=stop, **kw), rd, [out])

    def tr(self, out, in_, ident):
        return self.op("pe", lambda e: e.transpose(out, in_, ident), [in_, ident], [out])

    def actv(self, out, in_, func, bias=None, scale=1.0, accum_out=None, eng="act"):
        rd = [in_]
        kw = {}
        if bias is not None:
            kw["bias"] = bias
            if not isinstance(bias, (int, float)):
                rd.append(bias)
        if not isinstance(scale, (int, float)):
            rd.append(scale)
        wr = [out]
        if accum_out is not None:
            kw["accum_out"] = accum_out
            wr.append(accum_out)
        return self.op(eng, lambda e: e.activation(out=out, in_=in_, func=func, scale=scale, **kw), rd, wr)

    def tt(self, out, in0, in1, op, eng="dve"):
        return self.op(eng, lambda e: e.tensor_tensor(out=out, in0=in0, in1=in1, op=op), [in0, in1], [out])

    def ts(self, out, in0, s1, s2=None, op0=ALU.mult, op1=None, eng="dve", accum_out=None):
        rd = [in0]
        if not isinstance(s1, (int, float)):
            rd.append(s1)
        if s2 is not None and not isinstance(s2, (int, float)):
            rd.append(s2)
        kw = {}
        wr = [out]
        if op1 is not None:
            kw["op1"] = op1
        if accum_out is not None:
            kw["accum_out"] = accum_out
            wr.append(accum_out)
        return self.op(eng, lambda e: e.tensor_scalar(out=out, in0=in0, scalar1=s1, scalar2=s2, op0=op0, **kw), rd, wr)

    def stt(self, out, in0, scalar, in1, op0, op1, eng="dve"):
        rd = [in0, in1]
        if not isinstance(scalar, (int, float)):
            rd.append(scalar)
        return self.op(eng, lambda e: e.scalar_tensor_tensor(out=out, in0=in0, scalar=scalar, in1=in1, op0=op0, op1=op1), rd, [out])

    def copy(self, out, in_, eng="dve"):
        if eng == "act":
            return self.op("act", lambda e: e.copy(out=out, in_=in_), [in_], [out])
        return self.op(eng, lambda e: e.tensor_copy(out=out, in_=in_), [in_], [out])

    def memset(self, ap, val, eng="dve"):
        return self.op(eng, lambda e: e.memset(ap, val), [], [ap])

    def scan(self, out, d0, d1, initial, op0, op1):
        rd = [d0, d1]
        if not isinstance(initial, (int, float)):
            rd.append(initial)
        return self.op("dve", lambda e: e.tensor_tensor_scan(out=out, data0=d0, data1=d1, initial=initial, op0=op0, op1=op1), rd, [out])

    def reduce(self, out, in_, op=ALU.add, axis=AX.X, eng="dve"):
        return self.op(eng, lambda e: e.tensor_reduce(out=out, in_=in_, axis=axis, op=op), [in_], [out])

    def recip(self, out, in_):
        return self.op("dve", lambda e: e.reciprocal(out=out, in_=in_), [in_], [out])

D = 2048
NT = 2048
NB = NT // 128
KC = 16
DFF = 5632
JC = DFF // 128
SB_BASE = 16512
SB_END = 229376
HEADS = 16
C_DECAY = -0.6065306597126334


def dram_bcast(ap, nparts):
    pat = [list(p) for p in ap.ap]
    assert pat[0][1] == 1
    pat[0] = [0, nparts]
    return bass.AP(ap.tensor, int(ap.offset), pat)


def fbcast(ap, n):
    pat = [list(p) for p in ap.ap]
    assert pat[-1][1] == 1
    pat[-1] = [0, n]
    return bass.AP(ap.tensor, int(ap.offset), pat)


def mid_bcast(ap, n):
    pat = [list(p) for p in ap.ap]
    pat = [pat[0], [0, n]] + pat[1:]
    return bass.AP(ap.tensor, int(ap.offset), pat)


class K:
    def __init__(self, dbg=None):
        self.nc = bass.Bass("TRN2", target_bir_lowering=False)
        self.P = Prog(self.nc)
        self.dbg = dbg or {}
        self.nm = 0
        nc = self.nc
        self.ps = [nc.alloc_psum_tensor(f"ps{i}", [128, 512], F32) for i in range(6)]
        self.pb = [nc.alloc_psum_tensor(f"pb{i}", [128, 1024], BF16) for i in range(2)]
        self.din_cache = {}

    def din(self, name, shape, dt=F32):
        if name not in self.din_cache:
            self.din_cache[name] = self.nc.dram_tensor(name, list(shape), dt, kind="ExternalInput").ap()
        return self.din_cache[name]

    def dout(self, name, shape, dt=F32):
        return self.nc.dram_tensor(name, list(shape), dt, kind="ExternalOutput").ap()

    def dscratch(self, name, shape, dt):
        return self.nc.dram_tensor(name, list(shape), dt, kind="Internal").ap()

    def sb(self, name, shape, dt, off):
        esz = 4 if dt == F32 else 2
        n = 1
        for s in shape[1:]:
            n *= s
        assert off % 32 == 0, (name, off)
        assert off >= SB_BASE and off + n * esz <= SB_END, (name, off, n * esz)
        self.nm += 1
        return self.nc.alloc_sbuf_tensor_at(f"{name}_{self.nm}", list(shape), dt, offset=off)

    def build(self):
        P = self.P
        nc = self.nc
        o = SB_BASE
        self.ident = self.sb("ident", [128, 128], BF16, o); o += 256
        self.identf = self.sb("identf", [128, 128], F32, o); o += 512
        self.s1 = self.sb("s1", [128, KC], F32, o); o += 64
        self.sh1 = self.sb("sh1", [128, KC], F32, o); o += 64
        self.s2 = self.sb("s2", [128, KC], F32, o); o += 64
        self.sh2 = self.sb("sh2", [128, KC], F32, o); o += 64
        self.cact = self.sb("cact", [128, KC], F32, o); o += 64
        self.mhalf = self.sb("mhalf", [128, 1], F32, o); o += 32
        self.flag = self.sb("flag", [128, 1], F32, o); o += 32
        self.hprev = self.sb("hprev", [128, KC], BF16, o); o += 32
        self.const_used = o
        self.CONST_END = o = SB_BASE + 13312
        self.A = o
        self.hT = self.sb("hT", [128, KC, NT + 1], BF16, self.A)
        self.B = self.A + 65600
        self.ysc = self.dscratch("ysc", [NB, 128, KC, 128], BF16)
        self.C = self.B + 65536
        self.C_SIZE = SB_END - self.C
        idn = self.din("idn", [128, 128])
        P.dma(self.identf[:], idn)
        P.copy(self.ident[:], self.identf[:])
        P.memset(self.mhalf[:], -0.5, eng="pool")
        P.dma(self.flag[:], self.din("flag", [128, 1]))
        self.x_own = self.din("x_own", [NT, D])
        self.x_pre = self.din("x_pre", [NT, D])
        self.out = self.dout("out", [NT, D])
        self.preconvert()
        self.phase0()
        P.barrier()
        mode = self.dbg.get("mode", "full")
        if mode == "full":
            self.sb_consts()
            self.rwkv_consts()
            P.barrier()
            self.norm_phase(self.x_pre, self.s1, self.sh1, first=True)
            P.barrier()
            self.rwkv_phase("pre")
            P.barrier()
            g0b = self.mod_gen((3, 4), self.B, self.ps[5])
            self.sb_phase("pre", extra=g0b)
            for _ in g0b:
                pass
            self.phase0b_finish(self.ps[5])
            P.copy(self.hprev[:], self.hT[:, :, NT])
            P.ts(self.Hst[:].rearrange("p h v -> p (h v)"), self.Hst[:].rearrange("p h v -> p (h v)"), self.flag[0:64, :], None, op0=ALU.mult)
            P.barrier()
            self.norm_phase(self.x_own, self.s1, self.sh1, first=False)
            P.ts(self.hT[:, :, 0], self.hprev[:], self.flag[:], None, op0=ALU.mult)
            P.barrier()
            self.rwkv_phase("own")
            P.barrier()
            self.sb_phase("own")
            P.barrier()
            self.phaseF()
        if mode == "p0":
            self.dump_sb("d_modT", self.modT, [128, 96], F32)
        elif mode == "norm":
            self.norm_phase(self.x_own, self.s1, self.sh1, first=True)
            self.dump_sb("d_hT", self.hT, [128, KC, NT + 1], BF16)
        elif mode == "sb":
            self.sb_consts()
            self.norm_phase(self.x_pre, self.s1, self.sh1, first=True)
            P.barrier()
            self.sb_phase("pre")
            P.barrier()
            self.norm_phase(self.x_own, self.s1, self.sh1, first=False)
            P.barrier()
            self.sb_phase("own")
        elif mode == "rwkv":
            self.sb_consts()
            self.rwkv_consts()
            self.norm_phase(self.x_pre, self.s1, self.sh1, first=True)
            P.barrier()
            self.rwkv_phase("pre")
            P.copy(self.hprev[:], self.hT[:, :, NT])
            P.ts(self.Hst[:].rearrange("p h v -> p (h v)"), self.Hst[:].rearrange("p h v -> p (h v)"), self.flag[0:64, :], None, op0=ALU.mult)
            P.barrier()
            self.norm_phase(self.x_own, self.s1, self.sh1, first=False)
            P.ts(self.hT[:, :, 0], self.hprev[:], self.flag[:], None, op0=ALU.mult)
            P.barrier()
            self.rwkv_phase("own")
            dy = self.dout("d_ysc", [NB, 128, KC, 128], BF16)
            P.dma(dy, self.ysc)
            self.dump_sb("d_H", self.Hst, [64, HEADS, 64], F32)
        elif mode == "ffn":
            yin = self.din("ysc_in", [NB, 128, KC, 128], BF16)
            P.dma(self.ysc, yin)
            self.phaseF()
        P.emit()
        return nc

    def preconvert(self):
        P = self.P
        self.wo_t = self.dscratch("wo_t", [4, 128, KC, 512], BF16)
        self.wgu_t = self.dscratch("wgu_t", [JC // 2, 128, 2, KC, 256], BF16)
        self.wd_t = self.dscratch("wd_t", [4, JC // 4, 128, 4, 512], BF16)
        w_out = self.din("w_out", [D, D]).rearrange("(kc p) n -> p kc n", p=128)
        wgu = self.din("w_gate_up", [D, 2 * DFF]).rearrange("(kc p) n -> p kc n", p=128)
        wdn = self.din("w_down", [DFF, D]).rearrange("(j p) n -> p j n", p=128)
        q = []
        for ct in range(4):
            q.append((self.wo_t[ct], w_out[:, :, ct * 512:(ct + 1) * 512]))
        for jg in range(JC // 2):
            for g in range(2):
                q.append((self.wgu_t[jg, :, g], wgu[:, :, g * DFF + jg * 256: g * DFF + (jg + 1) * 256]))
        for ct in range(4):
            for j4 in range(JC // 4):
                q.append((self.wd_t[ct, j4], wdn[:, j4 * 4:(j4 + 1) * 4, ct * 512:(ct + 1) * 512]))
        self.conv_q = q

    def conv_some(self, n):
        for _ in range(n):
            if self.conv_q:
                d, s_ = self.conv_q.pop(0)
                self.P.dma(d, s_, eng="pool")

    def dump_sb(self, name, t, shape, dt):
        d = self.dout(name, shape, dt)
        self.P.dma(d, t[:])

    def phase0(self):
        P = self.P
        C = self.C
        ccol = self.din("ccol", [128, KC])
        ctmp = self.sb("ctmp", [128, KC], F32, C)
        P.dma(ctmp[:], ccol)
        P.actv(self.cact[:], ctmp[:], AF.Silu)
        bsb = self.sb("bsb", [128, 96], F32, C + 64)
        P.dma(bsb[:], self.din("b_ada_col", [128, 96]))
        modT = self.sb("modT", [128, 96], F32, C + 64 + 384)
        n1 = self.sb("n1", [128, KC], F32, C + 1024 - 128)
        P.dma(n1[:], self.din("n1g_col", [128, KC]))
        for _ in self.mod_gen((0, 1), C + 1024, self.ps[0]):
            pass
        P.tt(modT[:, 0:32], self.ps[0][:, 0:32], bsb[:, 0:32], ALU.add)
        P.stt(self.s1[:], modT[:, 16:32], 1.0, n1[:], ALU.add, ALU.mult)
        P.copy(self.sh1[:], modT[:, 0:16])

    def mod_gen(self, js, wbase, ps):
        P = self.P
        wbuf = [self.sb(f"wada{wbase}_{i}", [128, KC, 512], F32, wbase + i * KC * 512 * 4) for i in range(2)]
        wv = self.din("w_ada", [D, 6 * D]).rearrange("(kc p) n -> p kc n", p=128)
        it = 0
        for j in js:
            for q in range(4):
                wb = wbuf[it % 2]
                it += 1
                P.dma(wb[:], wv[:, :, j * D + q * 512: j * D + (q + 1) * 512])
                for fb in range(4):
                    col = j * 16 + q * 4 + fb
                    for kc in range(KC):
                        P.mm(ps[:, col:col + 1], wb[:, kc, fb * 128:(fb + 1) * 128], self.cact[:, kc:kc + 1],
                             start=(kc == 0), stop=(kc == KC - 1))
                    yield

    def phase0b_finish(self, ps):
        P = self.P
        o = self.p0b_off
        bsb2 = self.sb("bsb2", [128, 32], F32, o); o += 128
        modT2 = self.sb("modT2", [128, 32], F32, o); o += 128
        n2 = self.sb("n2", [128, KC], F32, o); o += 64
        assert o <= self.CONST_END, o
        P.dma(bsb2[:], self.din("b_ada_col", [128, 96])[:, 48:80])
        P.dma(n2[:], self.din("n2g_col", [128, KC]))
        P.tt(modT2[:], ps[:, 48:80], bsb2[:], ALU.add)
        P.stt(self.s2[:], modT2[:, 16:32], 1.0, n2[:], ALU.add, ALU.mult)
        P.copy(self.sh2[:], modT2[:, 0:16])

    def _cb_src(self):
        a = self.cact[:]
        pat = [list(p) for p in a.ap]
        return bass.AP(a.tensor, int(a.offset), [pat[0], pat[1], [0, 128]])

    def norm_phase(self, xsrc, sc, sh, first, cbase=None):
        P = self.P
        C = self.C if cbase is None else cbase
        xt = [self.sb(f"xt{i}", [128, D], F32, C + i * 8192) for i in range(2)]
        xn = [self.sb(f"xn{i}", [128, D], BF16, C + 16384 + i * 4096) for i in range(2)]
        junk = self.sb("junk", [128, D], BF16, C + 24576)
        st = self.sb("nst", [128, 4 * NB], F32, C + 28672)
        if first:
            P.memset(self.hT[:, :, 0:1], 0.0)
        for blk in range(self.dbg.get("nb", NB)):
            x_t = xt[blk % 2]
            x_n = xn[blk % 2]
            P.dma(x_t[:], xsrc[blk * 128:(blk + 1) * 128, :])
            ss = st[:, 4 * blk:4 * blk + 1]
            rs = st[:, 4 * blk + 1:4 * blk + 2]
            P.actv(junk[:], x_t[:], AF.Square, accum_out=ss)
            P.ts(rs, ss, 1.0 / D, 1e-6, op0=ALU.mult, op1=ALU.add)
            P.actv(rs, rs, AF.Sqrt)
            P.recip(rs, rs)
            P.ts(x_n[:], x_t[:], rs, None, op0=ALU.mult)
            if self.dbg.get("notr"):
                continue
            for g in range(2):
                pb = self.pb[g]
                for i in range(8):
                    kc = g * 8 + i
                    P.tr(pb[:, i * 128:(i + 1) * 128], x_n[:, kc * 128:(kc + 1) * 128], self.ident[:])
                for i in range(8):
                    kc = g * 8 + i
                    dst = self.hT[:, kc, 1 + blk * 128: 1 + (blk + 1) * 128]
                    src = pb[:, i * 128:(i + 1) * 128]
                    if self.dbg.get("noevac"):
                        continue
                    if (i % 2 == 0 or self.dbg.get("onlyact")) and not self.dbg.get("onlydve"):
                        P.actv(dst, src, AF.Identity, bias=sh[:, kc:kc + 1], scale=sc[:, kc:kc + 1])
                    else:
                        P.ts(dst, src, sc[:, kc:kc + 1], sh[:, kc:kc + 1], op0=ALU.mult, op1=ALU.add)

    def gate_bcast(self, j, dst, cbase):
        P = self.P
        w_ada = self.din("w_ada", [D, 6 * D])
        wv = w_ada.rearrange("(kc p) n -> p kc n", p=128)
        brow = self.din("b_ada_row", [1, 6 * D])
        wbuf = [self.sb(f"wg{i}", [128, KC, 256], F32, cbase + i * KC * 256 * 4) for i in range(2)]
        bb = self.sb("bb", [128, D], F32, cbase + 2 * KC * 256 * 4)
        cbc = self.sb("cbc", [128, KC, 128], F32, cbase + 2 * KC * 256 * 4 + 8192)
        P.copy(cbc[:], self._cb_src())
        P.dma(bb[:], dram_bcast(brow[:, j * D:(j + 1) * D], 128))
        for q in range(8):
            wb = wbuf[q % 2]
            P.dma(wb[:], wv[:, :, j * D + q * 256: j * D + (q + 1) * 256])
            ps = self.ps[q % 2]
            for kc in range(KC):
                P.mm(ps[:, 0:256], cbc[:, kc, :], wb[:, kc, :], start=(kc == 0), stop=(kc == KC - 1))
            P.tt(dst[:, q * 256:(q + 1) * 256], ps[:, 0:256], bb[:, q * 256:(q + 1) * 256], ALU.add)

    def phaseF(self):
        P = self.P
        C = self.C
        nc = self.nc
        self.conv_some(1000)
        gt = self.sb("gt", [128, D], F32, C)
        self.gate_bcast(2, gt, C + 8192)
        P.barrier()
        wres = self.sb("wo_res", [128, 4, KC, 512], BF16, self.B)
        for ct in range(4):
            P.dma(wres[:, ct], self.wo_t[ct])
        o = C + 8192
        yblk = [self.sb(f"yblk{i}", [128, KC, 128], BF16, o + i * 4096) for i in range(2)]; o += 8192
        xq = [self.sb(f"xq{i}", [128, D], F32, o + i * 8192) for i in range(2)]; o += 16384
        x1 = [self.sb(f"x1{i}", [128, D], F32, o + i * 8192) for i in range(2)]; o += 16384
        assert o <= SB_END
        for blk in range(NB):
            ts_ = slice(blk * 128, (blk + 1) * 128)
            yb = yblk[blk % 2]
            P.dma(yb[:], self.ysc[blk])
            P.dma(xq[blk % 2][:], self.x_own[ts_, :])
            for ct in range(4):
                cs = slice(ct * 512, (ct + 1) * 512)
                ps = self.ps[ct]
                for kc in range(KC):
                    P.mm(ps[:], yb[:, kc, :], wres[:, ct, kc, :], start=(kc == 0), stop=(kc == KC - 1))
                P.tt(x1[blk % 2][:, cs], ps[:], gt[:, cs], ALU.mult)
            P.tt(x1[blk % 2][:], x1[blk % 2][:], xq[blk % 2][:], ALU.add, eng="pool")
            P.dma(self.out[ts_, :], x1[blk % 2][:], eng="act")
        P.barrier()
        self.norm_phase(self.out, self.s2, self.sh2, first=True, cbase=C + 8192)
        P.barrier()
        self.gate_bcast(5, gt, C + 8192)
        P.barrier()
        TT = 512
        actT = self.sb("actT", [128, JC, TT], BF16, self.B)
        ob = self.B + JC * TT * 2
        xr = [self.sb(f"xr{i}", [128, 512], F32, ob + i * 2048) for i in range(2)]; ob += 4096
        xo = [self.sb(f"xo{i}", [128, 512], F32, ob + i * 2048) for i in range(2)]; ob += 4096
        sil = [self.sb(f"sil{i}", [128, 512], F32, ob + i * 2048) for i in range(2)]; ob += 4096
        assert ob <= self.C
        o = C + 8192
        wgu_sb = [self.sb(f"wgusb{i}", [128, 2, KC, 256], BF16, o + i * 16384) for i in range(2)]; o += 32768
        wd_bf = [self.sb(f"wdbf{i}", [128, 4, 512], BF16, o + i * 4096) for i in range(3)]; o += 12288
        assert o <= SB_END, o
        for tt in range(NT // TT):
            t0 = tt * TT
            tsl = slice(1 + t0, 1 + t0 + TT)
            for jg in range(JC // 2):
                b = jg % 2
                P.dma(wgu_sb[b][:], self.wgu_t[jg])
                for jj in range(2):
                    j = jg * 2 + jj
                    pg = self.ps[4]
                    pu = self.ps[5]
                    for kc in range(KC):
                        P.mm(pg[:], wgu_sb[b][:, 0, kc, jj * 128:(jj + 1) * 128], self.hT[:, kc, tsl], start=(kc == 0), stop=(kc == KC - 1))
                    for kc in range(KC):
                        P.mm(pu[:], wgu_sb[b][:, 1, kc, jj * 128:(jj + 1) * 128], self.hT[:, kc, tsl], start=(kc == 0), stop=(kc == KC - 1))
                    s_ = sil[j % 2]
                    P.actv(s_[:], pg[:], AF.Silu)
                    P.tt(actT[:, j, :], s_[:], pu[:], ALU.mult)
            nd = 0
            for ct in range(4):
                cs = slice(ct * 512, (ct + 1) * 512)
                for j4 in range(JC // 4):
                    wb = wd_bf[nd % 3]
                    nd += 1
                    P.dma(wb[:], self.wd_t[ct, j4])
                    for jj in range(4):
                        j = j4 * 4 + jj
                        for tb in range(4):
                            tl = slice(tb * 128, (tb + 1) * 128)
                            P.mm(self.ps[tb][:], actT[:, j, tl], wb[:, jj, :], start=(j == 0), stop=(j == JC - 1))
                for tb in range(4):
                    r0 = t0 + tb * 128
                    xr_ = xr[tb % 2]
                    xo_ = xo[tb % 2]
                    P.dma(xr_[:], self.out[r0:r0 + 128, cs])
                    P.tt(xo_[:], self.ps[tb][:], gt[:, cs], ALU.mult)
                    P.tt(xo_[:], xo_[:], xr_[:], ALU.add, eng="pool")
                    P.dma(self.out[r0:r0 + 128, cs], xo_[:], eng="act")

    def sb_consts(self):
        P = self.P
        o = (self.const_used + 31) // 32 * 32
        self.mdiag = self.sb("mdiag", [128, 128], F32, o); o += 512
        self.zeros = self.sb("zeros", [128, 512], F32, o); o += 2048
        assert o <= self.CONST_END, o
        self.rw_const_off = o
        P.dma(self.mdiag[:], self.din("mdiag", [128, 128]))
        P.memset(self.zeros[:], 0.0)

    def sb_phase(self, seg, extra=None):
        P = self.P
        C = self.C
        own = seg == "own"
        w_in = self.din("w_in", [D, 6144]).rearrange("(kc p) n -> p kc n", p=128)
        if not hasattr(self, "kpre"):
            self.kpre = self.dscratch("kpre", [8, 64, 2, NT], BF16)
            self.vpre = self.dscratch("vpre", [8, 128, NB, 2, 64], BF16)
        Wsb = self.sb("Wsb", [128, KC, 384], BF16, C)
        stg = self.sb("sbstg", [128, 8, 384], F32, C + 12288)
        kT = self.sb("kT", [64, 2, 2 * NT], BF16, C + 24576)
        qT = self.sb("qT", [64, 2, NT], BF16, C + 40960)
        Vt = self.sb("Vt", [128, 2 * NB, 2, 64], BF16, C + 49152)
        o = C + 57344
        tmps = []
        for ci in range(2):
            sq = self.sb(f"sbsq{ci}", [128, 256], F32, o); o += 1024
            qkn = self.sb(f"qkn{ci}", [128, 4, 64], BF16, o); o += 512
            qkf = self.sb(f"qkf{ci}", [128, 4, 64], F32, o); o += 1024
            ssr = self.sb(f"ssr{ci}", [128, 8], F32, o); o += 32
            tmps.append((sq, qkn, qkf, ssr))
        gains = self.sb("gains", [128, 4, 64], F32, o); o += 1024
        assert o <= C + 64000, o
        extra_done = [extra is None]
        o = C
        o = C + 256
        G = [self.sb(f"G{i}", [128, 512], F32, o + i * 2048) for i in range(3)]; o += 6144
        attn = [self.sb(f"attn{i}", [128, 512], BF16, o + i * 1024) for i in range(3)]; o += 3072
        attnT = [self.sb(f"attnT{i}", [128, 4, 512], BF16, o + i * 4096) for i in range(2)]; o += 8192
        assert o <= C + 24576, o
        ysb_buf = [self.sb(f"ysbb{i}", [128, 512], BF16, C + 64000 + i * 1024) for i in range(2)]
        CPf = self.sb("CPf", [128, 4, 2 * NT + 8], F32, self.B)
        qg = self.din("q_gain", [1, 64]); kg = self.din("k_gain", [1, 64])
        for j in range(4):
            P.dma(gains[:, j, :], dram_bcast(qg if j < 2 else kg, 128))
        P.ts(gains[:, 0:2, :], gains[:, 0:2, :], 0.125, None, op0=ALU.mult)
        koff = NT if own else 0
        for sp in range(8):
            for hf in range(2):
                for j in range(3):
                    c0 = 3072 + j * 1024 + sp * 128
                    P.dma(stg[:, :, j * 128:(j + 1) * 128], w_in[:, hf * 8:(hf + 1) * 8, c0:c0 + 128])
                P.copy(Wsb[:, hf * 8:(hf + 1) * 8, :], stg[:], eng=("pool" if hf else "dve"))
            if own:
                P.dma(kT[:, :, 0:NT], self.kpre[sp])
                P.dma(Vt[:, 0:NB], self.vpre[sp])
            def ip_gen(ci, blk):
                sq_, qkn_, qkf_, ssr_ = tmps[ci]
                ps = self.ps[2 * ci + (blk // 2) % 2]
                for kc in range(KC):
                    P.mm(ps[:, 0:384], self.hT[:, kc, 1 + blk * 128: 1 + (blk + 1) * 128], Wsb[:, kc, :],
                         start=(kc == 0), stop=(kc == KC - 1))
                    if kc == 7:
                        yield
                yield
                P.actv(sq_[:], ps[:, 0:256], AF.Square)
                s4 = ssr_[:, 0:4]
                P.reduce(s4, sq_[:].rearrange("p (g d) -> p g d", d=64))
                yield
                P.ts(s4, s4, 1.0 / 64, 1e-6, op0=ALU.mult, op1=ALU.add)
                P.actv(s4, s4, AF.Sqrt)
                P.recip(s4, s4)
                yield
                P.tt(qkf_[:], ps[:, 0:256].rearrange("p (g d) -> p g d", d=64), self._b3(s4, 64), ALU.mult)
                P.tt(qkn_[:], qkf_[:], gains[:], ALU.mult, eng="pool")
                vdst = Vt[:, koff // 128 + blk].rearrange("p h d -> p (h d)")
                if own:
                    P.actv(vdst, ps[:, 256:384], AF.Copy)
                else:
                    P.actv(vdst, ps[:, 256:384], AF.Identity, scale=self.flag[:])
                yield
                pb = self.pb[ci]
                for j in range(4):
                    P.tr(pb[0:64, j * 128:(j + 1) * 128], qkn_[:, j, :], self.ident[:])
                if own:
                    P.copy(qT[:, :, blk * 128:(blk + 1) * 128], pb[0:64, 0:256].rearrange("p (h t) -> p h t", h=2), eng="act")
                P.copy(kT[:, :, koff + blk * 128: koff + (blk + 1) * 128], pb[0:64, 256:512].rearrange("p (h t) -> p h t", h=2))
                yield

            def ip_chain(ci):
                for blk in range(ci, NB, 2):
                    for _ in ip_gen(ci, blk):
                        yield

            gens = [ip_chain(0), ip_chain(1)]
            if extra is not None:
                gens.append(extra)
            alive = [True] * len(gens)
            for _ in range(3):
                next(gens[0])
            while any(alive[:2]):
                for gi in range(len(gens)):
                    if alive[gi]:
                        try:
                            next(gens[gi])
                        except StopIteration:
                            alive[gi] = False
                            if gi == 2:
                                extra_done[0] = True
            if not own:
                P.dma(self.kpre[sp], kT[:, :, 0:NT])
                P.dma(self.vpre[sp], Vt[:, 0:NB])
                if extra is not None and extra_done[0]:
                    extra = None
                P.barrier()
                continue
            P.barrier()
            tiles = []
            for R in range(NB // 4):
                tq0 = NT + R * 512
                td = tq0 // 512
                for hh in range(2):
                    for ti in range(td, -1, -1):
                        k0 = ti * 512
                        for j in range(4):
                            w = (j + 1) * 128 if ti == td else 512
                            tiles.append(dict(R=R, hh=hh, k0=k0, w=w, j=j, first=(ti == td), diag=(ti == td),
                                              gfirst=(ti == td), glast=(ti == 0), it=len(tiles), grp=(len(tiles) // 4)))

            def S1(t):
                it = t["it"]; w = t["w"]; g = G[it % 3]
                pz = self.ps[it % 4]
                qb = t["R"] * 4 + t["j"]
                P.mm(pz[:, 0:w], qT[:, t["hh"], qb * 128:(qb + 1) * 128], kT[:, t["hh"], t["k0"]:t["k0"] + w])
                P.actv(g[:, 0:w], pz[:, 0:w], AF.Sigmoid, scale=-1.0)
                if t["first"]:
                    P.tt(g[:, w - 128:w], g[:, w - 128:w], self.mdiag[:], ALU.max)

            def S2(t):
                it = t["it"]; w = t["w"]; g = G[it % 3]; k0 = t["k0"]; j = t["j"]
                if t["first"]:
                    P.memset(CPf[:, j, k0 + w:k0 + w + 1], 1.0)
                    init = 1.0
                else:
                    init = CPf[:, j, k0 + w:k0 + w + 1]
                a = CPf[:, j, k0:k0 + w]
                rev_cp = bass.AP(a.tensor, int(a.offset) + w - 1, [list(a.ap[0]), [-1, w]])
                P.scan(rev_cp, self._rev(g, w), self.zeros[:, 0:w], init, ALU.mult, ALU.add)

            def S2b(t):
                it = t["it"]; w = t["w"]; at = attn[it % 3]; k0 = t["k0"]; j = t["j"]
                P.tt(at[:, 0:w], CPf[:, j, k0 + 1:k0 + w + 1], CPf[:, j, k0:k0 + w], ALU.subtract, eng="pool")

            def S3(t):
                it = t["it"]; w = t["w"]; at = attn[it % 3]; j = t["j"]
                aTs = attnT[t["grp"] % 2]
                pt = self.pb[it % 2]
                nb_ = w // 128
                for c in range(nb_):
                    P.tr(pt[:, c * 128:(c + 1) * 128], at[:, c * 128:(c + 1) * 128], self.ident[:])
                P.copy(aTs[:, 0:nb_, j * 128:(j + 1) * 128], pt[:, 0:w].rearrange("p (c t) -> p c t", t=128),
                       eng=("act" if it % 3 else "dve"))

            def S4(t):
                if t["j"] != 3:
                    return
                aTs = attnT[t["grp"] % 2]; hh = t["hh"]; R = t["R"]
                po = self.ps[4 + (R % 2)]
                for c in range(4):
                    j0 = c if t["diag"] else 0
                    kb = (t["k0"] + c * 128) // 128
                    P.mm(po[64 * hh:64 * hh + 64, j0 * 128:512], Vt[:, kb, hh, :], aTs[:, c, j0 * 128:512],
                         start=(t["gfirst"] and c == 0), stop=(t["glast"] and c == 3), skip_group_check=True)
                if t["glast"] and hh == 1:
                    yb = ysb_buf[R % 2]
                    P.copy(yb[:], po[:, :])
                    P.dma(self.ysc[R * 4:(R + 1) * 4, :, 8 + sp, :].rearrange("b p t -> p b t"),
                          yb[:].rearrange("p (b t) -> p b t", t=128))

            nt = len(tiles)
            stages = (S1, S2, S2b, S3, S4)
            for step in range(nt + len(stages) - 1):
                for k in range(len(stages) - 1, -1, -1):
                    if 0 <= step - k < nt:
                        stages[k](tiles[step - k])
            P.barrier()

    def _b3(self, ap2, n):
        pat = [list(p) for p in ap2.ap]
        return bass.AP(ap2.tensor, int(ap2.offset), pat + [[0, n]])

    def _rev(self, t, w):
        a = t[:, 0:w]
        return bass.AP(a.tensor, int(a.offset) + w - 1, [list(a.ap[0]), [-1, w]])

    def rwkv_consts(self):
        P = self.P
        o = self.rw_const_off
        self.tri = self.sb("tri", [128, 5, 128], F32, o); o += 2560
        self.mask2 = self.sb("mask2", [128, 256], F32, o); o += 1024
        self.maskT = self.sb("maskT", [128, 128], F32, o); o += 512
        self.bones = self.sb("bones", [128, 128], F32, o); o += 512
        self.sel = self.sb("sel", [128, 2], F32, o); o += 32
        self.lnx = self.sb("lnx", [128, 16], F32, o); o += 64
        self.mucol = self.sb("mucol", [128, 6, KC], F32, o); o += 6 * KC * 4
        self.Hst = self.sb("Hst", [64, HEADS, 64], F32, o); o += 4096
        assert o <= self.CONST_END, o
        P.dma(self.tri[:], self.din("tri5", [128, 5, 128]))
        P.dma(self.mask2[:], self.din("mask2", [128, 256]))
        P.dma(self.maskT[:], self.din("maskT", [128, 128]))
        P.dma(self.bones[:], self.din("bones", [128, 128]))
        P.dma(self.sel[:], self.din("sel", [128, 2]))
        P.dma(self.lnx[:, 0:8], self.din("lnxg_col", [128, 8]))
        P.dma(self.lnx[:, 8:16], self.din("lnxb_col", [128, 8]))
        for i, nm in enumerate(("mu_w_col", "mu_a_col", "mu_g_col")):
            P.dma(self.mucol[:, i, :], self.din(nm, [128, KC]))
            P.ts(self.mucol[:, 3 + i, :], self.mucol[:, i, :], -1.0, 1.0, op0=ALU.mult, op1=ALU.add)
        P.memset(self.Hst[:], 0.0)
        self.p0b_off = (o + 31) // 32 * 32

    def rwkv_phase(self, seg):
        P = self.P
        C = self.C
        B = self.B
        own = seg == "own"
        rot = {"i": 0}

        def evac_copy(dst, src):
            rot["i"] += 1
            if rot["i"] % 2:
                P.copy(dst, src)
            else:
                P.copy(dst, src, eng="act")

        w_in = self.din("w_in", [D, 6144]).rearrange("(kc p) n -> p kc n", p=128)
        zwa = self.sb("zwa", [128, NT], BF16, C)
        zg1 = self.sb("zg1", [128, NT], BF16, C + 4096)
        zg2 = self.sb("zg2", [32, NT], BF16, C + 8192)
        o = B
        raw = self.sb("l1raw", [128, KC, 288], F32, o); o += 18432
        Wl = self.sb("Wl", [128, KC, 2, 288], BF16, o); o += 18432
        w1v = self.din("w1", [D, 64]).rearrange("(kc p) n -> p kc n", p=128)
        a1v = self.din("a1", [D, 64]).rearrange("(kc p) n -> p kc n", p=128)
        g1v = self.din("g1", [D, 160]).rearrange("(kc p) n -> p kc n", p=128)
        P.dma(raw[:, :, 0:64], w1v)
        P.dma(raw[:, :, 64:128], a1v)
        P.dma(raw[:, :, 128:288], g1v)
        for (c0, c1, mi) in ((0, 64, 0), (64, 128, 1), (128, 288, 2)):
            P.tt(Wl[:, :, 0, c0:c1], raw[:, :, c0:c1], self._b3(self.mucol[:, 3 + mi, :], c1 - c0), ALU.mult)
            P.tt(Wl[:, :, 1, c0:c1], raw[:, :, c0:c1], self._b3(self.mucol[:, mi, :], c1 - c0), ALU.mult, eng="pool")
        for tt in range(NT // 512):
            t0 = tt * 512
            pz = self.ps[(3 * tt) % 6]; pg1 = self.ps[(3 * tt + 1) % 6]; pg2 = self.ps[(3 * tt + 2) % 6]
            for (pp, c0, c1) in ((pz, 0, 128), (pg1, 128, 256), (pg2, 256, 288)):
                n = 0
                for var in (0, 1):
                    for kc in range(KC):
                        P.mm(pp[0:c1 - c0, :], Wl[:, kc, var, c0:c1], self.hT[:, kc, 1 + t0 - var: 1 + t0 - var + 512],
                             start=(n == 0), stop=(n == 2 * KC - 1))
                        n += 1
            P.actv(zwa[0:64, t0:t0 + 512], pz[0:64, :], AF.Tanh)
            P.copy(zwa[64:128, t0:t0 + 512], pz[64:128, :])
            P.actv(zg1[:, t0:t0 + 512], pg1[:, :], AF.Sigmoid)
            P.actv(zg2[:, t0:t0 + 512], pg2[0:32, :], AF.Sigmoid)
        P.barrier()
        rows = {"mu": self.din("mu_rkv", [1, 3072]), "w0": self.din("w0", [1, 1024]), "a0": self.din("a0", [1, 1024]),
                "kk": self.din("k_k", [1, 1024]), "ka": self.din("k_a", [1, 1024]), "rk": self.din("r_k", [1, 1024])}
        w2d = self.din("w2", [64, 1024]); a2d = self.din("a2", [64, 1024]); g2d = self.din("g2", [160, 1024])
        ident2 = mid_bcast(self.ident[:], 2)

        class Ctx:
            pass

        def make_ctx(ci, base):
            c = Ctx()
            c.ci = ci
            c.banks = self.ps[3 * ci:3 * ci + 3]
            c.pbank = self.pb[ci]
            c.rr = 0
            o = base
            c.Wr = self.sb(f"Wr{ci}", [128, KC, 384], BF16, o); o += 12288
            c.mu_b = self.sb(f"mu_b{ci}", [128, 384], F32, o); o += 1536
            c.omm_b = self.sb(f"omm_b{ci}", [128, 384], F32, o); o += 1536
            c.Psb = [self.sb(f"Psb{ci}_{i}", [128, 384], F32, o + i * 1536) for i in range(2)]; o += 3072
            c.Psh = self.sb(f"Psh{ci}", [128, 384], F32, o); o += 1536
            c.prkv = self.sb(f"prkv{ci}", [128, 384], F32, o); o += 1536
            c.XN = [self.sb(f"XN{ci}_{i}", [128, 2, 2, 128], BF16, o + i * 1024) for i in range(2)]; o += 2048
            c.l2w = self.sb(f"l2w{ci}", [128, 4, 128], BF16, o); o += 1024
            c.bvec = self.sb(f"bvec{ci}", [128, 5, 128], F32, o); o += 2560
            T0 = o
            c.stg = self.sb(f"rstg{ci}", [128, 8, 384], F32, T0)
            c.stg2 = self.sb(f"stg2{ci}", [128, 4, 128], F32, T0 + 12288)
            o = T0

            def f32t(nm, n=128):
                nonlocal o
                t = self.sb(f"{nm}{ci}", [128, n], F32, o); o += n * 4
                return t
            for nm in ("t_u", "sigw", "t_a", "asig"):
                setattr(c, nm, f32t(nm))
            c.E = self.sb(f"E{ci}", [128, 4, 128], F32, o); o += 2048
            for nm in ("PCt", "kk", "kk2", "kkn", "bb", "t1", "kmod", "tt1", "Ysb", "sqy", "mean", "m2", "var_", "dd", "yn"):
                setattr(c, nm, f32t(nm))
            c.ss2 = self.sb(f"ss2{ci}", [128, 8], F32, o); o += 32
            c.TM = self.sb(f"TMops{ci}", [128, 8, 128], BF16, o); o += 2048
            c.AW = self.sb(f"AW{ci}", [128, 2, 128], BF16, o); o += 512
            c.BG = self.sb(f"BG{ci}", [128, 2, 128], BF16, o); o += 512
            c.FM = self.sb(f"FMops{ci}", [64, 8, 128], BF16, o); o += 2048
            c.BGT = self.sb(f"BGT{ci}", [128, 2, 128], F32, o); o += 1024
            c.GTbm = self.sb(f"GTbm{ci}", [128, 2, 256], BF16, o); o += 1024
            c.GTkm = self.sb(f"GTkm{ci}", [128, 2, 256], BF16, o); o += 1024
            c.M0m = self.sb(f"M0m{ci}", [128, 2, 128], BF16, o); o += 512
            c.Xb = [self.sb(f"X{ci}_{i}", [128, 2, 128], BF16, o + i * 512) for i in range(2)]; o += 1024
            c.Nb = [self.sb(f"N{ci}_{i}", [128, 2, 128], BF16, o + i * 512) for i in range(2)]; o += 1024
            c.Mb = [self.sb(f"M{ci}_{i}", [128, 2, 128], BF16, o + i * 512) for i in range(2)]; o += 1024
            c.AU = self.sb(f"AU{ci}", [128, 2, 128], BF16, o); o += 512
            c.RhT = self.sb(f"RhT{ci}", [64, 2, 128], BF16, o); o += 512
            c.TcT = self.sb(f"TcT{ci}", [64, 4, 64], BF16, o); o += 512
            c.Dc = self.sb(f"Dc{ci}", [64, 4, 64], F32, o); o += 1024
            c.PCfm = self.sb(f"PCfm{ci}", [64, 8], F32, o); o += 32
            c.Htmp = self.sb(f"Htmp{ci}", [64, 2, 64], F32, o); o += 512
            c.Hbf = self.sb(f"Hbf{ci}", [64, 2, 64], BF16, o); o += 256
            c.ybuf = [self.sb(f"ybuf{ci}_{i}", [128, 128], BF16, o + i * 256) for i in range(2)]; o += 512
            c.end = o
            return c

        ctxs = [make_ctx(0, B), make_ctx(1, C + 12288)]
        assert ctxs[0].end <= C, ctxs[0].end - C
        assert ctxs[1].end <= SB_END, ctxs[1].end - SB_END

        def prep(c, rp):
            f0 = rp * 128
            for j in range(3):
                P.dma(c.mu_b[:, j * 128:(j + 1) * 128], dram_bcast(rows["mu"][:, j * 1024 + f0: j * 1024 + f0 + 128], 128))
            P.ts(c.omm_b[:], c.mu_b[:], -1.0, 1.0, op0=ALU.mult, op1=ALU.add)
            for hf in range(2):
                for j in range(3):
                    c0 = j * 1024 + f0
                    P.dma(c.stg[:, :, j * 128:(j + 1) * 128], w_in[:, hf * 8:(hf + 1) * 8, c0:c0 + 128])
                P.copy(c.Wr[:, hf * 8:(hf + 1) * 8, :], c.stg[:], eng=("pool" if hf else "dve"))
            P.dma(c.stg2[0:64, 0, :], w2d[:, f0:f0 + 128])
            P.dma(c.stg2[64:128, 1, :], a2d[:, f0:f0 + 128])
            P.dma(c.stg2[:, 2, :], g2d[0:128, f0:f0 + 128])
            P.dma(c.stg2[0:32, 3, :], g2d[128:160, f0:f0 + 128])
            P.copy(c.l2w[0:64, 0, :], c.stg2[0:64, 0, :])
            P.copy(c.l2w[64:128, 1, :], c.stg2[64:128, 1, :])
            P.copy(c.l2w[:, 2, :], c.stg2[:, 2, :])
            P.copy(c.l2w[0:32, 3, :], c.stg2[0:32, 3, :])
            for i, nm in enumerate(("w0", "a0", "kk", "ka", "rk")):
                P.dma(c.bvec[:, i, :], dram_bcast(rows[nm][:, f0:f0 + 128], 128))
            P.copy(c.Hbf[:], self.Hst[:, 2 * rp:2 * rp + 2, :])

        def prep2(c, rp):
            pm = c.banks[0]
            for kc in range(KC):
                P.mm(pm[0:1, 0:384], self.hT[:, kc, 0:1], c.Wr[:, kc, :], start=(kc == 0), stop=(kc == KC - 1))
            P.copy(c.Psh[0:1, :], pm[0:1, 0:384])
            P.dma(c.Psb[1][127:128, :], c.Psh[0:1, :])

        def blk_gen(c, rp, blk):
            def nextps():
                c.rr += 1
                return c.banks[c.rr % 3]
            TM = c.TM; AW = c.AW; BG = c.BG; FM = c.FM; BGT = c.BGT; GTbm = c.GTbm; GTkm = c.GTkm; M0m = c.M0m
            AU = c.AU; RhT = c.RhT; TcT = c.TcT; Dc = c.Dc; PCfm = c.PCfm; Htmp = c.Htmp; Hbf = c.Hbf
            E = c.E; bvec = c.bvec; l2w = c.l2w; Wr = c.Wr
            tk = slice(blk * 128, (blk + 1) * 128)
            p_rkv = c.banks[0]
            for kc in range(KC):
                P.mm(p_rkv[:, 0:384], self.hT[:, kc, 1 + blk * 128: 1 + (blk + 1) * 128], Wr[:, kc, :],
                     start=(kc == 0), stop=(kc == KC - 1))
            yield
            Pc = c.Psb[blk % 2]; Pp = c.Psb[(blk + 1) % 2]
            P.copy(Pc[:], p_rkv[:, 0:384], eng="act")
            P.dma(c.Psh[1:128, :], Pc[0:127, :])
            P.dma(c.Psh[0:1, :], Pp[127:128, :])
            P.tt(c.prkv[:], p_rkv[:, 0:384], c.omm_b[:], ALU.mult)
            yield
            P.tt(c.Psh[:], c.Psh[:], c.mu_b[:], ALU.mult, eng="pool")
            P.tt(c.prkv[:], c.prkv[:], c.Psh[:], ALU.add, eng="pool")
            r_ps = c.prkv[:, 0:128]; k_ps = c.prkv[:, 128:256]; v_ps = c.prkv[:, 256:384]
            p_l = c.banks[1]
            P.mm(p_l[:, 0:128], zwa[0:64, tk], l2w[0:64, 0, :])
            P.mm(p_l[:, 256:384], zg1[:, tk], l2w[:, 2, :], start=True, stop=False)
            P.mm(p_l[:, 256:384], zg2[0:32, tk], l2w[0:32, 3, :], start=False, stop=True)
            P.tt(c.t_u[:], p_l[:, 0:128], bvec[:, 0, :], ALU.add)
            P.copy(BG[:, 1, :], p_l[:, 256:384], eng="act")
            yield
            p_l2 = c.banks[2]
            P.mm(p_l2[:, 0:128], zwa[64:128, tk], l2w[64:128, 1, :])
            P.actv(c.sigw[:], c.t_u[:], AF.Sigmoid)
            P.tt(c.t_a[:], p_l2[:, 0:128], bvec[:, 1, :], ALU.add)
            P.actv(c.asig[:], c.t_a[:], AF.Sigmoid)
            yield
            P.tt(c.kk[:], k_ps, bvec[:, 2, :], ALU.mult)
            P.tt(c.kk2[:], c.kk[:], c.kk[:], ALU.mult, eng="pool")
            P.reduce(c.ss2[:, 0:2], c.kk2[:].rearrange("p (h d) -> p h d", d=64))
            P.ts(c.ss2[:, 0:2], c.ss2[:, 0:2], 1e-12, None, op0=ALU.add)
            P.actv(c.ss2[:, 0:2], c.ss2[:, 0:2], AF.Sqrt)
            P.recip(c.ss2[:, 0:2], c.ss2[:, 0:2])
            yield
            p_c = c.banks[1]
            for i in range(4):
                P.mm(p_c[:, i * 128:(i + 1) * 128], self.tri[:, i, :], c.sigw[:])
            P.actv(E[:].rearrange("p a f -> p (a f)"), p_c[:, :], AF.Exp)
            yield
            p_e = c.banks[2]
            P.mm(p_e[:, 0:128], self.tri[:, 4, :], c.sigw[:])
            P.actv(c.PCt[:], p_e[:, 0:128], AF.Exp)
            P.tt(c.kkn[:].rearrange("p (h d) -> p h d", d=64), c.kk[:].rearrange("p (h d) -> p h d", d=64), self._b3(c.ss2[:, 0:2], 64), ALU.mult, eng="pool")
            P.stt(c.t1[:], c.asig[:], -1.0, bvec[:, 3, :], ALU.add, ALU.mult)
            yield
            P.tt(c.bb[:], c.kkn[:], c.asig[:], ALU.mult, eng="pool")
            P.stt(c.kmod[:], c.t1[:], 1.0, k_ps, ALU.add, ALU.mult)
            yield
            P.stt(TM[:, 0, :], c.kkn[:], -1.0, E[:, 1, :], ALU.mult, ALU.mult)
            P.tt(TM[:, 1, :], r_ps, E[:, 0, :], ALU.mult)
            P.tt(TM[:, 2, :], c.bb[:], E[:, 3, :], ALU.mult, eng="pool")
            P.tt(TM[:, 3, :], c.kmod[:], E[:, 3, :], ALU.mult)
            yield
            P.tt(TM[:, 4, :], c.bb[:], E[:, 2, :], ALU.mult, eng="pool")
            P.tt(TM[:, 5, :], c.kmod[:], E[:, 2, :], ALU.mult, eng="pool")
            P.copy(TM[:, 6, :], v_ps, eng="act")
            P.copy(AW[:, :, 0:64], TM[:, 0, :].rearrange("p (h d) -> p h d", d=64), eng="pool")
            if own:
                P.tt(c.tt1[:], r_ps, c.kmod[:], ALU.mult)
                P.tt(c.tt1[:], c.tt1[:], bvec[:, 4, :], ALU.mult, eng="pool")
                P.reduce(c.ss2[:, 2:4], c.tt1[:].rearrange("p (h d) -> p h d", d=64))
                P.tt(BG[:, 0, :].rearrange("p (h d) -> p h d", d=64), v_ps.rearrange("p (h d) -> p h d", d=64), self._b3(c.ss2[:, 2:4], 64), ALU.mult)
            yield
            pbt = c.pbank
            for hh in range(2):
                for a in range(4):
                    P.tr(pbt[0:64, (hh * 4 + a) * 128:(hh * 4 + a + 1) * 128], TM[:, a, hh * 64:(hh + 1) * 64], self.ident[:])
            evac_copy(FM[:].rearrange("p a t -> p (a t)"), pbt[0:64, :])
            yield
            if own:
                pbg = c.pbank
                P.tr(pbg[:, 0:128], BG[:, 0, :], self.ident[:])
                P.tr(pbg[:, 128:256], BG[:, 1, :], self.ident[:])
                P.copy(BGT[:].rearrange("p a t -> p (a t)"), pbg[:, 0:256], eng="act")
                yield
            pGb = c.banks[1]; pM0 = c.banks[2]; pGk = c.banks[0]
            c.rr = 0
            for hh in range(2):
                aT_rT = FM[:, hh * 4:hh * 4 + 2, :].rearrange("p a t -> p (a t)")
                P.mm(pGb[:, hh * 256:(hh + 1) * 256], FM[:, hh * 4 + 2, :], aT_rT)
            P.tt(GTbm[:], pGb[:, :].rearrange("p (h t) -> p h t", h=2), mid_bcast(self.mask2[:], 2), ALU.mult)
            for hh in range(2):
                P.mm(pM0[:, hh * 128:(hh + 1) * 128], FM[:, hh * 4 + 0, :], FM[:, hh * 4 + 2, :])
            P.tt(M0m[:], pM0[:, 0:256].rearrange("p (h t) -> p h t", h=2), mid_bcast(self.maskT[:], 2), ALU.mult)
            yield
            for hh in range(2):
                aT_rT = FM[:, hh * 4:hh * 4 + 2, :].rearrange("p a t -> p (a t)")
                P.mm(pGk[:, hh * 256:(hh + 1) * 256], FM[:, hh * 4 + 3, :], aT_rT)
            P.tt(GTkm[:], pGk[:, :].rearrange("p (h t) -> p h t", h=2), mid_bcast(self.mask2[:], 2), ALU.mult)
            XNa = c.XN[0]
            Nk = GTbm[:, :, 0:128]
            Mk = M0m[:]
            P.tt(XNa[:, :, 0, :], Nk, ident2, ALU.add, eng="pool")
            yield
            pMn = nextps()
            for hh in range(2):
                P.mm(pMn[:, hh * 128:(hh + 1) * 128], Nk[:, hh, :], Mk[:, hh, :])
            Mn = c.Mb[0]
            evac_copy(Mn[:].rearrange("p h t -> p (h t)"), pMn[:, 0:256])
            pNn = nextps()
            for hh in range(2):
                P.mm(pNn[:, hh * 128:(hh + 1) * 128], Mk[:, hh, :], Nk[:, hh, :])
            evac_copy(XNa[:, :, 1, :], pNn[:, 0:256].rearrange("p (h t) -> p h t", h=2))
            yield
            cur = XNa
            for it in range(5):
                nxt = c.XN[(it + 1) % 2]
                last = it == 4
                pX = nextps()
                wcols = 128 if last else 256
                for hh in range(2):
                    P.mm(pX[:, hh * 256:hh * 256 + wcols], Mn[:, hh, :],
                         cur[:, hh, 0:(1 if last else 2), :].rearrange("p a t -> p (a t)"))
                pXv = pX[:, :].rearrange("p (h a t) -> p h a t", h=2, a=2)
                P.tt(nxt[:, :, 0, :], pXv[:, :, 0, :], cur[:, :, 0, :], ALU.add)
                if not last:
                    evac_copy(nxt[:, :, 1, :], pXv[:, :, 1, :])
                    pM2 = nextps()
                    for hh in range(2):
                        P.mm(pM2[:, hh * 128:(hh + 1) * 128], cur[:, hh, 1, :], Mn[:, hh, :])
                    Mn2 = c.Mb[(it + 1) % 2]
                    evac_copy(Mn2[:].rearrange("p h t -> p (h t)"), pM2[:, 0:256])
                    Mn = Mn2
                cur = nxt
                yield
            X = [cur[:, 0, 0, :], cur[:, 1, 0, :]]
            pW = nextps()
            for hh in range(2):
                P.mm(pW[:, hh * 64:(hh + 1) * 64], GTkm[:, hh, 0:128], TM[:, 6, hh * 64:(hh + 1) * 64])
            evac_copy(AW[:, :, 64:128], pW[:, 0:128].rearrange("p (h d) -> p h d", h=2))
            yield
            pAU = nextps()
            for hh in range(2):
                P.mm(pAU[:, hh * 128:(hh + 1) * 128], X[hh], AW[:, hh, :])
            evac_copy(AU[:].rearrange("p h t -> p (h t)"), pAU[:, 0:256])
            yield
            if own:
                pR = nextps()
                for hh in range(2):
                    P.mm(pR[0:64, hh * 128:(hh + 1) * 128], AU[:, hh, 0:64], GTbm[:, hh, 128:256])
                for hh in range(2):
                    P.tt(RhT[:, hh, :], pR[0:64, hh * 128:(hh + 1) * 128], FM[:, hh * 4 + 1, :], ALU.add)
                yield
            pTs = [nextps(), nextps()]
            for hh in range(2):
                for cc in range(2):
                    rw = slice(cc * 64, cc * 64 + 64)
                    pT = pTs[cc]
                    col = hh * 128
                    P.mm(pT[0:64, col:col + 64], AU[rw, hh, 0:64], TM[rw, 4, hh * 64:(hh + 1) * 64], skip_group_check=True)
                    P.mm(pT[0:64, col + 64:col + 128], TM[rw, 4, hh * 64:(hh + 1) * 64], AU[rw, hh, 64:128], start=True, stop=False, skip_group_check=True)
                    P.mm(pT[0:64, col + 64:col + 128], TM[rw, 5, hh * 64:(hh + 1) * 64], TM[rw, 6, hh * 64:(hh + 1) * 64], start=False, stop=True, skip_group_check=True)
            for cc in range(2):
                pTv = pTs[cc][0:64, 0:256].rearrange("p (g x) -> p g x", x=128)
                P.copy(TcT[:, cc * 2:cc * 2 + 2, :], pTv[:, :, 0:64], eng="act")
                P.copy(Dc[:, cc * 2:cc * 2 + 2, :], pTv[:, :, 64:128])
            yield
            pP = nextps()
            for hh in range(2):
                P.mm(pP[0:64, hh * 2:hh * 2 + 2], c.PCt[:, hh * 64:(hh + 1) * 64], self.sel[:])
            P.copy(PCfm[:, 0:4], pP[0:64, 0:4])
            yield
            if own:
                pY = nextps()
                for hh in range(2):
                    yo = pY[64 * hh:64 * hh + 64, 0:128]
                    P.mm(yo, AU[:, hh, 64:128], GTbm[:, hh, 128:256], start=True, stop=False, skip_group_check=True)
                    P.mm(yo, TM[:, 6, hh * 64:(hh + 1) * 64], GTkm[:, hh, 128:256], start=False, stop=False, skip_group_check=True)
            for cc in range(2):
                pH = nextps()
                for hh in range(2):
                    if own:
                        P.mm(pY[64 * hh:64 * hh + 64, cc * 64:(cc + 1) * 64], Hbf[:, hh, :], RhT[:, hh, cc * 64:(cc + 1) * 64],
                             start=False, stop=(cc == 1), skip_group_check=True)
                    P.mm(pH[0:64, hh * 64:(hh + 1) * 64], TcT[:, cc * 2 + hh, :], Hbf[:, hh, :])
                for hh in range(2):
                    Hh = self.Hst[:, 2 * rp + hh, :]
                    P.stt(Htmp[:, hh, :], Hh, PCfm[:, hh * 2 + cc:hh * 2 + cc + 1], pH[0:64, hh * 64:(hh + 1) * 64], ALU.mult, ALU.add)
                    P.tt(Hh, Htmp[:, hh, :], Dc[:, cc * 2 + hh, :], ALU.add, eng="pool")
                P.copy(Hbf[:], self.Hst[:, 2 * rp:2 * rp + 2, :], eng="pool")
                yield
            if not own:
                return
            P.copy(c.Ysb[:], pY[:, 0:128], eng="act")
            P.actv(c.sqy[:], c.Ysb[:], AF.Square)
            yield
            pS = nextps()
            P.mm(pS[:, 0:128], self.bones[:], c.Ysb[:])
            P.mm(pS[:, 128:256], self.bones[:], c.sqy[:])
            P.copy(c.mean[:], pS[:, 0:128], eng="act")
            P.tt(c.m2[:], c.mean[:], c.mean[:], ALU.mult, eng="pool")
            P.tt(c.var_[:], pS[:, 128:256], c.m2[:], ALU.subtract)
            yield
            P.ts(c.var_[:], c.var_[:], 64e-5, None, op0=ALU.add)
            P.actv(c.var_[:], c.var_[:], AF.Sqrt)
            P.recip(c.var_[:], c.var_[:])
            P.tt(c.dd[:], c.Ysb[:], c.mean[:], ALU.subtract, eng="pool")
            yield
            P.tt(c.dd[:], c.dd[:], c.var_[:], ALU.mult, eng="pool")
            P.ts(c.yn[:], c.dd[:], self.lnx[:, rp:rp + 1], self.lnx[:, 8 + rp:9 + rp], op0=ALU.mult, op1=ALU.add)
            P.tt(c.yn[:], c.yn[:], BGT[:, 0, :], ALU.add, eng="pool")
            yb = c.ybuf[blk % 2]
            P.tt(yb[:], c.yn[:], BGT[:, 1, :], ALU.mult)
            P.dma(self.ysc[blk, :, rp, :], yb[:])
            yield

        def chain(c, rp):
            for blk in range(NB):
                for _ in blk_gen(c, rp, blk):
                    yield

        for rp2 in range(4):
            rps = (2 * rp2, 2 * rp2 + 1)
            for c, rp in zip(ctxs, rps):
                prep(c, rp)
            self.conv_some(16)
            P.barrier()
            for c, rp in zip(ctxs, rps):
                prep2(c, rp)
            gens = [chain(c, rp) for c, rp in zip(ctxs, rps)]
            alive = [True, True]
            for _ in range(6):
                next(gens[0])
            while any(alive):
                for gi in range(2):
                    if alive[gi]:
                        try:
                            next(gens[gi])
                        except StopIteration:
                            alive[gi] = False
            P.barrier()

_NC_CACHE = {}


def _col(v, n):
    return np.ascontiguousarray(np.asarray(v, dtype=np.float32).reshape(n, 128).T)


def _rwkv_consts():
    tok = np.arange(128)
    same = (tok[:, None] // 64) == (tok[None, :] // 64)
    s = tok[:, None]; t = tok[None, :]
    Cd = np.float32(C_DECAY)
    tri = np.zeros((128, 5, 128), np.float32)
    tri[:, 0] = np.where(same & (s <= t), Cd, 0)
    tri[:, 1] = np.where(same & (s < t), Cd, 0)
    tri[:, 2] = np.where(same & (s > t), Cd, 0)
    tri[:, 3] = np.where(same & (s <= t), -Cd, 0)
    tri[:, 4] = np.where(same, Cd, 0)
    m_lt = (same & (s < t)).astype(np.float32)
    m_le = (same & (s <= t)).astype(np.float32)
    sel = np.zeros((128, 2), np.float32); sel[0, 0] = 1; sel[64, 1] = 1
    bones = np.kron(np.eye(2, dtype=np.float32), np.full((64, 64), 1.0 / 64, np.float32))
    return {"tri5": tri, "mask2": np.concatenate([m_lt, m_le], 1), "maskT": np.ascontiguousarray(m_lt.T),
            "sel": sel, "bones": bones}


def make_in_maps(I, cores=range(8), extra=None):
    f = lambda k: np.asarray(I[k])[0]
    shared = {
        "idn": np.eye(128, dtype=np.float32),
        "w_ada": np.ascontiguousarray(f("w_ada")),
        "b_ada_col": _col(f("b_ada"), 96),
        "b_ada_row": np.ascontiguousarray(f("b_ada").reshape(1, -1)),
        "n1g_col": _col(f("norm1_gain"), 16),
        "n2g_col": _col(f("norm2_gain"), 16),
        "w_out": np.ascontiguousarray(f("w_out")),
        "w_in": np.ascontiguousarray(f("w_in")),
        "mu_rkv": f("mu_rkv").reshape(1, -1), "w0": f("w0").reshape(1, -1), "a0": f("a0").reshape(1, -1),
        "k_k": f("k_k").reshape(1, -1), "k_a": f("k_a").reshape(1, -1), "r_k": f("r_k").reshape(1, -1),
        "w1": f("w1"), "a1": f("a1"), "g1": f("g1"), "w2": f("w2"), "a2": f("a2"), "g2": f("g2"),
        "mu_w_col": _col(f("mu_w"), 16), "mu_a_col": _col(f("mu_a"), 16), "mu_g_col": _col(f("mu_g"), 16),
        "lnxg_col": _col(f("ln_x_gain"), 8), "lnxb_col": _col(f("ln_x_bias"), 8),
        **_rwkv_consts(),
        "mdiag": np.triu(np.ones((128, 128), np.float32)),
        "q_gain": np.ascontiguousarray(f("q_norm_gain").reshape(1, 64)),
        "k_gain": np.ascontiguousarray(f("k_norm_gain").reshape(1, 64)),
        "w_gate_up": np.ascontiguousarray(f("w_gate_up")),
        "w_down": np.ascontiguousarray(f("w_down")),
    }
    maps = []
    x = np.asarray(I["x"])
    c = np.asarray(I["c"])
    for core in cores:
        b, half = core // 2, core % 2
        m = dict(shared)
        m["x_own"] = np.ascontiguousarray(x[b, half * NT:(half + 1) * NT])
        m["x_pre"] = np.ascontiguousarray(x[b, 0:NT])
        m["ccol"] = _col(c[b], 16)
        m["flag"] = np.full((128, 1), float(half), np.float32)
        if extra:
            m.update(extra(core))
        maps.append(m)
    return maps


def kernel(**inputs):
    if "k" not in _NC_CACHE:
        k = K()
        _NC_CACHE["k"] = (k, k.build())
    k, nc = _NC_CACHE["k"]
    maps = make_in_maps(inputs)
    needed = set(k.din_cache.keys())
    maps = [{k: v for k, v in m.items() if k in needed} for m in maps]
    res = run_bass_kernel_spmd(nc, maps, core_ids=list(range(8)))
    out = np.empty((4, 4096, D), np.float32)
    for core in range(8):
        b, half = core // 2, core % 2
        out[b, half * NT:(half + 1) * NT] = res.results[core]["out"]
    return out
```

```python
import numpy as np
from concourse.bass_utils import run_bass_kernel_spmd
import concourse.bass as bass
import concourse.mybir as mybir
from contextlib import ExitStack

F32 = mybir.dt.float32
BF16 = mybir.dt.bfloat16
ALU = mybir.AluOpType
AF = mybir.ActivationFunctionType
AX = mybir.AxisListType

EPOCH = 16000
NDSEM = {"sp": 16, "pool": 12, "act": 8, "dve": 2, "pe": 2}
ENGS = ("pe", "dve", "act", "pool", "sp")


def _region(ap):
    t = ap.tensor
    name = t.name
    pat = ap.ap
    off = int(ap.offset)
    space = str(ap.space)
    if "DRAM" in space.upper() or "HBM" in space.upper():
        lo = off
        hi = off
        for st, cnt in pat:
            if st >= 0:
                hi += st * (cnt - 1)
            else:
                lo += st * (cnt - 1)
        return (name, 0, 1, lo, hi + 1)
    shp = t.shape
    row = 1
    for s in shp[1:]:
        row *= s
    p0 = off // row
    f0 = off % row
    np_ = pat[0][1]
    lo = f0
    hi = f0
    for st, cnt in pat[1:]:
        if st >= 0:
            hi += st * (cnt - 1)
        else:
            lo += st * (cnt - 1)
    if "PSUM" in space.upper():
        esz = 4 if ap.dtype == F32 else 2
        b0 = (lo * esz) // 2048
        b1 = (hi * esz) // 2048
        return (name, 0, 128, b0 * 2048 // esz, (b1 + 1) * 2048 // esz)
    return (name, p0, p0 + np_, lo, hi + 1)


def _overlap(a, b):
    return a[1] < b[2] and b[1] < a[2] and a[3] < b[4] and b[3] < a[4]


def _covers(a, b):
    return a[1] <= b[1] and a[2] >= b[2] and a[3] <= b[3] and a[4] >= b[4]


class Instr:
    __slots__ = ("eng", "fn", "deps", "is_dma", "idx", "inc_n", "dma_n", "raw_same")

    def __init__(self, eng, fn, is_dma):
        self.eng = eng
        self.fn = fn
        self.is_dma = is_dma
        self.deps = set()
        self.inc_n = None
        self.dma_n = None


class Prog:
    def __init__(self, nc):
        self.nc = nc
        self.ins = []
        self.track = {}
        self.n_dma = 0

    def op(self, eng, fn, reads=(), writes=(), dma=False):
        I = Instr(eng, fn, dma)
        I.idx = len(self.ins)
        self.ins.append(I)
        if dma:
            if not hasattr(self, "n_dma_e"):
                self.n_dma_e = {e: 0 for e in ENGS}
            I.dma_n = self.n_dma_e[eng]
            self.n_dma_e[eng] += 1
            self.n_dma += 1
        reads = list(reads)
        writes = list(writes)
        for ap in reads:
            if "PSUM" in str(ap.space).upper():
                writes.append(ap)
        for ap in reads:
            r = _region(ap)
            lst = self.track.setdefault(r[0], [])
            merged = False
            for rec in lst:
                q, j, w, en = rec
                if w:
                    if _overlap(r, q):
                        I.deps.add((j, "raw"))
                elif (not dma) and en == eng and q == r:
                    rec[1] = I.idx
                    merged = True
            if not merged:
                lst.append([r, I.idx, False, None if dma else eng])
        for ap in writes:
            r = _region(ap)
            lst = self.track.setdefault(r[0], [])
            keep = []
            for rec in lst:
                q, j, w, en = rec
                if j == I.idx:
                    keep.append(rec)
                    continue
                if _overlap(r, q):
                    I.deps.add((j, "waw" if w else "war"))
                    if _covers(r, q):
                        continue
                keep.append(rec)
            keep.append([r, I.idx, True, None if dma else eng])
            self.track[r[0]] = keep
        return I

    def barrier(self):
        last = {}
        dmas = []
        for I in self.ins[self._bar_start:] if hasattr(self, "_bar_start") else self.ins:
            if I.fn is None:
                continue
            if I.is_dma:
                dmas.append(I.idx)
            else:
                last[I.eng] = I.idx
        for e in ENGS:
            I = Instr(e, None, False)
            I.idx = len(self.ins)
            self.ins.append(I)
            for f, j in last.items():
                if f != e:
                    I.deps.add((j, "raw"))
            for j in dmas:
                I.deps.add((j, "raw"))
        self._bar_start = len(self.ins)
        self.track = {}

    def emit(self):
        nc = self.nc
        ins = self.ins
        need_inc = set()
        real = []
        for I in ins:
            rd = {}
            for (j, kind) in I.deps:
                J = ins[j]
                if J.is_dma:
                    rd[j] = True
                    continue
                if J.eng == I.eng and not I.is_dma:
                    if I.eng == "pe":
                        continue
                    if kind != "raw":
                        continue
                rd[j] = True
            real.append(sorted(rd.keys()))
            for j in rd:
                if not ins[j].is_dma:
                    need_inc.add(j)
        cnt = {e: 0 for e in ENGS}
        for I in ins:
            if (not I.is_dma) and I.idx in need_inc:
                I.inc_n = cnt[I.eng]
                cnt[I.eng] += 1
        with ExitStack() as es:
            sems = {}
            for e in ENGS:
                n_ep = (cnt[e] + EPOCH - 1) // EPOCH
                sems[e] = [es.enter_context(nc.semaphore(f"s_{e}_{k}")) for k in range(n_ep)]
            ndma_e = getattr(self, "n_dma_e", {e: 0 for e in ENGS})
            dsems = {}
            for e in ENGS:
                n = min(NDSEM[e], ndma_e.get(e, 0))
                dsems[e] = [es.enter_context(nc.semaphore(f"s_dma_{e}_{k}")) for k in range(n)]
            per_eng = {e: [] for e in ENGS}
            for I in ins:
                per_eng[I.eng].append(I)
            block = es.enter_context(nc.Block())

            def run_engine(ename, eh):
                seen = {}
                seen_d = {}
                for I in per_eng[ename]:
                    waits = {}
                    dwaits = {}
                    deps = list(real[I.idx])
                    for j in deps:
                        J = ins[j]
                        if J.is_dma:
                            K = len(dsems[J.eng])
                            k = (J.eng, J.dma_n % K)
                            v = 16 * (J.dma_n // K + 1)
                            if seen_d.get(k, 0) < v:
                                dwaits[k] = max(dwaits.get(k, 0), v)
                        else:
                            key = (J.eng, J.inc_n // EPOCH)
                            v = J.inc_n % EPOCH + 1
                            if seen.get(key, 0) < v:
                                waits[key] = max(waits.get(key, 0), v)
                    if I.is_dma:
                        K = len(dsems[I.eng])
                        if I.dma_n >= K:
                            k = (I.eng, I.dma_n % K)
                            v = 16 * (I.dma_n // K)
                            if seen_d.get(k, 0) < v:
                                dwaits[k] = max(dwaits.get(k, 0), v)
                    for key, v in waits.items():
                        eh.wait_ge(sems[key[0]][key[1]], v)
                        seen[key] = v
                    for k, v in dwaits.items():
                        eh.wait_ge(dsems[k[0]][k[1]], v)
                        seen_d[k] = v
                    if I.fn is None:
                        continue
                    bi = I.fn(eh)
                    if I.is_dma:
                        K = len(dsems[I.eng])
                        bi.then_inc(dsems[I.eng][I.dma_n % K], 16)
                    elif I.inc_n is not None:
                        bi.then_inc(sems[I.eng][I.inc_n // EPOCH], 1)
                last = {}
                for I in per_eng[ename]:
                    if I.is_dma and I.fn is not None:
                        K = len(dsems[I.eng])
                        k = (I.eng, I.dma_n % K)
                        last[k] = max(last.get(k, 0), 16 * (I.dma_n // K + 1))
                for k, v in last.items():
                    if seen_d.get(k, 0) < v:
                        eh.wait_ge(dsems[k[0]][k[1]], v)

            @block.sync
            def _(e):
                run_engine("sp", e)

            @block.tensor
            def _(e):
                run_engine("pe", e)

            @block.vector
            def _(e):
                run_engine("dve", e)

            @block.scalar
            def _(e):
                run_engine("act", e)

            @block.gpsimd
            def _(e):
                run_engine("pool", e)

    def dma(self, out, in_, eng="sp"):
        return self.op(eng, lambda e: e.dma_start(out=out, in_=in_), [in_], [out], dma=True)

    def mm(self, out, lhsT, rhs, start=True, stop=True, **kw):
        rd = [lhsT, rhs] + ([] if start else [out])
        return self.op("pe", lambda e: e.matmul(out, lhsT, rhs, start=start, stop=stop, **kw), rd, [out])

    def tr(self, out, in_, ident):
        return self.op("pe", lambda e: e.transpose(out, in_, ident), [in_, ident], [out])

    def actv(self, out, in_, func, bias=None, scale=1.0, accum_out=None, eng="act"):
        rd = [in_]
        kw = {}
        if bias is not None:
            kw["bias"] = bias
            if not isinstance(bias, (int, float)):
                rd.append(bias)
        if not isinstance(scale, (int, float)):
            rd.append(scale)
        wr = [out]
        if accum_out is not None:
            kw["accum_out"] = accum_out
            wr.append(accum_out)
        return self.op(eng, lambda e: e.activation(out=out, in_=in_, func=func, scale=scale, **kw), rd, wr)

    def tt(self, out, in0, in1, op, eng="dve"):
        return self.op(eng, lambda e: e.tensor_tensor(out=out, in0=in0, in1=in1, op=op), [in0, in1], [out])

    def ts(self, out, in0, s1, s2=None, op0=ALU.mult, op1=None, eng="dve", accum_out=None):
        rd = [in0]
        if not isinstance(s1, (int, float)):
            rd.append(s1)
        if s2 is not None and not isinstance(s2, (int, float)):
            rd.append(s2)
        kw = {}
        wr = [out]
        if op1 is not None:
            kw["op1"] = op1
        if accum_out is not None:
            kw["accum_out"] = accum_out
            wr.append(accum_out)
        return self.op(eng, lambda e: e.tensor_scalar(out=out, in0=in0, scalar1=s1, scalar2=s2, op0=op0, **kw), rd, wr)

    def stt(self, out, in0, scalar, in1, op0, op1, eng="dve"):
        rd = [in0, in1]
        if not isinstance(scalar, (int, float)):
            rd.append(scalar)
        return self.op(eng, lambda e: e.scalar_tensor_tensor(out=out, in0=in0, scalar=scalar, in1=in1, op0=op0, op1=op1), rd, [out])

    def copy(self, out, in_, eng="dve"):
        if eng == "act":
            return self.op("act", lambda e: e.copy(out=out, in_=in_), [in_], [out])
        return self.op(eng, lambda e: e.tensor_copy(out=out, in_=in_), [in_], [out])

    def memset(self, ap, val, eng="dve"):
        return self.op(eng, lambda e: e.memset(ap, val), [], [ap])

    def scan(self, out, d0, d1, initial, op0, op1):
        rd = [d0, d1]
        if not isinstance(initial, (int, float)):
            rd.append(initial)
        return self.op("dve", lambda e: e.tensor_tensor_scan(out=out, data0=d0, data1=d1, initial=initial, op0=op0, op1=op1), rd, [out])

    def reduce(self, out, in_, op=ALU.add, axis=AX.X, eng="dve"):
        return self.op(eng, lambda e: e.tensor_reduce(out=out, in_=in_, axis=axis, op=op), [in_], [out])

    def recip(self, out, in_):
        return self.op("dve", lambda e: e.reciprocal(out=out, in_=in_), [in_], [out])

D = 2048
NT = 2048
NB = NT // 128
KC = 16
DFF = 5632
JC = DFF // 128
SB_BASE = 16512
SB_END = 229376
HEADS = 16
C_DECAY = -0.6065306597126334


def dram_bcast(ap, nparts):
    pat = [list(p) for p in ap.ap]
    assert pat[0][1] == 1
    pat[0] = [0, nparts]
    return bass.AP(ap.tensor, int(ap.offset), pat)


def fbcast(ap, n):
    pat = [list(p) for p in ap.ap]
    assert pat[-1][1] == 1
    pat[-1] = [0, n]
    return bass.AP(ap.tensor, int(ap.offset), pat)


def mid_bcast(ap, n):
    pat = [list(p) for p in ap.ap]
    pat = [pat[0], [0, n]] + pat[1:]
    return bass.AP(ap.tensor, int(ap.offset), pat)


class K:
    def __init__(self, dbg=None):
        self.nc = bass.Bass("TRN2", target_bir_lowering=False)
        self.P = Prog(self.nc)
        self.dbg = dbg or {}
        self.nm = 0
        nc = self.nc
        self.ps = [nc.alloc_psum_tensor(f"ps{i}", [128, 512], F32) for i in range(6)]
        self.pb = [nc.alloc_psum_tensor(f"pb{i}", [128, 1024], BF16) for i in range(2)]
        self.din_cache = {}

    def din(self, name, shape, dt=F32):
        if name not in self.din_cache:
            self.din_cache[name] = self.nc.dram_tensor(name, list(shape), dt, kind="ExternalInput").ap()
        return self.din_cache[name]

    def dout(self, name, shape, dt=F32):
        return self.nc.dram_tensor(name, list(shape), dt, kind="ExternalOutput").ap()

    def dscratch(self, name, shape, dt):
        return self.nc.dram_tensor(name, list(shape), dt, kind="Internal").ap()

    def sb(self, name, shape, dt, off):
        esz = 4 if dt == F32 else 2
        n = 1
        for s in shape[1:]:
            n *= s
        assert off % 32 == 0, (name, off)
        assert off >= SB_BASE and off + n * esz <= SB_END, (name, off, n * esz)
        self.nm += 1
        return self.nc.alloc_sbuf_tensor_at(f"{name}_{self.nm}", list(shape), dt, offset=off)

    def build(self):
        P = self.P
        nc = self.nc
        o = SB_BASE
        self.ident = self.sb("ident", [128, 128], BF16, o); o += 256
        self.identf = self.sb("identf", [128, 128], F32, o); o += 512
        self.s1 = self.sb("s1", [128, KC], F32, o); o += 64
        self.sh1 = self.sb("sh1", [128, KC], F32, o); o += 64
        self.s2 = self.sb("s2", [128, KC], F32, o); o += 64
        self.sh2 = self.sb("sh2", [128, KC], F32, o); o += 64
        self.cact = self.sb("cact", [128, KC], F32, o); o += 64
        self.mhalf = self.sb("mhalf", [128, 1], F32, o); o += 32
        self.flag = self.sb("flag", [128, 1], F32, o); o += 32
        self.hprev = self.sb("hprev", [128, KC], BF16, o); o += 32
        self.const_used = o
        self.CONST_END = o = SB_BASE + 13312
        self.A = o
        self.hT = self.sb("hT", [128, KC, NT + 1], BF16, self.A)
        self.B = self.A + 65600
        self.ysc = self.dscratch("ysc", [NB, 128, KC, 128], BF16)
        self.C = self.B + 65536
        self.C_SIZE = SB_END - self.C
        idn = self.din("idn", [128, 128])
        P.dma(self.identf[:], idn)
        P.copy(self.ident[:], self.identf[:])
        P.memset(self.mhalf[:], -0.5, eng="pool")
        P.dma(self.flag[:], self.din("flag", [128, 1]))
        self.x_own = self.din("x_own", [NT, D])
        self.x_pre = self.din("x_pre", [NT, D])
        self.out = self.dout("out", [NT, D])
        self.preconvert()
        self.phase0()
        P.barrier()
        mode = self.dbg.get("mode", "full")
        if mode == "full":
            self.sb_consts()
            self.rwkv_consts()
            P.barrier()
            self.norm_phase(self.x_pre, self.s1, self.sh1, first=True)
            P.barrier()
            self.rwkv_phase("pre")
            P.barrier()
            g0b = self.mod_gen((3, 4), self.B, self.ps[5])
            self.sb_phase("pre", extra=g0b)
            for _ in g0b:
                pass
            self.phase0b_finish(self.ps[5])
            P.copy(self.hprev[:], self.hT[:, :, NT])
            P.ts(self.Hst[:].rearrange("p h v -> p (h v)"), self.Hst[:].rearrange("p h v -> p (h v)"), self.flag[0:64, :], None, op0=ALU.mult)
            P.barrier()
            self.norm_phase(self.x_own, self.s1, self.sh1, first=False)
            P.ts(self.hT[:, :, 0], self.hprev[:], self.flag[:], None, op0=ALU.mult)
            P.barrier()
            self.rwkv_phase("own")
            P.barrier()
            self.sb_phase("own")
            P.barrier()
            self.phaseF()
        if mode == "p0":
            self.dump_sb("d_modT", self.modT, [128, 96], F32)
        elif mode == "norm":
            self.norm_phase(self.x_own, self.s1, self.sh1, first=True)
            self.dump_sb("d_hT", self.hT, [128, KC, NT + 1], BF16)
        elif mode == "sb":
            self.sb_consts()
            self.norm_phase(self.x_pre, self.s1, self.sh1, first=True)
            P.barrier()
            self.sb_phase("pre")
            P.barrier()
            self.norm_phase(self.x_own, self.s1, self.sh1, first=False)
            P.barrier()
            self.sb_phase("own")
        elif mode == "rwkv":
            self.sb_consts()
            self.rwkv_consts()
            self.norm_phase(self.x_pre, self.s1, self.sh1, first=True)
            P.barrier()
            self.rwkv_phase("pre")
            P.copy(self.hprev[:], self.hT[:, :, NT])
            P.ts(self.Hst[:].rearrange("p h v -> p (h v)"), self.Hst[:].rearrange("p h v -> p (h v)"), self.flag[0:64, :], None, op0=ALU.mult)
            P.barrier()
            self.norm_phase(self.x_own, self.s1, self.sh1, first=False)
            P.ts(self.hT[:, :, 0], self.hprev[:], self.flag[:], None, op0=ALU.mult)
            P.barrier()
            self.rwkv_phase("own")
            dy = self.dout("d_ysc", [NB, 128, KC, 128], BF16)
            P.dma(dy, self.ysc)
            self.dump_sb("d_H", self.Hst, [64, HEADS, 64], F32)
        elif mode == "ffn":
            yin = self.din("ysc_in", [NB, 128, KC, 128], BF16)
            P.dma(self.ysc, yin)
            self.phaseF()
        P.emit()
        return nc

    def preconvert(self):
        P = self.P
        self.wo_t = self.dscratch("wo_t", [4, 128, KC, 512], BF16)
        self.wgu_t = self.dscratch("wgu_t", [JC // 2, 128, 2, KC, 256], BF16)
        self.wd_t = self.dscratch("wd_t", [4, JC // 4, 128, 4, 512], BF16)
        w_out = self.din("w_out", [D, D]).rearrange("(kc p) n -> p kc n", p=128)
        wgu = self.din("w_gate_up", [D, 2 * DFF]).rearrange("(kc p) n -> p kc n", p=128)
        wdn = self.din("w_down", [DFF, D]).rearrange("(j p) n -> p j n", p=128)
        q = []
        for ct in range(4):
            q.append((self.wo_t[ct], w_out[:, :, ct * 512:(ct + 1) * 512]))
        for jg in range(JC // 2):
            for g in range(2):
                q.append((self.wgu_t[jg, :, g], wgu[:, :, g * DFF + jg * 256: g * DFF + (jg + 1) * 256]))
        for ct in range(4):
            for j4 in range(JC // 4):
                q.append((self.wd_t[ct, j4], wdn[:, j4 * 4:(j4 + 1) * 4, ct * 512:(ct + 1) * 512]))
        self.conv_q = q

    def conv_some(self, n):
        for _ in range(n):
            if self.conv_q:
                d, s_ = self.conv_q.pop(0)
                self.P.dma(d, s_, eng="pool")

    def dump_sb(self, name, t, shape, dt):
        d = self.dout(name, shape, dt)
        self.P.dma(d, t[:])

    def phase0(self):
        P = self.P
        C = self.C
        ccol = self.din("ccol", [128, KC])
        ctmp = self.sb("ctmp", [128, KC], F32, C)
        P.dma(ctmp[:], ccol)
        P.actv(self.cact[:], ctmp[:], AF.Silu)
        bsb = self.sb("bsb", [128, 96], F32, C + 64)
        P.dma(bsb[:], self.din("b_ada_col", [128, 96]))
        modT = self.sb("modT", [128, 96], F32, C + 64 + 384)
        n1 = self.sb("n1", [128, KC], F32, C + 1024 - 128)
        P.dma(n1[:], self.din("n1g_col", [128, KC]))
        for _ in self.mod_gen((0, 1), C + 1024, self.ps[0]):
            pass
        P.tt(modT[:, 0:32], self.ps[0][:, 0:32], bsb[:, 0:32], ALU.add)
        P.stt(self.s1[:], modT[:, 16:32], 1.0, n1[:], ALU.add, ALU.mult)
        P.copy(self.sh1[:], modT[:, 0:16])

    def mod_gen(self, js, wbase, ps):
        P = self.P
        wbuf = [self.sb(f"wada{wbase}_{i}", [128, KC, 512], F32, wbase + i * KC * 512 * 4) for i in range(2)]
        wv = self.din("w_ada", [D, 6 * D]).rearrange("(kc p) n -> p kc n", p=128)
        it = 0
        for j in js:
            for q in range(4):
                wb = wbuf[it % 2]
                it += 1
                P.dma(wb[:], wv[:, :, j * D + q * 512: j * D + (q + 1) * 512])
                for fb in range(4):
                    col = j * 16 + q * 4 + fb
                    for kc in range(KC):
                        P.mm(ps[:, col:col + 1], wb[:, kc, fb * 128:(fb + 1) * 128], self.cact[:, kc:kc + 1],
                             start=(kc == 0), stop=(kc == KC - 1))
                    yield

    def phase0b_finish(self, ps):
        P = self.P
        o = self.p0b_off
        bsb2 = self.sb("bsb2", [128, 32], F32, o); o += 128
        modT2 = self.sb("modT2", [128, 32], F32, o); o += 128
        n2 = self.sb("n2", [128, KC], F32, o); o += 64
        assert o <= self.CONST_END, o
        P.dma(bsb2[:], self.din("b_ada_col", [128, 96])[:, 48:80])
        P.dma(n2[:], self.din("n2g_col", [128, KC]))
        P.tt(modT2[:], ps[:, 48:80], bsb2[:], ALU.add)
        P.stt(self.s2[:], modT2[:, 16:32], 1.0, n2[:], ALU.add, ALU.mult)
        P.copy(self.sh2[:], modT2[:, 0:16])

    def _cb_src(self):
        a = self.cact[:]
        pat = [list(p) for p in a.ap]
        return bass.AP(a.tensor, int(a.offset), [pat[0], pat[1], [0, 128]])

    def norm_phase(self, xsrc, sc, sh, first, cbase=None):
        P = self.P
        C = self.C if cbase is None else cbase
        xt = [self.sb(f"xt{i}", [128, D], F32, C + i * 8192) for i in range(2)]
        xn = [self.sb(f"xn{i}", [128, D], BF16, C + 16384 + i * 4096) for i in range(2)]
        junk = self.sb("junk", [128, D], BF16, C + 24576)
        st = self.sb("nst", [128, 4 * NB], F32, C + 28672)
        if first:
            P.memset(self.hT[:, :, 0:1], 0.0)
        for blk in range(self.dbg.get("nb", NB)):
            x_t = xt[blk % 2]
            x_n = xn[blk % 2]
            P.dma(x_t[:], xsrc[blk * 128:(blk + 1) * 128, :])
            ss = st[:, 4 * blk:4 * blk + 1]
            rs = st[:, 4 * blk + 1:4 * blk + 2]
            P.actv(junk[:], x_t[:], AF.Square, accum_out=ss)
            P.ts(rs, ss, 1.0 / D, 1e-6, op0=ALU.mult, op1=ALU.add)
            P.actv(rs, rs, AF.Sqrt)
            P.recip(rs, rs)
            P.ts(x_n[:], x_t[:], rs, None, op0=ALU.mult)
            if self.dbg.get("notr"):
                continue
            for g in range(2):
                pb = self.pb[g]
                for i in range(8):
                    kc = g * 8 + i
                    P.tr(pb[:, i * 128:(i + 1) * 128], x_n[:, kc * 128:(kc + 1) * 128], self.ident[:])
                for i in range(8):
                    kc = g * 8 + i
                    dst = self.hT[:, kc, 1 + blk * 128: 1 + (blk + 1) * 128]
                    src = pb[:, i * 128:(i + 1) * 128]
                    if self.dbg.get("noevac"):
                        continue
                    if (i % 2 == 0 or self.dbg.get("onlyact")) and not self.dbg.get("onlydve"):
                        P.actv(dst, src, AF.Identity, bias=sh[:, kc:kc + 1], scale=sc[:, kc:kc + 1])
                    else:
                        P.ts(dst, src, sc[:, kc:kc + 1], sh[:, kc:kc + 1], op0=ALU.mult, op1=ALU.add)

    def gate_bcast(self, j, dst, cbase):
        P = self.P
        w_ada = self.din("w_ada", [D, 6 * D])
        wv = w_ada.rearrange("(kc p) n -> p kc n", p=128)
        brow = self.din("b_ada_row", [1, 6 * D])
        wbuf = [self.sb(f"wg{i}", [128, KC, 256], F32, cbase + i * KC * 256 * 4) for i in range(2)]
        bb = self.sb("bb", [128, D], F32, cbase + 2 * KC * 256 * 4)
        cbc = self.sb("cbc", [128, KC, 128], F32, cbase + 2 * KC * 256 * 4 + 8192)
        P.copy(cbc[:], self._cb_src())
        P.dma(bb[:], dram_bcast(brow[:, j * D:(j + 1) * D], 128))
        for q in range(8):
            wb = wbuf[q % 2]
            P.dma(wb[:], wv[:, :, j * D + q * 256: j * D + (q + 1) * 256])
            ps = self.ps[q % 2]
            for kc in range(KC):
                P.mm(ps[:, 0:256], cbc[:, kc, :], wb[:, kc, :], start=(kc == 0), stop=(kc == KC - 1))
            P.tt(dst[:, q * 256:(q + 1) * 256], ps[:, 0:256], bb[:, q * 256:(q + 1) * 256], ALU.add)

    def phaseF(self):
        P = self.P
        C = self.C
        nc = self.nc
        self.conv_some(1000)
        gt = self.sb("gt", [128, D], F32, C)
        self.gate_bcast(2, gt, C + 8192)
        P.barrier()
        wres = self.sb("wo_res", [128, 4, KC, 512], BF16, self.B)
        for ct in range(4):
            P.dma(wres[:, ct], self.wo_t[ct])
        o = C + 8192
        yblk = [self.sb(f"yblk{i}", [128, KC, 128], BF16, o + i * 4096) for i in range(2)]; o += 8192
        xq = [self.sb(f"xq{i}", [128, D], F32, o + i * 8192) for i in range(2)]; o += 16384
        x1 = [self.sb(f"x1{i}", [128, D], F32, o + i * 8192) for i in range(2)]; o += 16384
        assert o <= SB_END
        for blk in range(NB):
            ts_ = slice(blk * 128, (blk + 1) * 128)
            yb = yblk[blk % 2]
            P.dma(yb[:], self.ysc[blk])
            P.dma(xq[blk % 2][:], self.x_own[ts_, :])
            for ct in range(4):
                cs = slice(ct * 512, (ct + 1) * 512)
                ps = self.ps[ct]
                for kc in range(KC):
                    P.mm(ps[:], yb[:, kc, :], wres[:, ct, kc, :], start=(kc == 0), stop=(kc == KC - 1))
                P.tt(x1[blk % 2][:, cs], ps[:], gt[:, cs], ALU.mult)
            P.tt(x1[blk % 2][:], x1[blk % 2][:], xq[blk % 2][:], ALU.add, eng="pool")
            P.dma(self.out[ts_, :], x1[blk % 2][:], eng="act")
        P.barrier()
        self.norm_phase(self.out, self.s2, self.sh2, first=True, cbase=C + 8192)
        P.barrier()
        self.gate_bcast(5, gt, C + 8192)
        P.barrier()
        TT = 512
        actT = self.sb("actT", [128, JC, TT], BF16, self.B)
        ob = self.B + JC * TT * 2
        xr = [self.sb(f"xr{i}", [128, 512], F32, ob + i * 2048) for i in range(2)]; ob += 4096
        xo = [self.sb(f"xo{i}", [128, 512], F32, ob + i * 2048) for i in range(2)]; ob += 4096
        sil = [self.sb(f"sil{i}", [128, 512], F32, ob + i * 2048) for i in range(2)]; ob += 4096
        assert ob <= self.C
        o = C + 8192
        wgu_sb = [self.sb(f"wgusb{i}", [128, 2, KC, 256], BF16, o + i * 16384) for i in range(2)]; o += 32768
        wd_bf = [self.sb(f"wdbf{i}", [128, 4, 512], BF16, o + i * 4096) for i in range(3)]; o += 12288
        assert o <= SB_END, o
        for tt in range(NT // TT):
            t0 = tt * TT
            tsl = slice(1 + t0, 1 + t0 + TT)
            for jg in range(JC // 2):
                b = jg % 2
                P.dma(wgu_sb[b][:], self.wgu_t[jg])
                for jj in range(2):
                    j = jg * 2 + jj
                    pg = self.ps[4]
                    pu = self.ps[5]
                    for kc in range(KC):
                        P.mm(pg[:], wgu_sb[b][:, 0, kc, jj * 128:(jj + 1) * 128], self.hT[:, kc, tsl], start=(kc == 0), stop=(kc == KC - 1))
                    for kc in range(KC):
                        P.mm(pu[:], wgu_sb[b][:, 1, kc, jj * 128:(jj + 1) * 128], self.hT[:, kc, tsl], start=(kc == 0), stop=(kc == KC - 1))
                    s_ = sil[j % 2]
                    P.actv(s_[:], pg[:], AF.Silu)
                    P.tt(actT[:, j, :], s_[:], pu[:], ALU.mult)
            nd = 0
            for ct in range(4):
                cs = slice(ct * 512, (ct + 1) * 512)
                for j4 in range(JC // 4):
                    wb = wd_bf[nd % 3]
                    nd += 1
                    P.dma(wb[:], self.wd_t[ct, j4])
                    for jj in range(4):
                        j = j4 * 4 + jj
                        for tb in range(4):
                            tl = slice(tb * 128, (tb + 1) * 128)
                            P.mm(self.ps[tb][:], actT[:, j, tl], wb[:, jj, :], start=(j == 0), stop=(j == JC - 1))
                for tb in range(4):
                    r0 = t0 + tb * 128
                    xr_ = xr[tb % 2]
                    xo_ = xo[tb % 2]
                    P.dma(xr_[:], self.out[r0:r0 + 128, cs])
                    P.tt(xo_[:], self.ps[tb][:], gt[:, cs], ALU.mult)
                    P.tt(xo_[:], xo_[:], xr_[:], ALU.add, eng="pool")
                    P.dma(self.out[r0:r0 + 128, cs], xo_[:], eng="act")

    def sb_consts(self):
        P = self.P
        o = (self.const_used + 31) // 32 * 32
        self.mdiag = self.sb("mdiag", [128, 128], F32, o); o += 512
        self.zeros = self.sb("zeros", [128, 512], F32, o); o += 2048
        assert o <= self.CONST_END, o
        self.rw_const_off = o
        P.dma(self.mdiag[:], self.din("mdiag", [128, 128]))
        P.memset(self.zeros[:], 0.0)

    def sb_phase(self, seg, extra=None):
        P = self.P
        C = self.C
        own = seg == "own"
        w_in = self.din("w_in", [D, 6144]).rearrange("(kc p) n -> p kc n", p=128)
        if not hasattr(self, "kpre"):
            self.kpre = self.dscratch("kpre", [8, 64, 2, NT], BF16)
            self.vpre = self.dscratch("vpre", [8, 128, NB, 2, 64], BF16)
        Wsb = self.sb("Wsb", [128, KC, 384], BF16, C)
        stg = self.sb("sbstg", [128, 8, 384], F32, C + 12288)
        kT = self.sb("kT", [64, 2, 2 * NT], BF16, C + 24576)
        qT = self.sb("qT", [64, 2, NT], BF16, C + 40960)
        Vt = self.sb("Vt", [128, 2 * NB, 2, 64], BF16, C + 49152)
        o = C + 57344
        tmps = []
        for ci in range(2):
            sq = self.sb(f"sbsq{ci}", [128, 256], F32, o); o += 1024
            qkn = self.sb(f"qkn{ci}", [128, 4, 64], BF16, o); o += 512
            qkf = self.sb(f"qkf{ci}", [128, 4, 64], F32, o); o += 1024
            ssr = self.sb(f"ssr{ci}", [128, 8], F32, o); o += 32
            tmps.append((sq, qkn, qkf, ssr))
        gains = self.sb("gains", [128, 4, 64], F32, o); o += 1024
        assert o <= C + 64000, o
        extra_done = [extra is None]
        o = C
        o = C + 256
        G = [self.sb(f"G{i}", [128, 512], F32, o + i * 2048) for i in range(3)]; o += 6144
        attn = [self.sb(f"attn{i}", [128, 512], BF16, o + i * 1024) for i in range(3)]; o += 3072
        attnT = [self.sb(f"attnT{i}", [128, 4, 512], BF16, o + i * 4096) for i in range(2)]; o += 8192
        assert o <= C + 24576, o
        ysb_buf = [self.sb(f"ysbb{i}", [128, 512], BF16, C + 64000 + i * 1024) for i in range(2)]
        CPf = self.sb("CPf", [128, 4, 2 * NT + 8], F32, self.B)
        qg = self.din("q_gain", [1, 64]); kg = self.din("k_gain", [1, 64])
        for j in range(4):
            P.dma(gains[:, j, :], dram_bcast(qg if j < 2 else kg, 128))
        P.ts(gains[:, 0:2, :], gains[:, 0:2, :], 0.125, None, op0=ALU.mult)
        koff = NT if own else 0
        for sp in range(8):
            for hf in range(2):
                for j in range(3):
                    c0 = 3072 + j * 1024 + sp * 128
                    P.dma(stg[:, :, j * 128:(j + 1) * 128], w_in[:, hf * 8:(hf + 1) * 8, c0:c0 + 128])
                P.copy(Wsb[:, hf * 8:(hf + 1) * 8, :], stg[:], eng=("pool" if hf else "dve"))
            if own:
                P.dma(kT[:, :, 0:NT], self.kpre[sp])
                P.dma(Vt[:, 0:NB], self.vpre[sp])
            def ip_gen(ci, blk):
                sq_, qkn_, qkf_, ssr_ = tmps[ci]
                ps = self.ps[2 * ci + (blk // 2) % 2]
                for kc in range(KC):
                    P.mm(ps[:, 0:384], self.hT[:, kc, 1 + blk * 128: 1 + (blk + 1) * 128], Wsb[:, kc, :],
                         start=(kc == 0), stop=(kc == KC - 1))
                    if kc == 7:
                        yield
                yield
                P.actv(sq_[:], ps[:, 0:256], AF.Square)
                s4 = ssr_[:, 0:4]
                P.reduce(s4, sq_[:].rearrange("p (g d) -> p g d", d=64))
                yield
                P.ts(s4, s4, 1.0 / 64, 1e-6, op0=ALU.mult, op1=ALU.add)
                P.actv(s4, s4, AF.Sqrt)
                P.recip(s4, s4)
                yield
                P.tt(qkf_[:], ps[:, 0:256].rearrange("p (g d) -> p g d", d=64), self._b3(s4, 64), ALU.mult)
                P.tt(qkn_[:], qkf_[:], gains[:], ALU.mult, eng="pool")
                vdst = Vt[:, koff // 128 + blk].rearrange("p h d -> p (h d)")
                if own:
                    P.actv(vdst, ps[:, 256:384], AF.Copy)
                else:
                    P.actv(vdst, ps[:, 256:384], AF.Identity, scale=self.flag[:])
                yield
                pb = self.pb[ci]
                for j in range(4):
                    P.tr(pb[0:64, j * 128:(j + 1) * 128], qkn_[:, j, :], self.ident[:])
                if own:
                    P.copy(qT[:, :, blk * 128:(blk + 1) * 128], pb[0:64, 0:256].rearrange("p (h t) -> p h t", h=2), eng="act")
                P.copy(kT[:, :, koff + blk * 128: koff + (blk + 1) * 128], pb[0:64, 256:512].rearrange("p (h t) -> p h t", h=2))
                yield

            def ip_chain(ci):
                for blk in range(ci, NB, 2):
                    for _ in ip_gen(ci, blk):
                        yield

            gens = [ip_chain(0), ip_chain(1)]
            if extra is not None:
                gens.append(extra)
            alive = [True] * len(gens)
            for _ in range(3):
                next(gens[0])
            while any(alive[:2]):
                for gi in range(len(gens)):
                    if alive[gi]:
                        try:
                            next(gens[gi])
                        except StopIteration:
                            alive[gi] = False
                            if gi == 2:
                                extra_done[0] = True
            if not own:
                P.dma(self.kpre[sp], kT[:, :, 0:NT])
                P.dma(self.vpre[sp], Vt[:, 0:NB])
                if extra is not None and extra_done[0]:
                    extra = None
                P.barrier()
                continue
            P.barrier()
            tiles = []
            for R in range(NB // 4):
                tq0 = NT + R * 512
                td = tq0 // 512
                for hh in range(2):
                    for ti in range(td, -1, -1):
                        k0 = ti * 512
                        for j in range(4):
                            w = (j + 1) * 128 if ti == td else 512
                            tiles.append(dict(R=R, hh=hh, k0=k0, w=w, j=j, first=(ti == td), diag=(ti == td),
                                              gfirst=(ti == td), glast=(ti == 0), it=len(tiles), grp=(len(tiles) // 4)))

            def S1(t):
                it = t["it"]; w = t["w"]; g = G[it % 3]
                pz = self.ps[it % 4]
                qb = t["R"] * 4 + t["j"]
                P.mm(pz[:, 0:w], qT[:, t["hh"], qb * 128:(qb + 1) * 128], kT[:, t["hh"], t["k0"]:t["k0"] + w])
                P.actv(g[:, 0:w], pz[:, 0:w], AF.Sigmoid, scale=-1.0)
                if t["first"]:
                    P.tt(g[:, w - 128:w], g[:, w - 128:w], self.mdiag[:], ALU.max)

            def S2(t):
                it = t["it"]; w = t["w"]; g = G[it % 3]; k0 = t["k0"]; j = t["j"]
                if t["first"]:
                    P.memset(CPf[:, j, k0 + w:k0 + w + 1], 1.0)
                    init = 1.0
                else:
                    init = CPf[:, j, k0 + w:k0 + w + 1]
                a = CPf[:, j, k0:k0 + w]
                rev_cp = bass.AP(a.tensor, int(a.offset) + w - 1, [list(a.ap[0]), [-1, w]])
                P.scan(rev_cp, self._rev(g, w), self.zeros[:, 0:w], init, ALU.mult, ALU.add)

            def S2b(t):
                it = t["it"]; w = t["w"]; at = attn[it % 3]; k0 = t["k0"]; j = t["j"]
                P.tt(at[:, 0:w], CPf[:, j, k0 + 1:k0 + w + 1], CPf[:, j, k0:k0 + w], ALU.subtract, eng="pool")

            def S3(t):
                it = t["it"]; w = t["w"]; at = attn[it % 3]; j = t["j"]
                aTs = attnT[t["grp"] % 2]
                pt = self.pb[it % 2]
                nb_ = w // 128
                for c in range(nb_):
                    P.tr(pt[:, c * 128:(c + 1) * 128], at[:, c * 128:(c + 1) * 128], self.ident[:])
                P.copy(aTs[:, 0:nb_, j * 128:(j + 1) * 128], pt[:, 0:w].rearrange("p (c t) -> p c t", t=128),
                       eng=("act" if it % 3 else "dve"))

            def S4(t):
                if t["j"] != 3:
                    return
                aTs = attnT[t["grp"] % 2]; hh = t["hh"]; R = t["R"]
                po = self.ps[4 + (R % 2)]
                for c in range(4):
                    j0 = c if t["diag"] else 0
                    kb = (t["k0"] + c * 128) // 128
                    P.mm(po[64 * hh:64 * hh + 64, j0 * 128:512], Vt[:, kb, hh, :], aTs[:, c, j0 * 128:512],
                         start=(t["gfirst"] and c == 0), stop=(t["glast"] and c == 3), skip_group_check=True)
                if t["glast"] and hh == 1:
                    yb = ysb_buf[R % 2]
                    P.copy(yb[:], po[:, :])
                    P.dma(self.ysc[R * 4:(R + 1) * 4, :, 8 + sp, :].rearrange("b p t -> p b t"),
                          yb[:].rearrange("p (b t) -> p b t", t=128))

            nt = len(tiles)
            stages = (S1, S2, S2b, S3, S4)
            for step in range(nt + len(stages) - 1):
                for k in range(len(stages) - 1, -1, -1):
                    if 0 <= step - k < nt:
                        stages[k](tiles[step - k])
            P.barrier()

    def _b3(self, ap2, n):
        pat = [list(p) for p in ap2.ap]
        return bass.AP(ap2.tensor, int(ap2.offset), pat + [[0, n]])

    def _rev(self, t, w):
        a = t[:, 0:w]
        return bass.AP(a.tensor, int(a.offset) + w - 1, [list(a.ap[0]), [-1, w]])

    def rwkv_consts(self):
        P = self.P
        o = self.rw_const_off
        self.tri = self.sb("tri", [128, 5, 128], F32, o); o += 2560
        self.mask2 = self.sb("mask2", [128, 256], F32, o); o += 1024
        self.maskT = self.sb("maskT", [128, 128], F32, o); o += 512
        self.bones = self.sb("bones", [128, 128], F32, o); o += 512
        self.sel = self.sb("sel", [128, 2], F32, o); o += 32
        self.lnx = self.sb("lnx", [128, 16], F32, o); o += 64
        self.mucol = self.sb("mucol", [128, 6, KC], F32, o); o += 6 * KC * 4
        self.Hst = self.sb("Hst", [64, HEADS, 64], F32, o); o += 4096
        assert o <= self.CONST_END, o
        P.dma(self.tri[:], self.din("tri5", [128, 5, 128]))
        P.dma(self.mask2[:], self.din("mask2", [128, 256]))
        P.dma(self.maskT[:], self.din("maskT", [128, 128]))
        P.dma(self.bones[:], self.din("bones", [128, 128]))
        P.dma(self.sel[:], self.din("sel", [128, 2]))
        P.dma(self.lnx[:, 0:8], self.din("lnxg_col", [128, 8]))
        P.dma(self.lnx[:, 8:16], self.din("lnxb_col", [128, 8]))
        for i, nm in enumerate(("mu_w_col", "mu_a_col", "mu_g_col")):
            P.dma(self.mucol[:, i, :], self.din(nm, [128, KC]))
            P.ts(self.mucol[:, 3 + i, :], self.mucol[:, i, :], -1.0, 1.0, op0=ALU.mult, op1=ALU.add)
        P.memset(self.Hst[:], 0.0)
        self.p0b_off = (o + 31) // 32 * 32

    def rwkv_phase(self, seg):
        P = self.P
        C = self.C
        B = self.B
        own = seg == "own"
        rot = {"i": 0}

        def evac_copy(dst, src):
            rot["i"] += 1
            if rot["i"] % 2:
                P.copy(dst, src)
            else:
                P.copy(dst, src, eng="act")

        w_in = self.din("w_in", [D, 6144]).rearrange("(kc p) n -> p kc n", p=128)
        zwa = self.sb("zwa", [128, NT], BF16, C)
        zg1 = self.sb("zg1", [128, NT], BF16, C + 4096)
        zg2 = self.sb("zg2", [32, NT], BF16, C + 8192)
        o = B
        raw = self.sb("l1raw", [128, KC, 288], F32, o); o += 18432
        Wl = self.sb("Wl", [128, KC, 2, 288], BF16, o); o += 18432
        w1v = self.din("w1", [D, 64]).rearrange("(kc p) n -> p kc n", p=128)
        a1v = self.din("a1", [D, 64]).rearrange("(kc p) n -> p kc n", p=128)
        g1v = self.din("g1", [D, 160]).rearrange("(kc p) n -> p kc n", p=128)
        P.dma(raw[:, :, 0:64], w1v)
        P.dma(raw[:, :, 64:128], a1v)
        P.dma(raw[:, :, 128:288], g1v)
        for (c0, c1, mi) in ((0, 64, 0), (64, 128, 1), (128, 288, 2)):
            P.tt(Wl[:, :, 0, c0:c1], raw[:, :, c0:c1], self._b3(self.mucol[:, 3 + mi, :], c1 - c0), ALU.mult)
            P.tt(Wl[:, :, 1, c0:c1], raw[:, :, c0:c1], self._b3(self.mucol[:, mi, :], c1 - c0), ALU.mult, eng="pool")
        for tt in range(NT // 512):
            t0 = tt * 512
            pz = self.ps[(3 * tt) % 6]; pg1 = self.ps[(3 * tt + 1) % 6]; pg2 = self.ps[(3 * tt + 2) % 6]
            for (pp, c0, c1) in ((pz, 0, 128), (pg1, 128, 256), (pg2, 256, 288)):
                n = 0
                for var in (0, 1):
                    for kc in range(KC):
                        P.mm(pp[0:c1 - c0, :], Wl[:, kc, var, c0:c1], self.hT[:, kc, 1 + t0 - var: 1 + t0 - var + 512],
                             start=(n == 0), stop=(n == 2 * KC - 1))
                        n += 1
            P.actv(zwa[0:64, t0:t0 + 512], pz[0:64, :], AF.Tanh)
            P.copy(zwa[64:128, t0:t0 + 512], pz[64:128, :])
            P.actv(zg1[:, t0:t0 + 512], pg1[:, :], AF.Sigmoid)
            P.actv(zg2[:, t0:t0 + 512], pg2[0:32, :], AF.Sigmoid)
        P.barrier()
        rows = {"mu": self.din("mu_rkv", [1, 3072]), "w0": self.din("w0", [1, 1024]), "a0": self.din("a0", [1, 1024]),
                "kk": self.din("k_k", [1, 1024]), "ka": self.din("k_a", [1, 1024]), "rk": self.din("r_k", [1, 1024])}
        w2d = self.din("w2", [64, 1024]); a2d = self.din("a2", [64, 1024]); g2d = self.din("g2", [160, 1024])
        ident2 = mid_bcast(self.ident[:], 2)

        class Ctx:
            pass

        def make_ctx(ci, base):
            c = Ctx()
            c.ci = ci
            c.banks = self.ps[3 * ci:3 * ci + 3]
            c.pbank = self.pb[ci]
            c.rr = 0
            o = base
            c.Wr = self.sb(f"Wr{ci}", [128, KC, 2, 384], BF16, o); o += 24576
            c.XN = [self.sb(f"XN{ci}_{i}", [128, 2, 2, 128], BF16, o + i * 1024) for i in range(2)]; o += 2048
            c.l2w = self.sb(f"l2w{ci}", [128, 4, 128], BF16, o); o += 1024
            c.bvec = self.sb(f"bvec{ci}", [128, 5, 128], F32, o); o += 2560
            T0 = o
            c.stg = self.sb(f"rstg{ci}", [128, 8, 384], F32, T0)
            c.mu_b = self.sb(f"mu_b{ci}", [128, 384], F32, T0 + 12288)
            c.omm_b = self.sb(f"omm_b{ci}", [128, 384], F32, T0 + 12288 + 1536)
            c.stg2 = self.sb(f"stg2{ci}", [128, 4, 128], F32, T0 + 12288 + 3072)
            o = T0

            def f32t(nm, n=128):
                nonlocal o
                t = self.sb(f"{nm}{ci}", [128, n], F32, o); o += n * 4
                return t
            for nm in ("t_u", "sigw", "t_a", "asig"):
                setattr(c, nm, f32t(nm))
            c.E = self.sb(f"E{ci}", [128, 4, 128], F32, o); o += 2048
            for nm in ("PCt", "kk", "kk2", "kkn", "bb", "t1", "kmod", "tt1", "Ysb", "sqy", "mean", "m2", "var_", "dd", "yn"):
                setattr(c, nm, f32t(nm))
            c.ss2 = self.sb(f"ss2{ci}", [128, 8], F32, o); o += 32
            c.TM = self.sb(f"TMops{ci}", [128, 8, 128], BF16, o); o += 2048
            c.AW = self.sb(f"AW{ci}", [128, 2, 128], BF16, o); o += 512
            c.BG = self.sb(f"BG{ci}", [128, 2, 128], BF16, o); o += 512
            c.FM = self.sb(f"FMops{ci}", [64, 8, 128], BF16, o); o += 2048
            c.BGT = self.sb(f"BGT{ci}", [128, 2, 128], F32, o); o += 1024
            c.GTbm = self.sb(f"GTbm{ci}", [128, 2, 256], BF16, o); o += 1024
            c.GTkm = self.sb(f"GTkm{ci}", [128, 2, 256], BF16, o); o += 1024
            c.M0m = self.sb(f"M0m{ci}", [128, 2, 128], BF16, o); o += 512
            c.Mb = [self.sb(f"M{ci}_{i}", [128, 2, 128], BF16, o + i * 512) for i in range(2)]; o += 1024
            c.AU = self.sb(f"AU{ci}", [128, 2, 128], BF16, o); o += 512
            c.RhT = self.sb(f"RhT{ci}", [64, 2, 128], BF16, o); o += 512
            c.TcT = self.sb(f"TcT{ci}", [64, 4, 64], BF16, o); o += 512
            c.Dc = self.sb(f"Dc{ci}", [64, 4, 64], F32, o); o += 1024
            c.PCfm = self.sb(f"PCfm{ci}", [64, 8], F32, o); o += 32
            c.Htmp = self.sb(f"Htmp{ci}", [64, 2, 64], F32, o); o += 512
            c.Hbf = self.sb(f"Hbf{ci}", [64, 2, 64], BF16, o); o += 256
            c.ybuf = [self.sb(f"ybuf{ci}_{i}", [128, 128], BF16, o + i * 256) for i in range(2)]; o += 512
            c.end = o
            return c

        ctxs = [make_ctx(0, B), make_ctx(1, C + 12288)]
        assert ctxs[0].end <= C, ctxs[0].end - C
        assert ctxs[1].end <= SB_END, ctxs[1].end - SB_END

        def prep(c, rp):
            f0 = rp * 128
            for j in range(3):
                P.dma(c.mu_b[:, j * 128:(j + 1) * 128], dram_bcast(rows["mu"][:, j * 1024 + f0: j * 1024 + f0 + 128], 128))
            P.ts(c.omm_b[:], c.mu_b[:], -1.0, 1.0, op0=ALU.mult, op1=ALU.add)
            for hf in range(2):
                for j in range(3):
                    c0 = j * 1024 + f0
                    P.dma(c.stg[:, :, j * 128:(j + 1) * 128], w_in[:, hf * 8:(hf + 1) * 8, c0:c0 + 128])
                P.tt(c.Wr[:, hf * 8:(hf + 1) * 8, 0, :], c.stg[:], mid_bcast(c.omm_b[:], 8), ALU.mult)
                P.tt(c.Wr[:, hf * 8:(hf + 1) * 8, 1, :], c.stg[:], mid_bcast(c.mu_b[:], 8), ALU.mult, eng="pool")
            P.dma(c.stg2[0:64, 0, :], w2d[:, f0:f0 + 128])
            P.dma(c.stg2[64:128, 1, :], a2d[:, f0:f0 + 128])
            P.dma(c.stg2[:, 2, :], g2d[0:128, f0:f0 + 128])
            P.dma(c.stg2[0:32, 3, :], g2d[128:160, f0:f0 + 128])
            P.copy(c.l2w[0:64, 0, :], c.stg2[0:64, 0, :])
            P.copy(c.l2w[64:128, 1, :], c.stg2[64:128, 1, :])
            P.copy(c.l2w[:, 2, :], c.stg2[:, 2, :])
            P.copy(c.l2w[0:32, 3, :], c.stg2[0:32, 3, :])
            for i, nm in enumerate(("w0", "a0", "kk", "ka", "rk")):
                P.dma(c.bvec[:, i, :], dram_bcast(rows[nm][:, f0:f0 + 128], 128))
            P.copy(c.Hbf[:], self.Hst[:, 2 * rp:2 * rp + 2, :])

        def blk_gen(c, rp, blk):
            def nextps():
                c.rr += 1
                return c.banks[1 + c.rr % 2]

            def inproj(b):
                n = 0
                for var in (0, 1):
                    for kc in range(KC):
                        P.mm(c.banks[0][:, 0:384], self.hT[:, kc, 1 + b * 128 - var: 1 + (b + 1) * 128 - var], Wr[:, kc, var, :],
                             start=(n == 0), stop=(n == 2 * KC - 1))
                        n += 1
            TM = c.TM; AW = c.AW; BG = c.BG; FM = c.FM; BGT = c.BGT; GTbm = c.GTbm; GTkm = c.GTkm; M0m = c.M0m
            AU = c.AU; RhT = c.RhT; TcT = c.TcT; Dc = c.Dc; PCfm = c.PCfm; Htmp = c.Htmp; Hbf = c.Hbf
            E = c.E; bvec = c.bvec; l2w = c.l2w; Wr = c.Wr
            tk = slice(blk * 128, (blk + 1) * 128)
            p_rkv = c.banks[0]
            if blk == 0:
                inproj(0)
                yield
            r_ps = p_rkv[:, 0:128]; k_ps = p_rkv[:, 128:256]; v_ps = p_rkv[:, 256:384]
            p_l = c.banks[1]
            P.mm(p_l[:, 0:128], zwa[0:64, tk], l2w[0:64, 0, :])
            P.mm(p_l[:, 256:384], zg1[:, tk], l2w[:, 2, :], start=True, stop=False)
            P.mm(p_l[:, 256:384], zg2[0:32, tk], l2w[0:32, 3, :], start=False, stop=True)
            P.tt(c.t_u[:], p_l[:, 0:128], bvec[:, 0, :], ALU.add)
            P.copy(BG[:, 1, :], p_l[:, 256:384], eng="act")
            yield
            p_l2 = c.banks[2]
            P.mm(p_l2[:, 0:128], zwa[64:128, tk], l2w[64:128, 1, :])
            P.actv(c.sigw[:], c.t_u[:], AF.Sigmoid)
            P.tt(c.t_a[:], p_l2[:, 0:128], bvec[:, 1, :], ALU.add)
            P.actv(c.asig[:], c.t_a[:], AF.Sigmoid)
            yield
            P.tt(c.kk[:], k_ps, bvec[:, 2, :], ALU.mult)
            P.tt(c.kk2[:], c.kk[:], c.kk[:], ALU.mult, eng="pool")
            P.reduce(c.ss2[:, 0:2], c.kk2[:].rearrange("p (h d) -> p h d", d=64))
            P.ts(c.ss2[:, 0:2], c.ss2[:, 0:2], 1e-12, None, op0=ALU.add)
            P.actv(c.ss2[:, 0:2], c.ss2[:, 0:2], AF.Sqrt)
            P.recip(c.ss2[:, 0:2], c.ss2[:, 0:2])
            yield
            p_c = c.banks[1]
            for i in range(4):
                P.mm(p_c[:, i * 128:(i + 1) * 128], self.tri[:, i, :], c.sigw[:])
            P.actv(E[:].rearrange("p a f -> p (a f)"), p_c[:, :], AF.Exp)
            yield
            p_e = c.banks[2]
            P.mm(p_e[:, 0:128], self.tri[:, 4, :], c.sigw[:])
            P.actv(c.PCt[:], p_e[:, 0:128], AF.Exp)
            P.tt(c.kkn[:].rearrange("p (h d) -> p h d", d=64), c.kk[:].rearrange("p (h d) -> p h d", d=64), self._b3(c.ss2[:, 0:2], 64), ALU.mult, eng="pool")
            P.stt(c.t1[:], c.asig[:], -1.0, bvec[:, 3, :], ALU.add, ALU.mult)
            yield
            P.tt(c.bb[:], c.kkn[:], c.asig[:], ALU.mult, eng="pool")
            P.stt(c.kmod[:], c.t1[:], 1.0, k_ps, ALU.add, ALU.mult)
            yield
            P.stt(TM[:, 0, :], c.kkn[:], -1.0, E[:, 1, :], ALU.mult, ALU.mult)
            P.tt(TM[:, 1, :], r_ps, E[:, 0, :], ALU.mult)
            P.tt(TM[:, 2, :], c.bb[:], E[:, 3, :], ALU.mult, eng="pool")
            P.tt(TM[:, 3, :], c.kmod[:], E[:, 3, :], ALU.mult)
            yield
            P.tt(TM[:, 4, :], c.bb[:], E[:, 2, :], ALU.mult, eng="pool")
            P.tt(TM[:, 5, :], c.kmod[:], E[:, 2, :], ALU.mult, eng="pool")
            P.copy(TM[:, 6, :], v_ps, eng="act")
            P.copy(AW[:, :, 0:64], TM[:, 0, :].rearrange("p (h d) -> p h d", d=64), eng="pool")
            if own:
                P.tt(c.tt1[:], r_ps, c.kmod[:], ALU.mult)
                P.tt(c.tt1[:], c.tt1[:], bvec[:, 4, :], ALU.mult, eng="pool")
                P.reduce(c.ss2[:, 2:4], c.tt1[:].rearrange("p (h d) -> p h d", d=64))
                P.tt(BG[:, 0, :].rearrange("p (h d) -> p h d", d=64), v_ps.rearrange("p (h d) -> p h d", d=64), self._b3(c.ss2[:, 2:4], 64), ALU.mult)
            yield
            if blk + 1 < NB:
                inproj(blk + 1)
            pbt = c.pbank
            for hh in range(2):
                for a in range(4):
                    P.tr(pbt[0:64, (hh * 4 + a) * 128:(hh * 4 + a + 1) * 128], TM[:, a, hh * 64:(hh + 1) * 64], self.ident[:])
            evac_copy(FM[:].rearrange("p a t -> p (a t)"), pbt[0:64, :])
            yield
            if own:
                pbg = c.pbank
                P.tr(pbg[:, 0:128], BG[:, 0, :], self.ident[:])
                P.tr(pbg[:, 128:256], BG[:, 1, :], self.ident[:])
                P.copy(BGT[:].rearrange("p a t -> p (a t)"), pbg[:, 0:256], eng="act")
                yield
            pGb = c.banks[1]; pM0 = c.banks[2]; pGk = c.banks[1]
            c.rr = 1
            for hh in range(2):
                aT_rT = FM[:, hh * 4:hh * 4 + 2, :].rearrange("p a t -> p (a t)")
                P.mm(pGb[:, hh * 256:(hh + 1) * 256], FM[:, hh * 4 + 2, :], aT_rT)
            P.tt(GTbm[:], pGb[:, :].rearrange("p (h t) -> p h t", h=2), mid_bcast(self.mask2[:], 2), ALU.mult)
            for hh in range(2):
                P.mm(pM0[:, hh * 128:(hh + 1) * 128], FM[:, hh * 4 + 0, :], FM[:, hh * 4 + 2, :])
            P.tt(M0m[:], pM0[:, 0:256].rearrange("p (h t) -> p h t", h=2), mid_bcast(self.maskT[:], 2), ALU.mult)
            yield
            for hh in range(2):
                aT_rT = FM[:, hh * 4:hh * 4 + 2, :].rearrange("p a t -> p (a t)")
                P.mm(pGk[:, hh * 256:(hh + 1) * 256], FM[:, hh * 4 + 3, :], aT_rT)
            P.tt(GTkm[:], pGk[:, :].rearrange("p (h t) -> p h t", h=2), mid_bcast(self.mask2[:], 2), ALU.mult)
            XNa = c.XN[0]
            Nk = GTbm[:, :, 0:128]
            Mk = M0m[:]
            P.tt(XNa[:, :, 0, :], Nk, ident2, ALU.add, eng="pool")
            yield
            pMn = nextps()
            for hh in range(2):
                P.mm(pMn[:, hh * 128:(hh + 1) * 128], Nk[:, hh, :], Mk[:, hh, :])
            Mn = c.Mb[0]
            evac_copy(Mn[:].rearrange("p h t -> p (h t)"), pMn[:, 0:256])
            pNn = nextps()
            for hh in range(2):
                P.mm(pNn[:, hh * 128:(hh + 1) * 128], Mk[:, hh, :], Nk[:, hh, :])
            evac_copy(XNa[:, :, 1, :], pNn[:, 0:256].rearrange("p (h t) -> p h t", h=2))
            yield
            cur = XNa
            for it in range(5):
                nxt = c.XN[(it + 1) % 2]
                last = it == 4
                pX = nextps()
                wcols = 128 if last else 256
                for hh in range(2):
                    P.mm(pX[:, hh * 256:hh * 256 + wcols], Mn[:, hh, :],
                         cur[:, hh, 0:(1 if last else 2), :].rearrange("p a t -> p (a t)"))
                pXv = pX[:, :].rearrange("p (h a t) -> p h a t", h=2, a=2)
                P.tt(nxt[:, :, 0, :], pXv[:, :, 0, :], cur[:, :, 0, :], ALU.add)
                if not last:
                    evac_copy(nxt[:, :, 1, :], pXv[:, :, 1, :])
                    pM2 = nextps()
                    for hh in range(2):
                        P.mm(pM2[:, hh * 128:(hh + 1) * 128], cur[:, hh, 1, :], Mn[:, hh, :])
                    Mn2 = c.Mb[(it + 1) % 2]
                    evac_copy(Mn2[:].rearrange("p h t -> p (h t)"), pM2[:, 0:256])
                    Mn = Mn2
                cur = nxt
                yield
            X = [cur[:, 0, 0, :], cur[:, 1, 0, :]]
            pW = nextps()
            for hh in range(2):
                P.mm(pW[:, hh * 64:(hh + 1) * 64], GTkm[:, hh, 0:128], TM[:, 6, hh * 64:(hh + 1) * 64])
            evac_copy(AW[:, :, 64:128], pW[:, 0:128].rearrange("p (h d) -> p h d", h=2))
            yield
            pAU = nextps()
            for hh in range(2):
                P.mm(pAU[:, hh * 128:(hh + 1) * 128], X[hh], AW[:, hh, :])
            evac_copy(AU[:].rearrange("p h t -> p (h t)"), pAU[:, 0:256])
            yield
            if own:
                pR = nextps()
                for hh in range(2):
                    P.mm(pR[0:64, hh * 128:(hh + 1) * 128], AU[:, hh, 0:64], GTbm[:, hh, 128:256])
                for hh in range(2):
                    P.tt(RhT[:, hh, :], pR[0:64, hh * 128:(hh + 1) * 128], FM[:, hh * 4 + 1, :], ALU.add)
                yield
            pTs = [nextps(), nextps()]
            for hh in range(2):
                for cc in range(2):
                    rw = slice(cc * 64, cc * 64 + 64)
                    pT = pTs[cc]
                    col = hh * 128
                    P.mm(pT[0:64, col:col + 64], AU[rw, hh, 0:64], TM[rw, 4, hh * 64:(hh + 1) * 64], skip_group_check=True)
                    P.mm(pT[0:64, col + 64:col + 128], TM[rw, 4, hh * 64:(hh + 1) * 64], AU[rw, hh, 64:128], start=True, stop=False, skip_group_check=True)
                    P.mm(pT[0:64, col + 64:col + 128], TM[rw, 5, hh * 64:(hh + 1) * 64], TM[rw, 6, hh * 64:(hh + 1) * 64], start=False, stop=True, skip_group_check=True)
            for cc in range(2):
                pTv = pTs[cc][0:64, 0:256].rearrange("p (g x) -> p g x", x=128)
                P.copy(TcT[:, cc * 2:cc * 2 + 2, :], pTv[:, :, 0:64], eng="act")
                P.copy(Dc[:, cc * 2:cc * 2 + 2, :], pTv[:, :, 64:128])
            yield
            pP = nextps()
            for hh in range(2):
                P.mm(pP[0:64, hh * 2:hh * 2 + 2], c.PCt[:, hh * 64:(hh + 1) * 64], self.sel[:])
            P.copy(PCfm[:, 0:4], pP[0:64, 0:4])
            yield
            if own:
                pY = nextps()
                for hh in range(2):
                    yo = pY[64 * hh:64 * hh + 64, 0:128]
                    P.mm(yo, AU[:, hh, 64:128], GTbm[:, hh, 128:256], start=True, stop=False, skip_group_check=True)
                    P.mm(yo, TM[:, 6, hh * 64:(hh + 1) * 64], GTkm[:, hh, 128:256], start=False, stop=False, skip_group_check=True)
            for cc in range(2):
                if own:
                    pH = c.banks[2] if pY is c.banks[1] else c.banks[1]
                else:
                    pH = nextps()
                for hh in range(2):
                    if own:
                        P.mm(pY[64 * hh:64 * hh + 64, cc * 64:(cc + 1) * 64], Hbf[:, hh, :], RhT[:, hh, cc * 64:(cc + 1) * 64],
                             start=False, stop=(cc == 1), skip_group_check=True)
                    P.mm(pH[0:64, hh * 64:(hh + 1) * 64], TcT[:, cc * 2 + hh, :], Hbf[:, hh, :])
                for hh in range(2):
                    Hh = self.Hst[:, 2 * rp + hh, :]
                    P.stt(Htmp[:, hh, :], Hh, PCfm[:, hh * 2 + cc:hh * 2 + cc + 1], pH[0:64, hh * 64:(hh + 1) * 64], ALU.mult, ALU.add)
                    P.tt(Hh, Htmp[:, hh, :], Dc[:, cc * 2 + hh, :], ALU.add, eng="pool")
                P.copy(Hbf[:], self.Hst[:, 2 * rp:2 * rp + 2, :], eng="pool")
                yield
            if not own:
                return
            P.copy(c.Ysb[:], pY[:, 0:128], eng="act")
            P.actv(c.sqy[:], c.Ysb[:], AF.Square)
            yield
            pS = nextps()
            P.mm(pS[:, 0:128], self.bones[:], c.Ysb[:])
            P.mm(pS[:, 128:256], self.bones[:], c.sqy[:])
            P.copy(c.mean[:], pS[:, 0:128], eng="act")
            P.tt(c.m2[:], c.mean[:], c.mean[:], ALU.mult, eng="pool")
            P.tt(c.var_[:], pS[:, 128:256], c.m2[:], ALU.subtract)
            yield
            P.ts(c.var_[:], c.var_[:], 64e-5, None, op0=ALU.add)
            P.actv(c.var_[:], c.var_[:], AF.Sqrt)
            P.recip(c.var_[:], c.var_[:])
            P.tt(c.dd[:], c.Ysb[:], c.mean[:], ALU.subtract, eng="pool")
            yield
            P.tt(c.dd[:], c.dd[:], c.var_[:], ALU.mult, eng="pool")
            P.ts(c.yn[:], c.dd[:], self.lnx[:, rp:rp + 1], self.lnx[:, 8 + rp:9 + rp], op0=ALU.mult, op1=ALU.add)
            P.tt(c.yn[:], c.yn[:], BGT[:, 0, :], ALU.add, eng="pool")
            yb = c.ybuf[blk % 2]
            P.tt(yb[:], c.yn[:], BGT[:, 1, :], ALU.mult)
            P.dma(self.ysc[blk, :, rp, :], yb[:])
            yield

        def chain(c, rp):
            for blk in range(NB):
                for _ in blk_gen(c, rp, blk):
                    yield

        for rp2 in range(4):
            rps = (2 * rp2, 2 * rp2 + 1)
            for c, rp in zip(ctxs, rps):
                prep(c, rp)
            self.conv_some(16)
            P.barrier()
            gens = [chain(c, rp) for c, rp in zip(ctxs, rps)]
            alive = [True, True]
            for _ in range(6):
                next(gens[0])
            while any(alive):
                for gi in range(2):
                    if alive[gi]:
                        try:
                            next(gens[gi])
                        except StopIteration:
                            alive[gi] = False
            P.barrier()

_NC_CACHE = {}


def _col(v, n):
    return np.ascontiguousarray(np.asarray(v, dtype=np.float32).reshape(n, 128).T)


def _rwkv_consts():
    tok = np.arange(128)
    same = (tok[:, None] // 64) == (tok[None, :] // 64)
    s = tok[:, None]; t = tok[None, :]
    Cd = np.float32(C_DECAY)
    tri = np.zeros((128, 5, 128), np.float32)
    tri[:, 0] = np.where(same & (s <= t), Cd, 0)
    tri[:, 1] = np.where(same & (s < t), Cd, 0)
    tri[:, 2] = np.where(same & (s > t), Cd, 0)
    tri[:, 3] = np.where(same & (s <= t), -Cd, 0)
    tri[:, 4] = np.where(same, Cd, 0)
    m_lt = (same & (s < t)).astype(np.float32)
    m_le = (same & (s <= t)).astype(np.float32)
    sel = np.zeros((128, 2), np.float32); sel[0, 0] = 1; sel[64, 1] = 1
    bones = np.kron(np.eye(2, dtype=np.float32), np.full((64, 64), 1.0 / 64, np.float32))
    return {"tri5": tri, "mask2": np.concatenate([m_lt, m_le], 1), "maskT": np.ascontiguousarray(m_lt.T),
            "sel": sel, "bones": bones}


def make_in_maps(I, cores=range(8), extra=None):
    f = lambda k: np.asarray(I[k])[0]
    shared = {
        "idn": np.eye(128, dtype=np.float32),
        "w_ada": np.ascontiguousarray(f("w_ada")),
        "b_ada_col": _col(f("b_ada"), 96),
        "b_ada_row": np.ascontiguousarray(f("b_ada").reshape(1, -1)),
        "n1g_col": _col(f("norm1_gain"), 16),
        "n2g_col": _col(f("norm2_gain"), 16),
        "w_out": np.ascontiguousarray(f("w_out")),
        "w_in": np.ascontiguousarray(f("w_in")),
        "mu_rkv": f("mu_rkv").reshape(1, -1), "w0": f("w0").reshape(1, -1), "a0": f("a0").reshape(1, -1),
        "k_k": f("k_k").reshape(1, -1), "k_a": f("k_a").reshape(1, -1), "r_k": f("r_k").reshape(1, -1),
        "w1": f("w1"), "a1": f("a1"), "g1": f("g1"), "w2": f("w2"), "a2": f("a2"), "g2": f("g2"),
        "mu_w_col": _col(f("mu_w"), 16), "mu_a_col": _col(f("mu_a"), 16), "mu_g_col": _col(f("mu_g"), 16),
        "lnxg_col": _col(f("ln_x_gain"), 8), "lnxb_col": _col(f("ln_x_bias"), 8),
        **_rwkv_consts(),
        "mdiag": np.triu(np.ones((128, 128), np.float32)),
        "q_gain": np.ascontiguousarray(f("q_norm_gain").reshape(1, 64)),
        "k_gain": np.ascontiguousarray(f("k_norm_gain").reshape(1, 64)),
        "w_gate_up": np.ascontiguousarray(f("w_gate_up")),
        "w_down": np.ascontiguousarray(f("w_down")),
    }
    maps = []
    x = np.asarray(I["x"])
    c = np.asarray(I["c"])
    for core in cores:
        b, half = core // 2, core % 2
        m = dict(shared)
        m["x_own"] = np.ascontiguousarray(x[b, half * NT:(half + 1) * NT])
        m["x_pre"] = np.ascontiguousarray(x[b, 0:NT])
        m["ccol"] = _col(c[b], 16)
        m["flag"] = np.full((128, 1), float(half), np.float32)
        if extra:
            m.update(extra(core))
        maps.append(m)
    return maps


def kernel(**inputs):
    if "k" not in _NC_CACHE:
        k = K()
        _NC_CACHE["k"] = (k, k.build())
    k, nc = _NC_CACHE["k"]
    maps = make_in_maps(inputs)
    needed = set(k.din_cache.keys())
    maps = [{k: v for k, v in m.items() if k in needed} for m in maps]
    res = run_bass_kernel_spmd(nc, maps, core_ids=list(range(8)))
    out = np.empty((4, 4096, D), np.float32)
    for core in range(8):
        b, half = core // 2, core % 2
        out[b, half * NT:(half + 1) * NT] = res.results[core]["out"]
    return out
```

```python
import numpy as np
from concourse.bass_utils import run_bass_kernel_spmd
import concourse.bass as bass
import concourse.mybir as mybir
from contextlib import ExitStack

F32 = mybir.dt.float32
BF16 = mybir.dt.bfloat16
ALU = mybir.AluOpType
AF = mybir.ActivationFunctionType
AX = mybir.AxisListType

EPOCH = 16000
NDSEM = {"sp": 16, "pool": 12, "act": 8, "dve": 2, "pe": 2}
ENGS = ("pe", "dve", "act", "pool", "sp")


def _region(ap):
    t = ap.tensor
    name = t.name
    pat = ap.ap
    off = int(ap.offset)
    space = str(ap.space)
    if "DRAM" in space.upper() or "HBM" in space.upper():
        lo = off
        hi = off
        for st, cnt in pat:
            if st >= 0:
                hi += st * (cnt - 1)
            else:
                lo += st * (cnt - 1)
        return (name, 0, 1, lo, hi + 1)
    shp = t.shape
    row = 1
    for s in shp[1:]:
        row *= s
    p0 = off // row
    f0 = off % row
    np_ = pat[0][1]
    lo = f0
    hi = f0
    for st, cnt in pat[1:]:
        if st >= 0:
            hi += st * (cnt - 1)
        else:
            lo += st * (cnt - 1)
    if "PSUM" in space.upper():
        esz = 4 if ap.dtype == F32 else 2
        b0 = (lo * esz) // 2048
        b1 = (hi * esz) // 2048
        return (name, 0, 128, b0 * 2048 // esz, (b1 + 1) * 2048 // esz)
    return (name, p0, p0 + np_, lo, hi + 1)


def _overlap(a, b):
    return a[1] < b[2] and b[1] < a[2] and a[3] < b[4] and b[3] < a[4]


def _covers(a, b):
    return a[1] <= b[1] and a[2] >= b[2] and a[3] <= b[3] and a[4] >= b[4]


class Instr:
    __slots__ = ("eng", "fn", "deps", "is_dma", "idx", "inc_n", "dma_n", "raw_same")

    def __init__(self, eng, fn, is_dma):
        self.eng = eng
        self.fn = fn
        self.is_dma = is_dma
        self.deps = set()
        self.inc_n = None
        self.dma_n = None


class Prog:
    def __init__(self, nc):
        self.nc = nc
        self.ins = []
        self.track = {}
        self.n_dma = 0

    def op(self, eng, fn, reads=(), writes=(), dma=False):
        I = Instr(eng, fn, dma)
        I.idx = len(self.ins)
        self.ins.append(I)
        if dma:
            if not hasattr(self, "n_dma_e"):
                self.n_dma_e = {e: 0 for e in ENGS}
            I.dma_n = self.n_dma_e[eng]
            self.n_dma_e[eng] += 1
            self.n_dma += 1
        reads = list(reads)
        writes = list(writes)
        for ap in reads:
            if "PSUM" in str(ap.space).upper():
                writes.append(ap)
        for ap in reads:
            r = _region(ap)
            lst = self.track.setdefault(r[0], [])
            merged = False
            for rec in lst:
                q, j, w, en = rec
                if w:
                    if _overlap(r, q):
                        I.deps.add((j, "raw"))
                elif (not dma) and en == eng and q == r:
                    rec[1] = I.idx
                    merged = True
            if not merged:
                lst.append([r, I.idx, False, None if dma else eng])
        for ap in writes:
            r = _region(ap)
            lst = self.track.setdefault(r[0], [])
            keep = []
            for rec in lst:
                q, j, w, en = rec
                if j == I.idx:
                    keep.append(rec)
                    continue
                if _overlap(r, q):
                    I.deps.add((j, "waw" if w else "war"))
                    if _covers(r, q):
                        continue
                keep.append(rec)
            keep.append([r, I.idx, True, None if dma else eng])
            self.track[r[0]] = keep
        return I

    def barrier(self):
        last = {}
        dmas = []
        for I in self.ins[self._bar_start:] if hasattr(self, "_bar_start") else self.ins:
            if I.fn is None:
                continue
            if I.is_dma:
                dmas.append(I.idx)
            else:
                last[I.eng] = I.idx
        for e in ENGS:
            I = Instr(e, None, False)
            I.idx = len(self.ins)
            self.ins.append(I)
            for f, j in last.items():
                if f != e:
                    I.deps.add((j, "raw"))
            for j in dmas:
                I.deps.add((j, "raw"))
        self._bar_start = len(self.ins)
        self.track = {}

    def emit(self):
        nc = self.nc
        ins = self.ins
        need_inc = set()
        real = []
        for I in ins:
            rd = {}
            for (j, kind) in I.deps:
                J = ins[j]
                if J.is_dma:
                    rd[j] = True
                    continue
                if J.eng == I.eng and not I.is_dma:
                    if I.eng == "pe":
                        continue
                    if kind != "raw":
                        continue
                rd[j] = True
            real.append(sorted(rd.keys()))
            for j in rd:
                if not ins[j].is_dma:
                    need_inc.add(j)
        cnt = {e: 0 for e in ENGS}
        for I in ins:
            if (not I.is_dma) and I.idx in need_inc:
                I.inc_n = cnt[I.eng]
                cnt[I.eng] += 1
        with ExitStack() as es:
            sems = {}
            for e in ENGS:
                n_ep = (cnt[e] + EPOCH - 1) // EPOCH
                sems[e] = [es.enter_context(nc.semaphore(f"s_{e}_{k}")) for k in range(n_ep)]
            ndma_e = getattr(self, "n_dma_e", {e: 0 for e in ENGS})
            dsems = {}
            for e in ENGS:
                n = min(NDSEM[e], ndma_e.get(e, 0))
                dsems[e] = [es.enter_context(nc.semaphore(f"s_dma_{e}_{k}")) for k in range(n)]
            per_eng = {e: [] for e in ENGS}
            for I in ins:
                per_eng[I.eng].append(I)
            block = es.enter_context(nc.Block())

            def run_engine(ename, eh):
                seen = {}
                seen_d = {}
                for I in per_eng[ename]:
                    waits = {}
                    dwaits = {}
                    deps = list(real[I.idx])
                    for j in deps:
                        J = ins[j]
                        if J.is_dma:
                            K = len(dsems[J.eng])
                            k = (J.eng, J.dma_n % K)
                            v = 16 * (J.dma_n // K + 1)
                            if seen_d.get(k, 0) < v:
                                dwaits[k] = max(dwaits.get(k, 0), v)
                        else:
                            key = (J.eng, J.inc_n // EPOCH)
                            v = J.inc_n % EPOCH + 1
                            if seen.get(key, 0) < v:
                                waits[key] = max(waits.get(key, 0), v)
                    if I.is_dma:
                        K = len(dsems[I.eng])
                        if I.dma_n >= K:
                            k = (I.eng, I.dma_n % K)
                            v = 16 * (I.dma_n // K)
                            if seen_d.get(k, 0) < v:
                                dwaits[k] = max(dwaits.get(k, 0), v)
                    for key, v in waits.items():
                        eh.wait_ge(sems[key[0]][key[1]], v)
                        seen[key] = v
                    for k, v in dwaits.items():
                        eh.wait_ge(dsems[k[0]][k[1]], v)
                        seen_d[k] = v
                    if I.fn is None:
                        continue
                    bi = I.fn(eh)
                    if I.is_dma:
                        K = len(dsems[I.eng])
                        bi.then_inc(dsems[I.eng][I.dma_n % K], 16)
                    elif I.inc_n is not None:
                        bi.then_inc(sems[I.eng][I.inc_n // EPOCH], 1)
                last = {}
                for I in per_eng[ename]:
                    if I.is_dma and I.fn is not None:
                        K = len(dsems[I.eng])
                        k = (I.eng, I.dma_n % K)
                        last[k] = max(last.get(k, 0), 16 * (I.dma_n // K + 1))
                for k, v in last.items():
                    if seen_d.get(k, 0) < v:
                        eh.wait_ge(dsems[k[0]][k[1]], v)

            @block.sync
            def _(e):
                run_engine("sp", e)

            @block.tensor
            def _(e):
                run_engine("pe", e)

            @block.vector
            def _(e):
                run_engine("dve", e)

            @block.scalar
            def _(e):
                run_engine("act", e)

            @block.gpsimd
            def _(e):
                run_engine("pool", e)

    def dma(self, out, in_, eng="sp"):
        return self.op(eng, lambda e: e.dma_start(out=out, in_=in_), [in_], [out], dma=True)

    def mm(self, out, lhsT, rhs, start=True, stop=True, **kw):
        rd = [lhsT, rhs] + ([] if start else [out])
        return self.op("pe", lambda e: e.matmul(out, lhsT, rhs, start=start, stop=stop, **kw), rd, [out])

    def tr(self, out, in_, ident):
        return self.op("pe", lambda e: e.transpose(out, in_, ident), [in_, ident], [out])

    def actv(self, out, in_, func, bias=None, scale=1.0, accum_out=None, eng="act"):
        rd = [in_]
        kw = {}
        if bias is not None:
            kw["bias"] = bias
            if not isinstance(bias, (int, float)):
                rd.append(bias)
        if not isinstance(scale, (int, float)):
            rd.append(scale)
        wr = [out]
        if accum_out is not None:
            kw["accum_out"] = accum_out
            wr.append(accum_out)
        return self.op(eng, lambda e: e.activation(out=out, in_=in_, func=func, scale=scale, **kw), rd, wr)

    def tt(self, out, in0, in1, op, eng="dve"):
        return self.op(eng, lambda e: e.tensor_tensor(out=out, in0=in0, in1=in1, op=op), [in0, in1], [out])

    def ts(self, out, in0, s1, s2=None, op0=ALU.mult, op1=None, eng="dve", accum_out=None):
        rd = [in0]
        if not isinstance(s1, (int, float)):
            rd.append(s1)
        if s2 is not None and not isinstance(s2, (int, float)):
            rd.append(s2)
        kw = {}
        wr = [out]
        if op1 is not None:
            kw["op1"] = op1
        if accum_out is not None:
            kw["accum_out"] = accum_out
            wr.append(accum_out)
        return self.op(eng, lambda e: e.tensor_scalar(out=out, in0=in0, scalar1=s1, scalar2=s2, op0=op0, **kw), rd, wr)

    def stt(self, out, in0, scalar, in1, op0, op1, eng="dve"):
        rd = [in0, in1]
        if not isinstance(scalar, (int, float)):
            rd.append(scalar)
        return self.op(eng, lambda e: e.scalar_tensor_tensor(out=out, in0=in0, scalar=scalar, in1=in1, op0=op0, op1=op1), rd, [out])

    def copy(self, out, in_, eng="dve"):
        if eng == "act":
            return self.op("act", lambda e: e.copy(out=out, in_=in_), [in_], [out])
        return self.op(eng, lambda e: e.tensor_copy(out=out, in_=in_), [in_], [out])

    def memset(self, ap, val, eng="dve"):
        return self.op(eng, lambda e: e.memset(ap, val), [], [ap])

    def scan(self, out, d0, d1, initial, op0, op1):
        rd = [d0, d1]
        if not isinstance(initial, (int, float)):
            rd.append(initial)
        return self.op("dve", lambda e: e.tensor_tensor_scan(out=out, data0=d0, data1=d1, initial=initial, op0=op0, op1=op1), rd, [out])

    def reduce(self, out, in_, op=ALU.add, axis=AX.X, eng="dve"):
        return self.op(eng, lambda e: e.tensor_reduce(out=out, in_=in_, axis=axis, op=op), [in_], [out])

    def recip(self, out, in_):
        return self.op("dve", lambda e: e.reciprocal(out=out, in_=in_), [in_], [out])

D = 2048
NT = 2048
NB = NT // 128
KC = 16
DFF = 5632
JC = DFF // 128
SB_BASE = 16512
SB_END = 229376
HEADS = 16
C_DECAY = -0.6065306597126334


def dram_bcast(ap, nparts):
    pat = [list(p) for p in ap.ap]
    assert pat[0][1] == 1
    pat[0] = [0, nparts]
    return bass.AP(ap.tensor, int(ap.offset), pat)


def fbcast(ap, n):
    pat = [list(p) for p in ap.ap]
    assert pat[-1][1] == 1
    pat[-1] = [0, n]
    return bass.AP(ap.tensor, int(ap.offset), pat)


def mid_bcast(ap, n):
    pat = [list(p) for p in ap.ap]
    pat = [pat[0], [0, n]] + pat[1:]
    return bass.AP(ap.tensor, int(ap.offset), pat)


class K:
    def __init__(self, dbg=None):
        self.nc = bass.Bass("TRN2", target_bir_lowering=False)
        self.P = Prog(self.nc)
        self.dbg = dbg or {}
        self.nm = 0
        nc = self.nc
        self.ps = [nc.alloc_psum_tensor(f"ps{i}", [128, 512], F32) for i in range(6)]
        self.pb = [nc.alloc_psum_tensor(f"pb{i}", [128, 1024], BF16) for i in range(2)]
        self.din_cache = {}

    def din(self, name, shape, dt=F32):
        if name not in self.din_cache:
            self.din_cache[name] = self.nc.dram_tensor(name, list(shape), dt, kind="ExternalInput").ap()
        return self.din_cache[name]

    def dout(self, name, shape, dt=F32):
        return self.nc.dram_tensor(name, list(shape), dt, kind="ExternalOutput").ap()

    def dscratch(self, name, shape, dt):
        return self.nc.dram_tensor(name, list(shape), dt, kind="Internal").ap()

    def sb(self, name, shape, dt, off):
        esz = 4 if dt == F32 else 2
        n = 1
        for s in shape[1:]:
            n *= s
        assert off % 32 == 0, (name, off)
        assert off >= SB_BASE and off + n * esz <= SB_END, (name, off, n * esz)
        self.nm += 1
        return self.nc.alloc_sbuf_tensor_at(f"{name}_{self.nm}", list(shape), dt, offset=off)

    def build(self):
        P = self.P
        nc = self.nc
        o = SB_BASE
        self.ident = self.sb("ident", [128, 128], BF16, o); o += 256
        self.identf = self.sb("identf", [128, 128], F32, o); o += 512
        self.s1 = self.sb("s1", [128, KC], F32, o); o += 64
        self.sh1 = self.sb("sh1", [128, KC], F32, o); o += 64
        self.s2 = self.sb("s2", [128, KC], F32, o); o += 64
        self.sh2 = self.sb("sh2", [128, KC], F32, o); o += 64
        self.cact = self.sb("cact", [128, KC], F32, o); o += 64
        self.mhalf = self.sb("mhalf", [128, 1], F32, o); o += 32
        self.flag = self.sb("flag", [128, 1], F32, o); o += 32
        self.hprev = self.sb("hprev", [128, KC], BF16, o); o += 32
        self.const_used = o
        self.CONST_END = o = SB_BASE + 13312
        self.A = o
        self.hT = self.sb("hT", [128, KC, NT + 1], BF16, self.A)
        self.B = self.A + 65600
        self.ysc = self.dscratch("ysc", [NB, 128, KC, 128], BF16)
        self.C = self.B + 65536
        self.C_SIZE = SB_END - self.C
        idn = self.din("idn", [128, 128])
        P.dma(self.identf[:], idn)
        P.copy(self.ident[:], self.identf[:])
        P.memset(self.mhalf[:], -0.5, eng="pool")
        P.dma(self.flag[:], self.din("flag", [128, 1]))
        self.x_own = self.din("x_own", [NT, D])
        self.x_pre = self.din("x_pre", [NT, D])
        self.out = self.dout("out", [NT, D])
        self.preconvert()
        self.phase0()
        P.barrier()
        mode = self.dbg.get("mode", "full")
        if mode == "full":
            self.sb_consts()
            self.rwkv_consts()
            P.barrier()
            self.norm_phase(self.x_pre, self.s1, self.sh1, first=True)
            P.barrier()
            self.rwkv_phase("pre")
            P.barrier()
            g0b = self.mod_gen((3, 4), self.B, self.ps[5])
            self.sb_phase("pre", extra=g0b)
            for _ in g0b:
                pass
            self.phase0b_finish(self.ps[5])
            P.copy(self.hprev[:], self.hT[:, :, NT])
            P.ts(self.Hst[:].rearrange("p h v -> p (h v)"), self.Hst[:].rearrange("p h v -> p (h v)"), self.flag[0:64, :], None, op0=ALU.mult)
            P.barrier()
            self.norm_phase(self.x_own, self.s1, self.sh1, first=False)
            P.ts(self.hT[:, :, 0], self.hprev[:], self.flag[:], None, op0=ALU.mult)
            P.barrier()
            self.rwkv_phase("own")
            P.barrier()
            self.sb_phase("own")
            P.barrier()
            self.phaseF()
        if mode == "p0":
            self.dump_sb("d_modT", self.modT, [128, 96], F32)
        elif mode == "norm":
            self.norm_phase(self.x_own, self.s1, self.sh1, first=True)
            self.dump_sb("d_hT", self.hT, [128, KC, NT + 1], BF16)
        elif mode == "sb":
            self.sb_consts()
            self.norm_phase(self.x_pre, self.s1, self.sh1, first=True)
            P.barrier()
            self.sb_phase("pre")
            P.barrier()
            self.norm_phase(self.x_own, self.s1, self.sh1, first=False)
            P.barrier()
            self.sb_phase("own")
        elif mode == "rwkv":
            self.sb_consts()
            self.rwkv_consts()
            self.norm_phase(self.x_pre, self.s1, self.sh1, first=True)
            P.barrier()
            self.rwkv_phase("pre")
            P.copy(self.hprev[:], self.hT[:, :, NT])
            P.ts(self.Hst[:].rearrange("p h v -> p (h v)"), self.Hst[:].rearrange("p h v -> p (h v)"), self.flag[0:64, :], None, op0=ALU.mult)
            P.barrier()
            self.norm_phase(self.x_own, self.s1, self.sh1, first=False)
            P.ts(self.hT[:, :, 0], self.hprev[:], self.flag[:], None, op0=ALU.mult)
            P.barrier()
            self.rwkv_phase("own")
            dy = self.dout("d_ysc", [NB, 128, KC, 128], BF16)
            P.dma(dy, self.ysc)
            self.dump_sb("d_H", self.Hst, [64, HEADS, 64], F32)
        elif mode == "ffn":
            yin = self.din("ysc_in", [NB, 128, KC, 128], BF16)
            P.dma(self.ysc, yin)
            self.phaseF()
        P.emit()
        return nc

    def preconvert(self):
        P = self.P
        self.wo_t = self.dscratch("wo_t", [4, 128, KC, 512], BF16)
        self.wgu_t = self.dscratch("wgu_t", [JC // 2, 128, 2, KC, 256], BF16)
        self.wd_t = self.dscratch("wd_t", [4, JC // 4, 128, 4, 512], BF16)
        w_out = self.din("w_out", [D, D]).rearrange("(kc p) n -> p kc n", p=128)
        wgu = self.din("w_gate_up", [D, 2 * DFF]).rearrange("(kc p) n -> p kc n", p=128)
        wdn = self.din("w_down", [DFF, D]).rearrange("(j p) n -> p j n", p=128)
        q = []
        for ct in range(4):
            q.append((self.wo_t[ct], w_out[:, :, ct * 512:(ct + 1) * 512]))
        for jg in range(JC // 2):
            for g in range(2):
                q.append((self.wgu_t[jg, :, g], wgu[:, :, g * DFF + jg * 256: g * DFF + (jg + 1) * 256]))
        for ct in range(4):
            for j4 in range(JC // 4):
                q.append((self.wd_t[ct, j4], wdn[:, j4 * 4:(j4 + 1) * 4, ct * 512:(ct + 1) * 512]))
        self.conv_q = q

    def conv_some(self, n):
        for _ in range(n):
            if self.conv_q:
                d, s_ = self.conv_q.pop(0)
                self.P.dma(d, s_, eng="pool")

    def dump_sb(self, name, t, shape, dt):
        d = self.dout(name, shape, dt)
        self.P.dma(d, t[:])

    def phase0(self):
        P = self.P
        C = self.C
        ccol = self.din("ccol", [128, KC])
        ctmp = self.sb("ctmp", [128, KC], F32, C)
        P.dma(ctmp[:], ccol)
        P.actv(self.cact[:], ctmp[:], AF.Silu)
        bsb = self.sb("bsb", [128, 96], F32, C + 64)
        P.dma(bsb[:], self.din("b_ada_col", [128, 96]))
        modT = self.sb("modT", [128, 96], F32, C + 64 + 384)
        n1 = self.sb("n1", [128, KC], F32, C + 1024 - 128)
        P.dma(n1[:], self.din("n1g_col", [128, KC]))
        for _ in self.mod_gen((0, 1), C + 1024, self.ps[0]):
            pass
        P.tt(modT[:, 0:32], self.ps[0][:, 0:32], bsb[:, 0:32], ALU.add)
        P.stt(self.s1[:], modT[:, 16:32], 1.0, n1[:], ALU.add, ALU.mult)
        P.copy(self.sh1[:], modT[:, 0:16])

    def mod_gen(self, js, wbase, ps):
        P = self.P
        wbuf = [self.sb(f"wada{wbase}_{i}", [128, KC, 512], F32, wbase + i * KC * 512 * 4) for i in range(2)]
        wv = self.din("w_ada", [D, 6 * D]).rearrange("(kc p) n -> p kc n", p=128)
        it = 0
        for j in js:
            for q in range(4):
                wb = wbuf[it % 2]
                it += 1
                P.dma(wb[:], wv[:, :, j * D + q * 512: j * D + (q + 1) * 512])
                for fb in range(4):
                    col = j * 16 + q * 4 + fb
                    for kc in range(KC):
                        P.mm(ps[:, col:col + 1], wb[:, kc, fb * 128:(fb + 1) * 128], self.cact[:, kc:kc + 1],
                             start=(kc == 0), stop=(kc == KC - 1))
                    yield

    def phase0b_finish(self, ps):
        P = self.P
        o = self.p0b_off
        bsb2 = self.sb("bsb2", [128, 32], F32, o); o += 128
        modT2 = self.sb("modT2", [128, 32], F32, o); o += 128
        n2 = self.sb("n2", [128, KC], F32, o); o += 64
        assert o <= self.CONST_END, o
        P.dma(bsb2[:], self.din("b_ada_col", [128, 96])[:, 48:80])
        P.dma(n2[:], self.din("n2g_col", [128, KC]))
        P.tt(modT2[:], ps[:, 48:80], bsb2[:], ALU.add)
        P.stt(self.s2[:], modT2[:, 16:32], 1.0, n2[:], ALU.add, ALU.mult)
        P.copy(self.sh2[:], modT2[:, 0:16])

    def _cb_src(self):
        a = self.cact[:]
        pat = [list(p) for p in a.ap]
        return bass.AP(a.tensor, int(a.offset), [pat[0], pat[1], [0, 128]])

    def norm_phase(self, xsrc, sc, sh, first, cbase=None):
        P = self.P
        C = self.C if cbase is None else cbase
        xt = [self.sb(f"xt{i}", [128, D], F32, C + i * 8192) for i in range(2)]
        xn = [self.sb(f"xn{i}", [128, D], BF16, C + 16384 + i * 4096) for i in range(2)]
        junk = self.sb("junk", [128, D], BF16, C + 24576)
        st = self.sb("nst", [128, 4 * NB], F32, C + 28672)
        if first:
            P.memset(self.hT[:, :, 0:1], 0.0)
        for blk in range(self.dbg.get("nb", NB)):
            x_t = xt[blk % 2]
            x_n = xn[blk % 2]
            P.dma(x_t[:], xsrc[blk * 128:(blk + 1) * 128, :])
            ss = st[:, 4 * blk:4 * blk + 1]
            rs = st[:, 4 * blk + 1:4 * blk + 2]
            P.actv(junk[:], x_t[:], AF.Square, accum_out=ss)
            P.ts(rs, ss, 1.0 / D, 1e-6, op0=ALU.mult, op1=ALU.add)
            P.actv(rs, rs, AF.Sqrt)
            P.recip(rs, rs)
            P.ts(x_n[:], x_t[:], rs, None, op0=ALU.mult)
            if self.dbg.get("notr"):
                continue
            for g in range(2):
                pb = self.pb[g]
                for i in range(8):
                    kc = g * 8 + i
                    P.tr(pb[:, i * 128:(i + 1) * 128], x_n[:, kc * 128:(kc + 1) * 128], self.ident[:])
                for i in range(8):
                    kc = g * 8 + i
                    dst = self.hT[:, kc, 1 + blk * 128: 1 + (blk + 1) * 128]
                    src = pb[:, i * 128:(i + 1) * 128]
                    if self.dbg.get("noevac"):
                        continue
                    if (i % 2 == 0 or self.dbg.get("onlyact")) and not self.dbg.get("onlydve"):
                        P.actv(dst, src, AF.Identity, bias=sh[:, kc:kc + 1], scale=sc[:, kc:kc + 1])
                    else:
                        P.ts(dst, src, sc[:, kc:kc + 1], sh[:, kc:kc + 1], op0=ALU.mult, op1=ALU.add)

    def gate_bcast(self, j, dst, cbase):
        P = self.P
        w_ada = self.din("w_ada", [D, 6 * D])
        wv = w_ada.rearrange("(kc p) n -> p kc n", p=128)
        brow = self.din("b_ada_row", [1, 6 * D])
        wbuf = [self.sb(f"wg{i}", [128, KC, 256], F32, cbase + i * KC * 256 * 4) for i in range(2)]
        bb = self.sb("bb", [128, D], F32, cbase + 2 * KC * 256 * 4)
        cbc = self.sb("cbc", [128, KC, 128], F32, cbase + 2 * KC * 256 * 4 + 8192)
        P.copy(cbc[:], self._cb_src())
        P.dma(bb[:], dram_bcast(brow[:, j * D:(j + 1) * D], 128))
        for q in range(8):
            wb = wbuf[q % 2]
            P.dma(wb[:], wv[:, :, j * D + q * 256: j * D + (q + 1) * 256])
            ps = self.ps[q % 2]
            for kc in range(KC):
                P.mm(ps[:, 0:256], cbc[:, kc, :], wb[:, kc, :], start=(kc == 0), stop=(kc == KC - 1))
            P.tt(dst[:, q * 256:(q + 1) * 256], ps[:, 0:256], bb[:, q * 256:(q + 1) * 256], ALU.add)

    def phaseF(self):
        P = self.P
        C = self.C
        nc = self.nc
        self.conv_some(1000)
        gt = self.sb("gt", [128, D], F32, C)
        self.gate_bcast(2, gt, C + 8192)
        P.barrier()
        wres = self.sb("wo_res", [128, 4, KC, 512], BF16, self.B)
        for ct in range(4):
            P.dma(wres[:, ct], self.wo_t[ct])
        o = C + 8192
        yblk = [self.sb(f"yblk{i}", [128, KC, 128], BF16, o + i * 4096) for i in range(2)]; o += 8192
        xq = [self.sb(f"xq{i}", [128, D], F32, o + i * 8192) for i in range(2)]; o += 16384
        x1 = [self.sb(f"x1{i}", [128, D], F32, o + i * 8192) for i in range(2)]; o += 16384
        assert o <= SB_END
        for blk in range(NB):
            ts_ = slice(blk * 128, (blk + 1) * 128)
            yb = yblk[blk % 2]
            P.dma(yb[:], self.ysc[blk])
            P.dma(xq[blk % 2][:], self.x_own[ts_, :])
            for ct in range(4):
                cs = slice(ct * 512, (ct + 1) * 512)
                ps = self.ps[ct]
                for kc in range(KC):
                    P.mm(ps[:], yb[:, kc, :], wres[:, ct, kc, :], start=(kc == 0), stop=(kc == KC - 1))
                P.tt(x1[blk % 2][:, cs], ps[:], gt[:, cs], ALU.mult)
            P.tt(x1[blk % 2][:], x1[blk % 2][:], xq[blk % 2][:], ALU.add, eng="pool")
            P.dma(self.out[ts_, :], x1[blk % 2][:], eng="act")
        P.barrier()
        self.norm_phase(self.out, self.s2, self.sh2, first=True, cbase=C + 8192)
        P.barrier()
        self.gate_bcast(5, gt, C + 8192)
        P.barrier()
        TT = 512
        actT = self.sb("actT", [128, JC, TT], BF16, self.B)
        ob = self.B + JC * TT * 2
        xr = [self.sb(f"xr{i}", [128, 512], F32, ob + i * 2048) for i in range(2)]; ob += 4096
        xo = [self.sb(f"xo{i}", [128, 512], F32, ob + i * 2048) for i in range(2)]; ob += 4096
        sil = [self.sb(f"sil{i}", [128, 512], F32, ob + i * 2048) for i in range(2)]; ob += 4096
        assert ob <= self.C
        o = C + 8192
        wgu_sb = [self.sb(f"wgusb{i}", [128, 2, KC, 256], BF16, o + i * 16384) for i in range(2)]; o += 32768
        wd_bf = [self.sb(f"wdbf{i}", [128, 4, 512], BF16, o + i * 4096) for i in range(3)]; o += 12288
        assert o <= SB_END, o
        for tt in range(NT // TT):
            t0 = tt * TT
            tsl = slice(1 + t0, 1 + t0 + TT)
            for jg in range(JC // 2):
                b = jg % 2
                P.dma(wgu_sb[b][:], self.wgu_t[jg])
                for jj in range(2):
                    j = jg * 2 + jj
                    pg = self.ps[4]
                    pu = self.ps[5]
                    for kc in range(KC):
                        P.mm(pg[:], wgu_sb[b][:, 0, kc, jj * 128:(jj + 1) * 128], self.hT[:, kc, tsl], start=(kc == 0), stop=(kc == KC - 1))
                    for kc in range(KC):
                        P.mm(pu[:], wgu_sb[b][:, 1, kc, jj * 128:(jj + 1) * 128], self.hT[:, kc, tsl], start=(kc == 0), stop=(kc == KC - 1))
                    s_ = sil[j % 2]
                    P.actv(s_[:], pg[:], AF.Silu)
                    P.tt(actT[:, j, :], s_[:], pu[:], ALU.mult)
            nd = 0
            for ct in range(4):
                cs = slice(ct * 512, (ct + 1) * 512)
                for j4 in range(JC // 4):
                    wb = wd_bf[nd % 3]
                    nd += 1
                    P.dma(wb[:], self.wd_t[ct, j4])
                    for jj in range(4):
                        j = j4 * 4 + jj
                        for tb in range(4):
                            tl = slice(tb * 128, (tb + 1) * 128)
                            P.mm(self.ps[tb][:], actT[:, j, tl], wb[:, jj, :], start=(j == 0), stop=(j == JC - 1))
                for tb in range(4):
                    r0 = t0 + tb * 128
                    xr_ = xr[tb % 2]
                    xo_ = xo[tb % 2]
                    P.dma(xr_[:], self.out[r0:r0 + 128, cs])
                    P.tt(xo_[:], self.ps[tb][:], gt[:, cs], ALU.mult)
                    P.tt(xo_[:], xo_[:], xr_[:], ALU.add, eng="pool")
                    P.dma(self.out[r0:r0 + 128, cs], xo_[:], eng="act")

    def sb_consts(self):
        P = self.P
        o = (self.const_used + 31) // 32 * 32
        self.mdiag = self.sb("mdiag", [128, 128], F32, o); o += 512
        self.zeros = self.sb("zeros", [128, 512], F32, o); o += 2048
        assert o <= self.CONST_END, o
        self.rw_const_off = o
        P.dma(self.mdiag[:], self.din("mdiag", [128, 128]))
        P.memset(self.zeros[:], 0.0)

    def sb_phase(self, seg, extra=None):
        P = self.P
        C = self.C
        own = seg == "own"
        w_in = self.din("w_in", [D, 6144]).rearrange("(kc p) n -> p kc n", p=128)
        if not hasattr(self, "kpre"):
            self.kpre = self.dscratch("kpre", [8, 64, 2, NT], BF16)
            self.vpre = self.dscratch("vpre", [8, 128, NB, 2, 64], BF16)
        Wsb = self.sb("Wsb", [128, KC, 384], BF16, C)
        stg = self.sb("sbstg", [128, 8, 384], F32, C + 12288)
        kT = self.sb("kT", [64, 2, 2 * NT], BF16, C + 24576)
        qT = self.sb("qT", [64, 2, NT], BF16, C + 40960)
        Vt = self.sb("Vt", [128, 2 * NB, 2, 64], BF16, C + 49152)
        o = C + 57344
        tmps = []
        for ci in range(2):
            sq = self.sb(f"sbsq{ci}", [128, 256], F32, o); o += 1024
            qkn = self.sb(f"qkn{ci}", [128, 4, 64], BF16, o); o += 512
            qkf = self.sb(f"qkf{ci}", [128, 4, 64], F32, o); o += 1024
            ssr = self.sb(f"ssr{ci}", [128, 8], F32, o); o += 32
            tmps.append((sq, qkn, qkf, ssr))
        gains = self.sb("gains", [128, 4, 64], F32, o); o += 1024
        assert o <= C + 64000, o
        extra_done = [extra is None]
        o = C
        o = C + 256
        G = [self.sb(f"G{i}", [128, 512], F32, o + i * 2048) for i in range(3)]; o += 6144
        attn = [self.sb(f"attn{i}", [128, 512], BF16, o + i * 1024) for i in range(3)]; o += 3072
        attnT = [self.sb(f"attnT{i}", [128, 4, 512], BF16, o + i * 4096) for i in range(2)]; o += 8192
        assert o <= C + 24576, o
        ysb_buf = [self.sb(f"ysbb{i}", [128, 512], BF16, C + 64000 + i * 1024) for i in range(2)]
        CPf = self.sb("CPf", [128, 4, 2 * NT + 8], F32, self.B)
        qg = self.din("q_gain", [1, 64]); kg = self.din("k_gain", [1, 64])
        for j in range(4):
            P.dma(gains[:, j, :], dram_bcast(qg if j < 2 else kg, 128))
        P.ts(gains[:, 0:2, :], gains[:, 0:2, :], 0.125, None, op0=ALU.mult)
        koff = NT if own else 0
        for sp in range(8):
            for hf in range(2):
                for j in range(3):
                    c0 = 3072 + j * 1024 + sp * 128
                    P.dma(stg[:, :, j * 128:(j + 1) * 128], w_in[:, hf * 8:(hf + 1) * 8, c0:c0 + 128])
                P.copy(Wsb[:, hf * 8:(hf + 1) * 8, :], stg[:], eng=("pool" if hf else "dve"))
            if own:
                P.dma(kT[:, :, 0:NT], self.kpre[sp])
                P.dma(Vt[:, 0:NB], self.vpre[sp])
            def ip_gen(ci, blk):
                sq_, qkn_, qkf_, ssr_ = tmps[ci]
                ps = self.ps[2 * ci + (blk // 2) % 2]
                c_lo = 0 if own else 128
                for kc in range(KC):
                    P.mm(ps[:, c_lo:384], self.hT[:, kc, 1 + blk * 128: 1 + (blk + 1) * 128], Wsb[:, kc, c_lo:384],
                         start=(kc == 0), stop=(kc == KC - 1))
                    if kc == 7:
                        yield
                yield
                g0 = 0 if own else 2
                P.actv(sq_[:, c_lo:256], ps[:, c_lo:256], AF.Square)
                s4 = ssr_[:, g0:4]
                P.reduce(s4, sq_[:, c_lo:256].rearrange("p (g d) -> p g d", d=64))
                yield
                P.ts(s4, s4, 1.0 / 64, 1e-6, op0=ALU.mult, op1=ALU.add)
                P.actv(s4, s4, AF.Sqrt)
                P.recip(s4, s4)
                yield
                P.tt(qkf_[:, g0:4, :], ps[:, c_lo:256].rearrange("p (g d) -> p g d", d=64), self._b3(s4, 64), ALU.mult)
                P.tt(qkn_[:, g0:4, :], qkf_[:, g0:4, :], gains[:, g0:4, :], ALU.mult, eng="pool")
                vdst = Vt[:, koff // 128 + blk].rearrange("p h d -> p (h d)")
                if own:
                    P.actv(vdst, ps[:, 256:384], AF.Copy)
                else:
                    P.actv(vdst, ps[:, 256:384], AF.Identity, scale=self.flag[:])
                yield
                pb = self.pb[ci]
                for j in range(g0, 4):
                    P.tr(pb[0:64, j * 128:(j + 1) * 128], qkn_[:, j, :], self.ident[:])
                if own:
                    P.copy(qT[:, :, blk * 128:(blk + 1) * 128], pb[0:64, 0:256].rearrange("p (h t) -> p h t", h=2), eng="act")
                P.copy(kT[:, :, koff + blk * 128: koff + (blk + 1) * 128], pb[0:64, 256:512].rearrange("p (h t) -> p h t", h=2))
                yield

            def ip_chain(ci):
                for blk in range(ci, NB, 2):
                    for _ in ip_gen(ci, blk):
                        yield

            gens = [ip_chain(0), ip_chain(1)]
            if extra is not None:
                gens.append(extra)
            alive = [True] * len(gens)
            for _ in range(3):
                next(gens[0])
            while any(alive[:2]):
                for gi in range(len(gens)):
                    if alive[gi]:
                        try:
                            next(gens[gi])
                        except StopIteration:
                            alive[gi] = False
                            if gi == 2:
                                extra_done[0] = True
            if not own:
                P.dma(self.kpre[sp], kT[:, :, 0:NT])
                P.dma(self.vpre[sp], Vt[:, 0:NB])
                if extra is not None and extra_done[0]:
                    extra = None
                P.barrier()
                continue
            P.barrier()
            tiles = []
            for R in range(NB // 4):
                tq0 = NT + R * 512
                td = tq0 // 512
                for hh in range(2):
                    for ti in range(td, -1, -1):
                        k0 = ti * 512
                        for j in range(4):
                            w = (j + 1) * 128 if ti == td else 512
                            tiles.append(dict(R=R, hh=hh, k0=k0, w=w, j=j, first=(ti == td), diag=(ti == td),
                                              gfirst=(ti == td), glast=(ti == 0), it=len(tiles), grp=(len(tiles) // 4)))

            def S1(t):
                it = t["it"]; w = t["w"]; g = G[it % 3]
                pz = self.ps[it % 4]
                qb = t["R"] * 4 + t["j"]
                P.mm(pz[:, 0:w], qT[:, t["hh"], qb * 128:(qb + 1) * 128], kT[:, t["hh"], t["k0"]:t["k0"] + w])
                P.actv(g[:, 0:w], pz[:, 0:w], AF.Sigmoid, scale=-1.0)
                if t["first"]:
                    P.tt(g[:, w - 128:w], g[:, w - 128:w], self.mdiag[:], ALU.max)

            def S2(t):
                it = t["it"]; w = t["w"]; g = G[it % 3]; k0 = t["k0"]; j = t["j"]
                if t["first"]:
                    P.memset(CPf[:, j, k0 + w:k0 + w + 1], 1.0)
                    init = 1.0
                else:
                    init = CPf[:, j, k0 + w:k0 + w + 1]
                a = CPf[:, j, k0:k0 + w]
                rev_cp = bass.AP(a.tensor, int(a.offset) + w - 1, [list(a.ap[0]), [-1, w]])
                P.scan(rev_cp, self._rev(g, w), self.zeros[:, 0:w], init, ALU.mult, ALU.add)

            def S2b(t):
                it = t["it"]; w = t["w"]; at = attn[it % 3]; k0 = t["k0"]; j = t["j"]
                P.tt(at[:, 0:w], CPf[:, j, k0 + 1:k0 + w + 1], CPf[:, j, k0:k0 + w], ALU.subtract, eng="pool")

            def S3(t):
                it = t["it"]; w = t["w"]; at = attn[it % 3]; j = t["j"]
                aTs = attnT[t["grp"] % 2]
                pt = self.pb[it % 2]
                nb_ = w // 128
                for c in range(nb_):
                    P.tr(pt[:, c * 128:(c + 1) * 128], at[:, c * 128:(c + 1) * 128], self.ident[:])
                P.copy(aTs[:, 0:nb_, j * 128:(j + 1) * 128], pt[:, 0:w].rearrange("p (c t) -> p c t", t=128),
                       eng=("act" if it % 3 else "dve"))

            def S4(t):
                if t["j"] != 3:
                    return
                aTs = attnT[t["grp"] % 2]; hh = t["hh"]; R = t["R"]
                po = self.ps[4 + (R % 2)]
                for c in range(4):
                    j0 = c if t["diag"] else 0
                    kb = (t["k0"] + c * 128) // 128
                    P.mm(po[64 * hh:64 * hh + 64, j0 * 128:512], Vt[:, kb, hh, :], aTs[:, c, j0 * 128:512],
                         start=(t["gfirst"] and c == 0), stop=(t["glast"] and c == 3), skip_group_check=True)
                if t["glast"] and hh == 1:
                    yb = ysb_buf[R % 2]
                    P.copy(yb[:], po[:, :])
                    P.dma(self.ysc[R * 4:(R + 1) * 4, :, 8 + sp, :].rearrange("b p t -> p b t"),
                          yb[:].rearrange("p (b t) -> p b t", t=128))

            nt = len(tiles)
            stages = (S1, S2, S2b, S3, S4)
            offs = (0, 2, 4, 6, 8)
            for step in range(nt + offs[-1]):
                for k in range(len(stages) - 1, -1, -1):
                    if 0 <= step - offs[k] < nt:
                        stages[k](tiles[step - offs[k]])
            P.barrier()

    def _b3(self, ap2, n):
        pat = [list(p) for p in ap2.ap]
        return bass.AP(ap2.tensor, int(ap2.offset), pat + [[0, n]])

    def _rev(self, t, w):
        a = t[:, 0:w]
        return bass.AP(a.tensor, int(a.offset) + w - 1, [list(a.ap[0]), [-1, w]])

    def rwkv_consts(self):
        P = self.P
        o = self.rw_const_off
        self.tri = self.sb("tri", [128, 5, 128], F32, o); o += 2560
        self.mask2 = self.sb("mask2", [128, 256], F32, o); o += 1024
        self.maskT = self.sb("maskT", [128, 128], F32, o); o += 512
        self.bones = self.sb("bones", [128, 128], F32, o); o += 512
        self.sel = self.sb("sel", [128, 2], F32, o); o += 32
        self.lnx = self.sb("lnx", [128, 16], F32, o); o += 64
        self.mucol = self.sb("mucol", [128, 6, KC], F32, o); o += 6 * KC * 4
        self.Hst = self.sb("Hst", [64, HEADS, 64], F32, o); o += 4096
        assert o <= self.CONST_END, o
        P.dma(self.tri[:], self.din("tri5", [128, 5, 128]))
        P.dma(self.mask2[:], self.din("mask2", [128, 256]))
        P.dma(self.maskT[:], self.din("maskT", [128, 128]))
        P.dma(self.bones[:], self.din("bones", [128, 128]))
        P.dma(self.sel[:], self.din("sel", [128, 2]))
        P.dma(self.lnx[:, 0:8], self.din("lnxg_col", [128, 8]))
        P.dma(self.lnx[:, 8:16], self.din("lnxb_col", [128, 8]))
        for i, nm in enumerate(("mu_w_col", "mu_a_col", "mu_g_col")):
            P.dma(self.mucol[:, i, :], self.din(nm, [128, KC]))
            P.ts(self.mucol[:, 3 + i, :], self.mucol[:, i, :], -1.0, 1.0, op0=ALU.mult, op1=ALU.add)
        P.memset(self.Hst[:], 0.0)
        self.p0b_off = (o + 31) // 32 * 32

    def rwkv_phase(self, seg):
        P = self.P
        C = self.C
        B = self.B
        own = seg == "own"
        rot = {"i": 0}

        def evac_copy(dst, src):
            rot["i"] += 1
            if rot["i"] % 2:
                P.copy(dst, src)
            else:
                P.copy(dst, src, eng="act")

        w_in = self.din("w_in", [D, 6144]).rearrange("(kc p) n -> p kc n", p=128)
        zwa = self.sb("zwa", [128, NT], BF16, C)
        zg1 = self.sb("zg1", [128, NT], BF16, C + 4096)
        zg2 = self.sb("zg2", [32, NT], BF16, C + 8192)
        o = B
        raw = self.sb("l1raw", [128, KC, 288], F32, o); o += 18432
        Wl = self.sb("Wl", [128, KC, 2, 288], BF16, o); o += 18432
        w1v = self.din("w1", [D, 64]).rearrange("(kc p) n -> p kc n", p=128)
        a1v = self.din("a1", [D, 64]).rearrange("(kc p) n -> p kc n", p=128)
        g1v = self.din("g1", [D, 160]).rearrange("(kc p) n -> p kc n", p=128)
        P.dma(raw[:, :, 0:64], w1v)
        P.dma(raw[:, :, 64:128], a1v)
        P.dma(raw[:, :, 128:288], g1v)
        for (c0, c1, mi) in ((0, 64, 0), (64, 128, 1), (128, 288, 2)):
            P.tt(Wl[:, :, 0, c0:c1], raw[:, :, c0:c1], self._b3(self.mucol[:, 3 + mi, :], c1 - c0), ALU.mult)
            P.tt(Wl[:, :, 1, c0:c1], raw[:, :, c0:c1], self._b3(self.mucol[:, mi, :], c1 - c0), ALU.mult, eng="pool")
        for tt in range(NT // 512):
            t0 = tt * 512
            pz = self.ps[(3 * tt) % 6]; pg1 = self.ps[(3 * tt + 1) % 6]; pg2 = self.ps[(3 * tt + 2) % 6]
            for (pp, c0, c1) in ((pz, 0, 128), (pg1, 128, 256), (pg2, 256, 288)):
                n = 0
                for var in (0, 1):
                    for kc in range(KC):
                        P.mm(pp[0:c1 - c0, :], Wl[:, kc, var, c0:c1], self.hT[:, kc, 1 + t0 - var: 1 + t0 - var + 512],
                             start=(n == 0), stop=(n == 2 * KC - 1))
                        n += 1
            P.actv(zwa[0:64, t0:t0 + 512], pz[0:64, :], AF.Tanh)
            P.copy(zwa[64:128, t0:t0 + 512], pz[64:128, :])
            P.actv(zg1[:, t0:t0 + 512], pg1[:, :], AF.Sigmoid)
            P.actv(zg2[:, t0:t0 + 512], pg2[0:32, :], AF.Sigmoid)
        P.barrier()
        rows = {"mu": self.din("mu_rkv", [1, 3072]), "w0": self.din("w0", [1, 1024]), "a0": self.din("a0", [1, 1024]),
                "kk": self.din("k_k", [1, 1024]), "ka": self.din("k_a", [1, 1024]), "rk": self.din("r_k", [1, 1024])}
        w2d = self.din("w2", [64, 1024]); a2d = self.din("a2", [64, 1024]); g2d = self.din("g2", [160, 1024])
        ident2 = mid_bcast(self.ident[:], 2)

        class Ctx:
            pass

        def make_ctx(ci, base):
            c = Ctx()
            c.ci = ci
            c.banks = self.ps[3 * ci:3 * ci + 3]
            c.pbank = self.pb[ci]
            c.rr = 0
            o = base
            c.Wr = self.sb(f"Wr{ci}", [128, KC, 2, 384], BF16, o); o += 24576
            c.l2w = self.sb(f"l2w{ci}", [128, 4, 128], BF16, o); o += 1024
            c.bvec = self.sb(f"bvec{ci}", [128, 5, 128], F32, o); o += 2560
            T0 = o
            c.stg = self.sb(f"rstg{ci}", [128, 8, 384], F32, T0)
            c.mu_b = self.sb(f"mu_b{ci}", [128, 384], F32, T0 + 12288)
            c.omm_b = self.sb(f"omm_b{ci}", [128, 384], F32, T0 + 12288 + 1536)
            c.stg2 = self.sb(f"stg2{ci}", [128, 4, 128], F32, T0 + 12288 + 3072)
            o = T0

            def f32t(nm, n=128):
                nonlocal o
                t = self.sb(f"{nm}{ci}", [128, n], F32, o); o += n * 4
                return t
            for nm in ("t_u", "sigw", "t_a", "asig"):
                setattr(c, nm, f32t(nm))
            c.E = self.sb(f"E{ci}", [128, 4, 128], F32, o); o += 2048
            for nm in ("PCt", "kk", "kk2", "kkn", "bb", "t1", "kmod", "tt1", "Ysb", "sqy", "mean", "m2", "var_", "dd", "yn"):
                setattr(c, nm, f32t(nm))
            c.ss2 = self.sb(f"ss2{ci}", [128, 8], F32, o); o += 32
            c.TM = self.sb(f"TMops{ci}", [128, 8, 128], BF16, o); o += 2048
            c.AW = self.sb(f"AW{ci}", [128, 2, 128], BF16, o); o += 512
            c.BG = self.sb(f"BG{ci}", [128, 2, 128], BF16, o); o += 512
            c.FM = self.sb(f"FMops{ci}", [64, 8, 128], BF16, o); o += 2048
            c.BGT = self.sb(f"BGT{ci}", [128, 2, 128], F32, o); o += 1024
            c.GTbm = self.sb(f"GTbm{ci}", [128, 2, 256], BF16, o); o += 1024
            c.GTkm = self.sb(f"GTkm{ci}", [128, 2, 256], BF16, o); o += 1024
            c.M0m = self.sb(f"M0m{ci}", [128, 2, 128], BF16, o); o += 512
            c.Xb = [self.sb(f"X{ci}_{i}", [128, 2, 128], BF16, o + i * 512) for i in range(2)]; o += 1024
            c.Nb = [self.sb(f"N{ci}_{i}", [128, 2, 128], BF16, o + i * 512) for i in range(2)]; o += 1024
            c.Mb = [self.sb(f"M{ci}_{i}", [128, 2, 128], BF16, o + i * 512) for i in range(2)]; o += 1024
            c.AU = self.sb(f"AU{ci}", [128, 2, 128], BF16, o); o += 512
            c.RhT = self.sb(f"RhT{ci}", [64, 2, 128], BF16, o); o += 512
            c.TcT = self.sb(f"TcT{ci}", [64, 4, 64], BF16, o); o += 512
            c.Dc = self.sb(f"Dc{ci}", [64, 4, 64], F32, o); o += 1024
            c.PCfm = self.sb(f"PCfm{ci}", [64, 8], F32, o); o += 32
            c.Htmp = self.sb(f"Htmp{ci}", [64, 2, 64], F32, o); o += 512
            c.Hbf = self.sb(f"Hbf{ci}", [64, 2, 64], BF16, o); o += 256
            c.ybuf = [self.sb(f"ybuf{ci}_{i}", [128, 128], BF16, o + i * 256) for i in range(2)]; o += 512
            c.end = o
            return c

        ctxs = [make_ctx(0, B), make_ctx(1, C + 12288)]
        assert ctxs[0].end <= C, ctxs[0].end - C
        assert ctxs[1].end <= SB_END, ctxs[1].end - SB_END

        def prep(c, rp):
            f0 = rp * 128
            for j in range(3):
                P.dma(c.mu_b[:, j * 128:(j + 1) * 128], dram_bcast(rows["mu"][:, j * 1024 + f0: j * 1024 + f0 + 128], 128))
            P.ts(c.omm_b[:], c.mu_b[:], -1.0, 1.0, op0=ALU.mult, op1=ALU.add)
            for hf in range(2):
                for j in range(3):
                    c0 = j * 1024 + f0
                    P.dma(c.stg[:, :, j * 128:(j + 1) * 128], w_in[:, hf * 8:(hf + 1) * 8, c0:c0 + 128])
                P.tt(c.Wr[:, hf * 8:(hf + 1) * 8, 0, :], c.stg[:], mid_bcast(c.omm_b[:], 8), ALU.mult)
                P.tt(c.Wr[:, hf * 8:(hf + 1) * 8, 1, :], c.stg[:], mid_bcast(c.mu_b[:], 8), ALU.mult, eng="pool")
            P.dma(c.stg2[0:64, 0, :], w2d[:, f0:f0 + 128])
            P.dma(c.stg2[64:128, 1, :], a2d[:, f0:f0 + 128])
            P.dma(c.stg2[:, 2, :], g2d[0:128, f0:f0 + 128])
            P.dma(c.stg2[0:32, 3, :], g2d[128:160, f0:f0 + 128])
            P.copy(c.l2w[0:64, 0, :], c.stg2[0:64, 0, :])
            P.copy(c.l2w[64:128, 1, :], c.stg2[64:128, 1, :])
            P.copy(c.l2w[:, 2, :], c.stg2[:, 2, :])
            P.copy(c.l2w[0:32, 3, :], c.stg2[0:32, 3, :])
            for i, nm in enumerate(("w0", "a0", "kk", "ka", "rk")):
                P.dma(c.bvec[:, i, :], dram_bcast(rows[nm][:, f0:f0 + 128], 128))
            P.copy(c.Hbf[:], self.Hst[:, 2 * rp:2 * rp + 2, :])

        def blk_gen(c, rp, blk):
            def nextps():
                c.rr += 1
                return c.banks[c.rr % 3]
            TM = c.TM; AW = c.AW; BG = c.BG; FM = c.FM; BGT = c.BGT; GTbm = c.GTbm; GTkm = c.GTkm; M0m = c.M0m
            AU = c.AU; RhT = c.RhT; TcT = c.TcT; Dc = c.Dc; PCfm = c.PCfm; Htmp = c.Htmp; Hbf = c.Hbf
            E = c.E; bvec = c.bvec; l2w = c.l2w; Wr = c.Wr
            tk = slice(blk * 128, (blk + 1) * 128)
            p_rkv = c.banks[0]
            n = 0
            for var in (0, 1):
                for kc in range(KC):
                    P.mm(p_rkv[:, 0:384], self.hT[:, kc, 1 + blk * 128 - var: 1 + (blk + 1) * 128 - var], Wr[:, kc, var, :],
                         start=(n == 0), stop=(n == 2 * KC - 1))
                    n += 1
                yield
            r_ps = p_rkv[:, 0:128]; k_ps = p_rkv[:, 128:256]; v_ps = p_rkv[:, 256:384]
            p_l = c.banks[1]
            P.mm(p_l[:, 0:128], zwa[0:64, tk], l2w[0:64, 0, :])
            P.mm(p_l[:, 256:384], zg1[:, tk], l2w[:, 2, :], start=True, stop=False)
            P.mm(p_l[:, 256:384], zg2[0:32, tk], l2w[0:32, 3, :], start=False, stop=True)
            P.tt(c.t_u[:], p_l[:, 0:128], bvec[:, 0, :], ALU.add)
            P.copy(BG[:, 1, :], p_l[:, 256:384], eng="act")
            yield
            p_l2 = c.banks[2]
            P.mm(p_l2[:, 0:128], zwa[64:128, tk], l2w[64:128, 1, :])
            P.actv(c.sigw[:], c.t_u[:], AF.Sigmoid)
            P.tt(c.t_a[:], p_l2[:, 0:128], bvec[:, 1, :], ALU.add)
            P.actv(c.asig[:], c.t_a[:], AF.Sigmoid)
            yield
            P.tt(c.kk[:], k_ps, bvec[:, 2, :], ALU.mult)
            P.tt(c.kk2[:], c.kk[:], c.kk[:], ALU.mult, eng="pool")
            P.reduce(c.ss2[:, 0:2], c.kk2[:].rearrange("p (h d) -> p h d", d=64))
            P.ts(c.ss2[:, 0:2], c.ss2[:, 0:2], 1e-12, None, op0=ALU.add)
            P.actv(c.ss2[:, 0:2], c.ss2[:, 0:2], AF.Sqrt)
            P.recip(c.ss2[:, 0:2], c.ss2[:, 0:2])
            yield
            p_c = c.banks[1]
            for i in range(4):
                P.mm(p_c[:, i * 128:(i + 1) * 128], self.tri[:, i, :], c.sigw[:])
            P.actv(E[:].rearrange("p a f -> p (a f)"), p_c[:, :], AF.Exp)
            yield
            p_e = c.banks[2]
            P.mm(p_e[:, 0:128], self.tri[:, 4, :], c.sigw[:])
            P.actv(c.PCt[:], p_e[:, 0:128], AF.Exp)
            P.tt(c.kkn[:].rearrange("p (h d) -> p h d", d=64), c.kk[:].rearrange("p (h d) -> p h d", d=64), self._b3(c.ss2[:, 0:2], 64), ALU.mult, eng="pool")
            P.stt(c.t1[:], c.asig[:], -1.0, bvec[:, 3, :], ALU.add, ALU.mult)
            yield
            P.tt(c.bb[:], c.kkn[:], c.asig[:], ALU.mult, eng="pool")
            P.stt(c.kmod[:], c.t1[:], 1.0, k_ps, ALU.add, ALU.mult)
            yield
            P.stt(TM[:, 0, :], c.kkn[:], -1.0, E[:, 1, :], ALU.mult, ALU.mult)
            P.tt(TM[:, 1, :], r_ps, E[:, 0, :], ALU.mult)
            P.tt(TM[:, 2, :], c.bb[:], E[:, 3, :], ALU.mult, eng="pool")
            P.tt(TM[:, 3, :], c.kmod[:], E[:, 3, :], ALU.mult)
            yield
            P.tt(TM[:, 4, :], c.bb[:], E[:, 2, :], ALU.mult, eng="pool")
            P.tt(TM[:, 5, :], c.kmod[:], E[:, 2, :], ALU.mult, eng="pool")
            P.copy(TM[:, 6, :], v_ps, eng="act")
            P.copy(AW[:, :, 0:64], TM[:, 0, :].rearrange("p (h d) -> p h d", d=64), eng="pool")
            if own:
                P.tt(c.tt1[:], r_ps, c.kmod[:], ALU.mult)
                P.tt(c.tt1[:], c.tt1[:], bvec[:, 4, :], ALU.mult, eng="pool")
                P.reduce(c.ss2[:, 2:4], c.tt1[:].rearrange("p (h d) -> p h d", d=64))
                P.tt(BG[:, 0, :].rearrange("p (h d) -> p h d", d=64), v_ps.rearrange("p (h d) -> p h d", d=64), self._b3(c.ss2[:, 2:4], 64), ALU.mult)
            yield
            pbt = c.pbank
            for hh in range(2):
                for a in range(4):
                    P.tr(pbt[0:64, (hh * 4 + a) * 128:(hh * 4 + a + 1) * 128], TM[:, a, hh * 64:(hh + 1) * 64], self.ident[:])
            evac_copy(FM[:].rearrange("p a t -> p (a t)"), pbt[0:64, :])
            yield
            if own:
                pbg = c.pbank
                P.tr(pbg[:, 0:128], BG[:, 0, :], self.ident[:])
                P.tr(pbg[:, 128:256], BG[:, 1, :], self.ident[:])
                P.copy(BGT[:].rearrange("p a t -> p (a t)"), pbg[:, 0:256], eng="act")
                yield
            pGb = c.banks[1]; pM0 = c.banks[2]; pGk = c.banks[0]
            c.rr = 0
            for hh in range(2):
                aT_rT = FM[:, hh * 4:hh * 4 + 2, :].rearrange("p a t -> p (a t)")
                P.mm(pGb[:, hh * 256:(hh + 1) * 256], FM[:, hh * 4 + 2, :], aT_rT)
            P.tt(GTbm[:], pGb[:, :].rearrange("p (h t) -> p h t", h=2), mid_bcast(self.mask2[:], 2), ALU.mult)
            for hh in range(2):
                P.mm(pM0[:, hh * 128:(hh + 1) * 128], FM[:, hh * 4 + 0, :], FM[:, hh * 4 + 2, :])
            P.tt(M0m[:], pM0[:, 0:256].rearrange("p (h t) -> p h t", h=2), mid_bcast(self.maskT[:], 2), ALU.mult)
            yield
            for hh in range(2):
                aT_rT = FM[:, hh * 4:hh * 4 + 2, :].rearrange("p a t -> p (a t)")
                P.mm(pGk[:, hh * 256:(hh + 1) * 256], FM[:, hh * 4 + 3, :], aT_rT)
            P.tt(GTkm[:], pGk[:, :].rearrange("p (h t) -> p h t", h=2), mid_bcast(self.mask2[:], 2), ALU.mult)
            Nk = GTbm[:, :, 0:128]
            Mk = M0m[:]
            X = c.Xb[0]
            P.tt(X[:], Nk, ident2, ALU.add, eng="pool")
            yield
            for it in range(5):
                pMn = nextps()
                for hh in range(2):
                    P.mm(pMn[:, hh * 128:(hh + 1) * 128], Nk[:, hh, :], Mk[:, hh, :])
                Mn = c.Mb[it % 2]
                evac_copy(Mn[:].rearrange("p h t -> p (h t)"), pMn[:, 0:256])
                if it < 4:
                    pNn = nextps()
                    for hh in range(2):
                        P.mm(pNn[:, hh * 128:(hh + 1) * 128], Mk[:, hh, :], Nk[:, hh, :])
                    Nn = c.Nb[it % 2]
                    evac_copy(Nn[:].rearrange("p h t -> p (h t)"), pNn[:, 0:256])
                yield
                pX = nextps()
                for hh in range(2):
                    P.mm(pX[:, hh * 128:(hh + 1) * 128], Mn[:, hh, :], X[:, hh, :])
                Xn = c.Xb[(it + 1) % 2]
                P.tt(Xn[:].rearrange("p h t -> p (h t)"), pX[:, 0:256], X[:].rearrange("p h t -> p (h t)"), ALU.add)
                X = Xn
                Mk = Mn[:]
                if it < 4:
                    Nk = Nn[:]
                yield
            pW = nextps()
            for hh in range(2):
                P.mm(pW[:, hh * 64:(hh + 1) * 64], GTkm[:, hh, 0:128], TM[:, 6, hh * 64:(hh + 1) * 64])
            evac_copy(AW[:, :, 64:128], pW[:, 0:128].rearrange("p (h d) -> p h d", h=2))
            yield
            pAU = nextps()
            for hh in range(2):
                P.mm(pAU[:, hh * 128:(hh + 1) * 128], X[:, hh, :], AW[:, hh, :])
            evac_copy(AU[:].rearrange("p h t -> p (h t)"), pAU[:, 0:256])
            yield
            if own:
                pR = nextps()
                for hh in range(2):
                    P.mm(pR[0:64, hh * 128:(hh + 1) * 128], AU[:, hh, 0:64], GTbm[:, hh, 128:256])
                for hh in range(2):
                    P.tt(RhT[:, hh, :], pR[0:64, hh * 128:(hh + 1) * 128], FM[:, hh * 4 + 1, :], ALU.add)
                yield
            pTs = [nextps(), nextps()]
            for hh in range(2):
                for cc in range(2):
                    rw = slice(cc * 64, cc * 64 + 64)
                    pT = pTs[cc]
                    col = hh * 128
                    P.mm(pT[0:64, col:col + 64], AU[rw, hh, 0:64], TM[rw, 4, hh * 64:(hh + 1) * 64], skip_group_check=True)
                    P.mm(pT[0:64, col + 64:col + 128], TM[rw, 4, hh * 64:(hh + 1) * 64], AU[rw, hh, 64:128], start=True, stop=False, skip_group_check=True)
                    P.mm(pT[0:64, col + 64:col + 128], TM[rw, 5, hh * 64:(hh + 1) * 64], TM[rw, 6, hh * 64:(hh + 1) * 64], start=False, stop=True, skip_group_check=True)
            for cc in range(2):
                pTv = pTs[cc][0:64, 0:256].rearrange("p (g x) -> p g x", x=128)
                P.copy(TcT[:, cc * 2:cc * 2 + 2, :], pTv[:, :, 0:64], eng="act")
                P.copy(Dc[:, cc * 2:cc * 2 + 2, :], pTv[:, :, 64:128])
            yield
            pP = nextps()
            for hh in range(2):
                P.mm(pP[0:64, hh * 2:hh * 2 + 2], c.PCt[:, hh * 64:(hh + 1) * 64], self.sel[:])
            P.copy(PCfm[:, 0:4], pP[0:64, 0:4])
            yield
            if own:
                pY = nextps()
                for hh in range(2):
                    yo = pY[64 * hh:64 * hh + 64, 0:128]
                    P.mm(yo, AU[:, hh, 64:128], GTbm[:, hh, 128:256], start=True, stop=False, skip_group_check=True)
                    P.mm(yo, TM[:, 6, hh * 64:(hh + 1) * 64], GTkm[:, hh, 128:256], start=False, stop=False, skip_group_check=True)
            for cc in range(2):
                pH = nextps()
                for hh in range(2):
                    if own:
                        P.mm(pY[64 * hh:64 * hh + 64, cc * 64:(cc + 1) * 64], Hbf[:, hh, :], RhT[:, hh, cc * 64:(cc + 1) * 64],
                             start=False, stop=(cc == 1), skip_group_check=True)
                    P.mm(pH[0:64, hh * 64:(hh + 1) * 64], TcT[:, cc * 2 + hh, :], Hbf[:, hh, :])
                for hh in range(2):
                    Hh = self.Hst[:, 2 * rp + hh, :]
                    P.stt(Htmp[:, hh, :], Hh, PCfm[:, hh * 2 + cc:hh * 2 + cc + 1], pH[0:64, hh * 64:(hh + 1) * 64], ALU.mult, ALU.add)
                    P.tt(Hh, Htmp[:, hh, :], Dc[:, cc * 2 + hh, :], ALU.add, eng="pool")
                P.copy(Hbf[:], self.Hst[:, 2 * rp:2 * rp + 2, :], eng="pool")
                yield
            if not own:
                return
            P.copy(c.Ysb[:], pY[:, 0:128], eng="act")
            P.actv(c.sqy[:], c.Ysb[:], AF.Square)
            yield
            pS = nextps()
            P.mm(pS[:, 0:128], self.bones[:], c.Ysb[:])
            P.mm(pS[:, 128:256], self.bones[:], c.sqy[:])
            P.copy(c.mean[:], pS[:, 0:128], eng="act")
            P.tt(c.m2[:], c.mean[:], c.mean[:], ALU.mult, eng="pool")
            P.tt(c.var_[:], pS[:, 128:256], c.m2[:], ALU.subtract)
            yield
            P.ts(c.var_[:], c.var_[:], 64e-5, None, op0=ALU.add)
            P.actv(c.var_[:], c.var_[:], AF.Sqrt)
            P.recip(c.var_[:], c.var_[:])
            P.tt(c.dd[:], c.Ysb[:], c.mean[:], ALU.subtract, eng="pool")
            yield
            P.tt(c.dd[:], c.dd[:], c.var_[:], ALU.mult, eng="pool")
            P.ts(c.yn[:], c.dd[:], self.lnx[:, rp:rp + 1], self.lnx[:, 8 + rp:9 + rp], op0=ALU.mult, op1=ALU.add)
            P.tt(c.yn[:], c.yn[:], BGT[:, 0, :], ALU.add, eng="pool")
            yb = c.ybuf[blk % 2]
            P.tt(yb[:], c.yn[:], BGT[:, 1, :], ALU.mult)
            P.dma(self.ysc[blk, :, rp, :], yb[:])
            yield

        def chain(c, rp):
            for blk in range(NB):
                for _ in blk_gen(c, rp, blk):
                    yield

        for rp2 in range(4):
            rps = (2 * rp2, 2 * rp2 + 1)
            for c, rp in zip(ctxs, rps):
                prep(c, rp)
            self.conv_some(16)
            P.barrier()
            gens = [chain(c, rp) for c, rp in zip(ctxs, rps)]
            alive = [True, True]
            for _ in range(6):
                next(gens[0])
            while any(alive):
                for gi in range(2):
                    if alive[gi]:
                        try:
                            next(gens[gi])
                        except StopIteration:
                            alive[gi] = False
            P.barrier()

_NC_CACHE = {}


def _col(v, n):
    return np.ascontiguousarray(np.asarray(v, dtype=np.float32).reshape(n, 128).T)


def _rwkv_consts():
    tok = np.arange(128)
    same = (tok[:, None] // 64) == (tok[None, :] // 64)
    s = tok[:, None]; t = tok[None, :]
    Cd = np.float32(C_DECAY)
    tri = np.zeros((128, 5, 128), np.float32)
    tri[:, 0] = np.where(same & (s <= t), Cd, 0)
    tri[:, 1] = np.where(same & (s < t), Cd, 0)
    tri[:, 2] = np.where(same & (s > t), Cd, 0)
    tri[:, 3] = np.where(same & (s <= t), -Cd, 0)
    tri[:, 4] = np.where(same, Cd, 0)
    m_lt = (same & (s < t)).astype(np.float32)
    m_le = (same & (s <= t)).astype(np.float32)
    sel = np.zeros((128, 2), np.float32); sel[0, 0] = 1; sel[64, 1] = 1
    bones = np.kron(np.eye(2, dtype=np.float32), np.full((64, 64), 1.0 / 64, np.float32))
    return {"tri5": tri, "mask2": np.concatenate([m_lt, m_le], 1), "maskT": np.ascontiguousarray(m_lt.T),
            "sel": sel, "bones": bones}


def make_in_maps(I, cores=range(8), extra=None):
    f = lambda k: np.asarray(I[k])[0]
    shared = {
        "idn": np.eye(128, dtype=np.float32),
        "w_ada": np.ascontiguousarray(f("w_ada")),
        "b_ada_col": _col(f("b_ada"), 96),
        "b_ada_row": np.ascontiguousarray(f("b_ada").reshape(1, -1)),
        "n1g_col": _col(f("norm1_gain"), 16),
        "n2g_col": _col(f("norm2_gain"), 16),
        "w_out": np.ascontiguousarray(f("w_out")),
        "w_in": np.ascontiguousarray(f("w_in")),
        "mu_rkv": f("mu_rkv").reshape(1, -1), "w0": f("w0").reshape(1, -1), "a0": f("a0").reshape(1, -1),
        "k_k": f("k_k").reshape(1, -1), "k_a": f("k_a").reshape(1, -1), "r_k": f("r_k").reshape(1, -1),
        "w1": f("w1"), "a1": f("a1"), "g1": f("g1"), "w2": f("w2"), "a2": f("a2"), "g2": f("g2"),
        "mu_w_col": _col(f("mu_w"), 16), "mu_a_col": _col(f("mu_a"), 16), "mu_g_col": _col(f("mu_g"), 16),
        "lnxg_col": _col(f("ln_x_gain"), 8), "lnxb_col": _col(f("ln_x_bias"), 8),
        **_rwkv_consts(),
        "mdiag": np.triu(np.ones((128, 128), np.float32)),
        "q_gain": np.ascontiguousarray(f("q_norm_gain").reshape(1, 64)),
        "k_gain": np.ascontiguousarray(f("k_norm_gain").reshape(1, 64)),
        "w_gate_up": np.ascontiguousarray(f("w_gate_up")),
        "w_down": np.ascontiguousarray(f("w_down")),
    }
    maps = []
    x = np.asarray(I["x"])
    c = np.asarray(I["c"])
    for core in cores:
        b, half = core // 2, core % 2
        m = dict(shared)
        m["x_own"] = np.ascontiguousarray(x[b, half * NT:(half + 1) * NT])
        m["x_pre"] = np.ascontiguousarray(x[b, 0:NT])
        m["ccol"] = _col(c[b], 16)
        m["flag"] = np.full((128, 1), float(half), np.float32)
        if extra:
            m.update(extra(core))
        maps.append(m)
    return maps


def kernel(**inputs):
    if "k" not in _NC_CACHE:
        k = K()
        _NC_CACHE["k"] = (k, k.build())
    k, nc = _NC_CACHE["k"]
    maps = make_in_maps(inputs)
    needed = set(k.din_cache.keys())
    maps = [{k: v for k, v in m.items() if k in needed} for m in maps]
    res = run_bass_kernel_spmd(nc, maps, core_ids=list(range(8)))
    out = np.empty((4, 4096, D), np.float32)
    for core in range(8):
        b, half = core // 2, core % 2
        out[b, half * NT:(half + 1) * NT] = res.results[core]["out"]
    return out
```
